# Optimizing a Trainium2 kernel written in Bass

```python
import jax
import jax.numpy as jnp
from jax import lax
import numpy as np

D_MODEL = 1024
BATCH = 8
SEQ = 4096
DEPTH = 2

N_MIXERS = 2
N_RWKV_LAYERS = (DEPTH + 1) // 2
N_NSA_LAYERS = DEPTH // 2
HEAD_DIM = 64
N_HEADS = D_MODEL // HEAD_DIM
NORM_EPS = 1e-6
N_SHIFT_MIX = 6
N_RWKV_PROJ = 4
DECAY_LORA = 64
ICLR_LORA = 64
LN_X_EPS = 64e-5
N_KV_GROUPS = 4
HEADS_PER_GROUP = N_HEADS // N_KV_GROUPS
KV_WIDTH = N_KV_GROUPS * HEAD_DIM
N_BRANCHES = 3
CMP_BLOCK = 32
CMP_STRIDE = 16
CMP_HIDDEN = 256
SLC_BLOCK = 64
SLC_TOPK = 16
N_LOCAL_BLOCKS = 2
WINDOW = 512
Q_BLOCK = 32
FORCE_BONUS = 1e4
NEG_INF = -1e30
NSA_IN_WIDTH = D_MODEL + 6 * KV_WIDTH + D_MODEL + N_BRANCHES * N_HEADS

kernel_name = "hybrid_rwkv7_nsa_interleaved"


def rms_norm(x, g, eps=NORM_EPS):
    xf = x.astype(jnp.float32)
    y = xf * lax.rsqrt(jnp.mean(xf * xf, axis=-1, keepdims=True) + eps)
    return (y * g.astype(jnp.float32)).astype(x.dtype)


def masked_softmax(s, mask):
    s = jnp.where(mask, s.astype(jnp.float32), NEG_INF)
    return jnp.where(mask, jax.nn.softmax(s, axis=-1), 0.0)


def wkv7_scan(r, w, k, v, kk, a):
    B, T, H, N = r.shape

    def step(S, inp):
        r_t, w_t, k_t, v_t, kk_t, a_t = inp
        sa = jnp.einsum('bhvk,bhk->bhv', S, -kk_t)
        S = (S * w_t[:, :, None, :]
             + sa[..., None] * (kk_t * a_t)[:, :, None, :]
             + v_t[..., None] * k_t[:, :, None, :])
        return S, jnp.einsum('bhvk,bhk->bhv', S, r_t)

    xs = tuple(jnp.moveaxis(u, 1, 0) for u in (r, w, k, v, kk, a))
    S0 = jnp.zeros((B, H, N, N), jnp.float32)
    _, o = lax.scan(step, S0, xs)
    return jnp.moveaxis(o, 0, 1)


def rwkv7_mixer(h, mu, w_in, w0, w1, w2, a0, a1, a2, k_k, k_a, r_k, lnx_g, lnx_b, w_out):
    B, T, D = h.shape
    H, N = N_HEADS, HEAD_DIM
    f32 = jnp.float32
    dh = jnp.pad(h, ((0, 0), (1, 0), (0, 0)))[:, :-1] - h
    xm = h[None] + dh[None] * mu[:, None, None, :]
    proj = jnp.einsum('cbtd,dcf->cbtf', xm[:N_RWKV_PROJ], w_in.reshape(D, N_RWKV_PROJ, D))
    r, k, v, z = proj[0], proj[1], proj[2], proj[3]
    xw, xa = xm[4], xm[5]
    w_log = -jax.nn.softplus(-(w0 + jnp.tanh(xw @ w1) @ w2).astype(f32)) - 0.5
    decay = jnp.exp(-jnp.exp(w_log))
    a = jax.nn.sigmoid((a0 + (xa @ a1) @ a2).astype(f32))
    heads = lambda u: u.astype(f32).reshape(B, T, H, N)
    r, k, v, decay, a = heads(r), heads(k), heads(v), heads(decay), heads(a)
    kk = k * k_k.astype(f32).reshape(H, N)
    kk = kk / jnp.maximum(jnp.sqrt(jnp.sum(kk * kk, axis=-1, keepdims=True)), 1e-12)
    k = k * (1.0 + (a - 1.0) * k_a.astype(f32).reshape(H, N))
    o = wkv7_scan(r, decay, k, v, kk, a)
    mean = jnp.mean(o, axis=-1, keepdims=True)
    var = jnp.mean(jnp.square(o - mean), axis=-1, keepdims=True)
    o = ((o - mean) * lax.rsqrt(var + LN_X_EPS) * lnx_g.astype(f32).reshape(H, N)
         + lnx_b.astype(f32).reshape(H, N))
    o = o + jnp.sum(r * k * r_k.astype(f32), axis=-1, keepdims=True) * v
    y = o.reshape(B, T, D).astype(h.dtype) * jax.nn.silu(z)
    return y @ w_out


def compress_blocks(u, pe, w1, b1, w2):
    T = u.shape[1]
    n_cmp = (T - CMP_BLOCK) // CMP_STRIDE + 1
    idx = jnp.arange(n_cmp)[:, None] * CMP_STRIDE + jnp.arange(CMP_BLOCK)[None, :]
    blocks = u[:, idx] + pe[None, None, :, None, :]
    hid = jax.nn.gelu(jnp.einsum('bnlgd,ldf->bngf', blocks,
                                 w1.reshape(CMP_BLOCK, HEAD_DIM, CMP_HIDDEN)) + b1)
    return jnp.einsum('bngf,fd->bngd', hid, w2)


def nsa_mixer(h, w_in, q_gain, k_gain, cmp_pe, cmp_w1, cmp_b1, cmp_w2, w_out):
    B, T, D = h.shape
    G, HG, N = N_KV_GROUPS, HEADS_PER_GROUP, HEAD_DIM
    f32 = jnp.float32
    proj = h @ w_in
    sizes = [D] + [KV_WIDTH] * 6 + [D, N_BRANCHES * N_HEADS]
    offs = np.cumsum(sizes)[:-1].tolist()
    q, kc_raw, vc_raw, ks, vs, kw, vw, z, gate = jnp.split(proj, offs, axis=-1)
    q = rms_norm(q.reshape(B, T, G, HG, N), q_gain)
    kv = lambda u: u.reshape(B, T, G, N)
    kc = rms_norm(compress_blocks(kv(kc_raw), cmp_pe[0], cmp_w1[0], cmp_b1[0], cmp_w2[0]), k_gain[0])
    vc = compress_blocks(kv(vc_raw), cmp_pe[1], cmp_w1[1], cmp_b1[1], cmp_w2[1])
    ks = rms_norm(kv(ks), k_gain[1])
    vs = kv(vs)
    kw = rms_norm(kv(kw), k_gain[2])
    vw = kv(vw)

    n_cmp = kc.shape[1]
    n_slc = T // SLC_BLOCK
    n_top = min(SLC_TOPK, n_slc)
    n_qb = T // Q_BLOCK
    scale = HEAD_DIM ** -0.5
    cmp_start = jnp.arange(n_cmp) * CMP_STRIDE
    cmp_end = cmp_start + CMP_BLOCK - 1
    slc_start = jnp.arange(n_slc) * SLC_BLOCK
    overlap = ((cmp_start[:, None] < slc_start[None, :] + SLC_BLOCK)
               & (cmp_end[:, None] >= slc_start[None, :])).astype(f32)
    ks_blk = jnp.moveaxis(ks.reshape(B, n_slc, SLC_BLOCK, G, N), 3, 1)
    vs_blk = jnp.moveaxis(vs.reshape(B, n_slc, SLC_BLOCK, G, N), 3, 1)
    kw_pad = jnp.pad(kw, ((0, 0), (WINDOW, 0), (0, 0), (0, 0)))
    vw_pad = jnp.pad(vw, ((0, 0), (WINDOW, 0), (0, 0), (0, 0)))
    b_ix = jnp.arange(B)[:, None, None, None]
    g_ix = jnp.arange(G)[None, :, None, None]
    blk_ids = jnp.arange(n_slc)

    def query_block(args):
        qb, q_blk, g_blk = args
        t = qb * Q_BLOCK + jnp.arange(Q_BLOCK)
        s = jnp.einsum('bqghd,bngd->bghqn', q_blk, kc).astype(f32) * scale
        p_cmp = masked_softmax(s, cmp_end[None, :] <= t[:, None])
        o_cmp = jnp.einsum('bghqn,bngd->bqghd', p_cmp.astype(vc.dtype), vc)
        imp = jnp.einsum('bghqn,nj->bgqj', p_cmp, overlap)
        dist = (t // SLC_BLOCK)[:, None] - blk_ids[None, :]
        forced = (blk_ids[None, :] == 0) | ((dist >= 0) & (dist < N_LOCAL_BLOCKS))
        score = jnp.where(dist >= 0, imp + jnp.where(forced, FORCE_BONUS, 0.0), -jnp.inf)
        _, sel = lax.top_k(score, n_top)
        k_sel = ks_blk[b_ix, g_ix, sel]
        v_sel = vs_blk[b_ix, g_ix, sel].reshape(B, G, Q_BLOCK, n_top * SLC_BLOCK, N)
        tok = sel[..., None] * SLC_BLOCK + jnp.arange(SLC_BLOCK)
        sel_mask = (tok <= t[None, None, :, None, None]).reshape(B, G, 1, Q_BLOCK, n_top * SLC_BLOCK)
        s = jnp.einsum('bqghd,bgqnsd->bghqns', q_blk, k_sel).astype(f32) * scale
        p = masked_softmax(s.reshape(B, G, HG, Q_BLOCK, n_top * SLC_BLOCK), sel_mask)
        o_slc = jnp.einsum('bghqm,bgqmd->bqghd', p.astype(v_sel.dtype), v_sel)
        start = qb * Q_BLOCK
        k_win = lax.dynamic_slice_in_dim(kw_pad, start, Q_BLOCK + WINDOW, axis=1)
        v_win = lax.dynamic_slice_in_dim(vw_pad, start, Q_BLOCK + WINDOW, axis=1)
        kpos = start - WINDOW + jnp.arange(Q_BLOCK + WINDOW)
        lag = t[:, None] - kpos[None, :]
        win_mask = (lag >= 0) & (lag < WINDOW) & (kpos[None, :] >= 0)
        s = jnp.einsum('bqghd,bkgd->bghqk', q_blk, k_win).astype(f32) * scale
        p = masked_softmax(s, win_mask)
        o_win = jnp.einsum('bghqk,bkgd->bqghd', p.astype(v_win.dtype), v_win)
        g = jax.nn.sigmoid(g_blk.astype(f32))[..., None]
        o = g[:, :, 0] * o_cmp + g[:, :, 1] * o_slc + g[:, :, 2] * o_win
        return o.astype(q_blk.dtype)

    q_blocks = jnp.moveaxis(q.reshape(B, n_qb, Q_BLOCK, G, HG, N), 1, 0)
    g_blocks = jnp.moveaxis(gate.reshape(B, n_qb, Q_BLOCK, N_BRANCHES, G, HG), 1, 0)
    o = lax.map(query_block, (jnp.arange(n_qb), q_blocks, g_blocks))
    o = jnp.moveaxis(o, 0, 1).reshape(B, T, D)
    y = o * jax.nn.silu(z)
    return y @ w_out


def setup_inputs(seed: int = 0) -> dict:
    key = jax.random.key(seed)
    keys = iter(jax.random.split(key, 32))
    nrm = lambda shape, scale: jax.random.normal(next(keys), shape, jnp.float32) * scale
    D, H, N = D_MODEL, N_HEADS, HEAD_DIM
    Lr, Ln = N_RWKV_LAYERS, N_NSA_LAYERS
    return {
        "x": nrm((BATCH, SEQ, D), 1.0),
        "norm_g": 1.0 + nrm((DEPTH, D), 0.02),
        "rwkv_mu": jax.random.uniform(next(keys), (Lr, N_SHIFT_MIX, D), jnp.float32),
        "rwkv_w_in": nrm((Lr, D, N_RWKV_PROJ * D), D ** -0.5),
        "rwkv_w0": nrm((Lr, D), 0.5),
        "rwkv_w1": nrm((Lr, D, DECAY_LORA), D ** -0.5),
        "rwkv_w2": nrm((Lr, DECAY_LORA, D), 0.5 * DECAY_LORA ** -0.5),
        "rwkv_a0": nrm((Lr, D), 0.1),
        "rwkv_a1": nrm((Lr, D, ICLR_LORA), D ** -0.5),
        "rwkv_a2": nrm((Lr, ICLR_LORA, D), 0.5 * ICLR_LORA ** -0.5),
        "rwkv_k_k": 0.85 + nrm((Lr, D), 0.02),
        "rwkv_k_a": 1.0 + nrm((Lr, D), 0.02),
        "rwkv_r_k": nrm((Lr, H, N), 0.1),
        "rwkv_lnx_g": 1.0 + nrm((Lr, D), 0.02),
        "rwkv_lnx_b": nrm((Lr, D), 0.02),
        "rwkv_w_out": nrm((Lr, D, D), D ** -0.5),
        "nsa_w_in": nrm((Ln, D, NSA_IN_WIDTH), D ** -0.5),
        "nsa_q_gain": 1.0 + nrm((Ln, N), 0.02),
        "nsa_k_gain": 1.0 + nrm((Ln, N_BRANCHES, N), 0.02),
        "nsa_cmp_pe": nrm((Ln, 2, CMP_BLOCK, N), 0.1),
        "nsa_cmp_w1": nrm((Ln, 2, CMP_BLOCK * N, CMP_HIDDEN), (CMP_BLOCK * N) ** -0.5),
        "nsa_cmp_b1": nrm((Ln, 2, CMP_HIDDEN), 0.02),
        "nsa_cmp_w2": nrm((Ln, 2, CMP_HIDDEN, N), CMP_HIDDEN ** -0.5),
        "nsa_w_out": nrm((Ln, D, D), D ** -0.5),
    }


def reference(x, norm_g, rwkv_mu, rwkv_w_in, rwkv_w0, rwkv_w1, rwkv_w2, rwkv_a0, rwkv_a1,
              rwkv_a2, rwkv_k_k, rwkv_k_a, rwkv_r_k, rwkv_lnx_g, rwkv_lnx_b, rwkv_w_out,
              nsa_w_in, nsa_q_gain, nsa_k_gain, nsa_cmp_pe, nsa_cmp_w1, nsa_cmp_b1, nsa_cmp_w2,
              nsa_w_out):
    for i in range(DEPTH):
        h = rms_norm(x, norm_g[i])
        j = i // N_MIXERS
        if i % N_MIXERS == 0:
            y = rwkv7_mixer(h, rwkv_mu[j], rwkv_w_in[j], rwkv_w0[j], rwkv_w1[j], rwkv_w2[j],
                            rwkv_a0[j], rwkv_a1[j], rwkv_a2[j], rwkv_k_k[j], rwkv_k_a[j],
                            rwkv_r_k[j], rwkv_lnx_g[j], rwkv_lnx_b[j], rwkv_w_out[j])
        else:
            y = nsa_mixer(h, nsa_w_in[j], nsa_q_gain[j], nsa_k_gain[j], nsa_cmp_pe[j],
                          nsa_cmp_w1[j], nsa_cmp_b1[j], nsa_cmp_w2[j], nsa_w_out[j])
        x = x + y
    return x
```

```python
from contextlib import ExitStack
from concourse.bass_utils import run_bass_kernel_spmd
import numpy as np
import concourse.bass as bass
import concourse.mybir as mybir

F32 = mybir.dt.float32
BF16 = mybir.dt.bfloat16
ALU = mybir.AluOpType
AF = mybir.ActivationFunctionType
AX = mybir.AxisListType

ENGINES = ("tensor", "vector", "scalar", "gpsimd", "sync")
CH = 30000


class Prog:
    def __init__(self, nc, stack, same_engine_sync=True):
        self.nc = nc
        self.stack = stack
        self.ops = {e: [] for e in ENGINES}
        self.cnt = {e: 0 for e in ENGINES}
        self.sems = {}
        self.res_w = {}
        self.res_r = {}
        self.dma_cnt = {}
        self.seen = {e: {} for e in ENGINES}
        self.same_engine_sync = same_engine_sync
        self.nwaits = 0
        self.max_ops = 10**9
        self.nops = 0
        self.last_line = None

    def sem(self, key):
        if key not in self.sems:
            name = "s_" + "_".join(str(k) for k in (key if isinstance(key, tuple) else (key,)))
            self.sems[key] = self.stack.enter_context(self.nc.semaphore(name))
        return self.sems[key]

    def _deps(self, eng, reads, writes, pe_accum=False):
        waits = {}

        def need(dep):
            if dep is None:
                return
            semkey, val, deng = dep
            if deng == eng and semkey[0] == "c":
                if not self.same_engine_sync:
                    return
                if eng == "tensor" and pe_accum:
                    return
            if self.seen[eng].get(semkey, 0) >= val:
                return
            if waits.get(semkey, 0) < val:
                waits[semkey] = val

        for r in reads:
            need(self.res_w.get(r))
        for w in writes:
            need(self.res_w.get(w))
            for rd in self.res_r.get(w, ()):
                need(rd)
        for k, v in waits.items():
            self.seen[eng][k] = v
        self.nwaits += len(waits)
        return list(waits.items())

    def _record(self, dep, reads, writes):
        for r in reads:
            self.res_r.setdefault(r, []).append(dep)
        for w in writes:
            self.res_w[w] = dep
            self.res_r[w] = []

    def op(self, eng, fn, reads=(), writes=(), pe_accum=False):
        isps = lambda r: isinstance(r, tuple) and isinstance(r[0], str) and r[0].endswith("pb")
        writes = list(writes) + [r for r in reads if isps(r)]
        reads = [r for r in reads if not isps(r)]
        waits = self._deps(eng, reads, writes, pe_accum)
        i = self.cnt[eng]
        self.cnt[eng] += 1
        semkey = ("c", eng, i // CH)
        self.sem(semkey)
        for k, _ in waits:
            self.sem(k)
        dep = (semkey, i % CH + 1, eng)
        self.ops[eng].append((waits, fn, semkey, 1))
        self._record(dep, reads, writes)

    def dma(self, eng, semname, out, in_, reads=(), writes=(), **kw):
        waits = self._deps(eng, reads, writes)
        semkey = ("d", semname)
        self.sem(semkey)
        for k, _ in waits:
            self.sem(k)
        n = self.dma_cnt.get(semname, 0) + 1
        self.dma_cnt[semname] = n
        dep = (semkey, 16 * n, eng)
        self.ops[eng].append((waits, lambda e: e.dma_start(out=out, in_=in_, **kw), semkey, 16))
        self._record(dep, reads, writes)

    def final_wait(self, eng, resources):
        waits = self._deps(eng, resources, ())
        for k, _ in waits:
            self.sem(k)
        self.ops[eng].append((waits, None, None, 0))

    def emit(self):
        nc = self.nc
        with nc.Block() as block:
            def mk(engname):
                def body(e):
                    for waits, fn, semkey, inc in self.ops[engname]:
                        for k, v in waits:
                            e.wait_ge(self.sems[k], v)
                        if fn is not None:
                            try:
                                ins = getattr(e, fn[0])(*fn[1][0], **fn[1][1]) if isinstance(fn, tuple) else fn(e)
                            except Exception:
                                print("EMIT FAIL", engname, fn[0] if isinstance(fn, tuple) else fn, {k: (v if not hasattr(v, "shape") else ("AP", v.shape)) for k, v in fn[1][1].items()} if isinstance(fn, tuple) else "")
                                raise
                            ins.then_inc(self.sems[semkey], inc)
                return body
            block.tensor(mk("tensor"))
            block.vector(mk("vector"))
            block.scalar(mk("scalar"))
            block.gpsimd(mk("gpsimd"))
            block.sync(mk("sync"))


def C(*a, **k):
    return (a, k)


D = 1024
NV = 13
I_W0, I_A0, I_KK, I_KA, I_RK, I_LG, I_LB = 6, 7, 8, 9, 10, 11, 12
DEC = -float(np.exp(-0.5))


def rwkv_host_vecs(inp):
    rows = [inp["rwkv_mu"][0][i] for i in range(6)] + [inp[k][0].reshape(-1) for k in
            ["rwkv_w0", "rwkv_a0", "rwkv_k_k", "rwkv_k_a"]]
    rows.append(np.tile(inp["rwkv_r_k"][0].reshape(16, 64), 1).reshape(-1))
    rows += [inp["rwkv_lnx_g"][0], inp["rwkv_lnx_b"][0]]
    v = np.stack([np.asarray(r, np.float32).reshape(8, 128) for r in rows], 0)
    return np.ascontiguousarray(v.transpose(2, 0, 1))


def emit_rwkv(nc, P, st, x, xo, wd, T, pfx="r", limit=99):
    NB = T // 256
    sbn = [0]

    def sb(shape, dt, name=None):
        sbn[0] += 1
        return st.enter_context(nc.sbuf_tensor(f"{pfx}_{name or 't'}{sbn[0]}", shape, dt))

    pball = st.enter_context(nc.psum_tensor(f"{pfx}_psum", [128, 8, 512], F32))
    pb = [pball[:, i, :] for i in range(8)]
    PB = lambda i: (pfx + "pb", i)

    RN = {"t1": "sq", "rk": "sq", "Lp": "rn", "eLp": "kkr", "BtT": "kf", "KtT": "a"}
    cn = lambda l: [RN.get(x, x) if isinstance(x, str) else x for x in l]

    def V(fn, r=(), w=()): P.op("vector", fn, cn(r), cn(w))
    def G(fn, r=(), w=()): P.op("gpsimd", fn, cn(r), cn(w))
    def A(fn, r=(), w=()): P.op("scalar", fn, cn(r), cn(w))

    def mm(out, lhsT, rhs, start=True, stop=True, r=(), w=()):
        P.op("tensor", ("matmul", C(out=out, lhsT=lhsT, rhs=rhs, start=start, stop=stop)), cn(r), cn(w), pe_accum=not start)

    def tr(out, in_, ident, r=(), w=()):
        P.op("tensor", ("transpose", C(out=out, in_=in_, identity=ident)), cn(r), cn(w))

    ones = sb([128, 512], F32, "ones")
    ident = sb([128, 128], F32, "ident")
    ident4 = sb([128, 4, 128], F32, "ident4")
    triS4 = sb([128, 4, 128], F32, "triS4")
    triI4 = sb([128, 4, 128], F32, "triI4")
    triL4 = sb([128, 4, 128], F32, "triL4")
    BD = sb([128, 128], F32, "BD")
    m01 = sb([128, 256], F32, "m01")
    G(("memset", C(ones[:], 1.0)), w=["ones"])
    G(("affine_select", C(out=ident[:], in_=ones[:, 0:128], pattern=[[-1, 128]], compare_op=ALU.is_equal,
                                fill=0.0, base=0, channel_multiplier=1)), r=["ones"], w=["ident"])
    o4 = ones[:].rearrange("p (a b) -> p a b", a=4)
    G(("affine_select", C(out=ident4[:], in_=o4, pattern=[[0, 4], [-1, 128]], compare_op=ALU.is_equal,
                                fill=0.0, base=0, channel_multiplier=1)), r=["ones"], w=["ident4"])
    G(("affine_select", C(out=triS4[:], in_=o4, pattern=[[0, 4], [1, 128]], compare_op=ALU.is_gt,
                                fill=0.0, base=0, channel_multiplier=-1)), r=["ones"], w=["triS4"])
    G(("affine_select", C(out=triI4[:], in_=o4, pattern=[[0, 4], [1, 128]], compare_op=ALU.is_ge,
                                fill=0.0, base=0, channel_multiplier=-1)), r=["ones"], w=["triI4"])
    G(("affine_select", C(out=triL4[:], in_=o4, pattern=[[0, 4], [-1, 128]], compare_op=ALU.is_gt,
                                fill=0.0, base=0, channel_multiplier=1)), r=["ones"], w=["triL4"])
    G(("memset", C(BD[:], 0.0)), w=["BD"])
    G(("memset", C(BD[0:64, 0:64], 1.0)), w=["BD"])
    G(("memset", C(BD[64:128, 64:128], 1.0)), w=["BD"])
    G(("memset", C(m01[:], 1.0)), w=["m01"])
    G(("memset", C(m01[:, 0:1], 0.0)), w=["m01"])
    G(("memset", C(m01[:, 128:129], 0.0)), w=["m01"])

    vecs = sb([128, NV, 8], F32, "vecs")
    gb = sb([128, D], F32, "gb")
    wslot = [sb([128, 8, 4, 128], BF16, f"wslot{i}") for i in range(2)]
    wscr = nc.dram_tensor(pfx + "_wscr", [8, 128, 8, 4, 128], BF16, kind="Internal").ap()
    wscr_w = wscr.rearrange("h p d c f -> p h d c f")
    woutb = sb([128, 8, 1024], BF16, "woutb")
    w1b = sb([128, 8, 64], BF16, "w1b")
    a1b = sb([128, 8, 64], BF16, "a1b")
    w2b = sb([64, 1024], BF16, "w2b")
    a2b = sb([64, 1024], BF16, "a2b")
    stg = [sb([128, 1024], F32, f"stg{i}") for i in range(2)]
    stgb = [sb([128, 1024], BF16, f"stgb{i}") for i in range(2)]
    P.dma("sync", pfx + "vecs", vecs[:], wd["vecs"], writes=["vecs"])
    P.dma("sync", pfx + "gb", gb[:], wd["g"].partition_broadcast(128), writes=["gb"])
    nst = [0]
    WS_ALL = [("wscr", c, ci) for c in range(8) for ci in range(4)]

    def load_cast(dst_ap, src_ap, np_, ncols, wres):
        i = nst[0] % 2
        nst[0] += 1
        q = "sync" if i == 0 else "gpsimd"
        P.dma(q, pfx + f"stg{i}", stg[i][0:np_, 0:ncols], src_ap, writes=[("stg", i)])
        eng = "vector" if i == 0 else "gpsimd"
        P.op(eng, ("tensor_copy", C(out=dst_ap, in_=stg[i][0:np_, 0:ncols])), [("stg", i)], [wres])

    win_v = wd["w_in"].rearrange("(c p) f -> p c f", p=128)
    for c in range(8):
        for ci in range(4):
            i = nst[0] % 2
            load_cast(stgb[i][:], win_v[:, c, ci * 1024:(ci + 1) * 1024], 128, 1024, ("stgb", i))
            P.dma("sync" if i == 0 else "gpsimd", pfx + f"wscr{i}", wscr_w[:, :, c, ci, :],
                  stgb[i][:].rearrange("p (h f) -> p h f", h=8), reads=[("stgb", i)], writes=[("wscr", c, ci)])
    wout_v = wd["w_out"].rearrange("(c p) f -> p c f", p=128)
    for c in range(8):
        load_cast(woutb[:, c, :], wout_v[:, c, :], 128, 1024, "woutb")
    load_cast(w1b[:], wd["w1"].rearrange("(c p) f -> p c f", p=128), 128, 512, "w1b")
    load_cast(a1b[:], wd["a1"].rearrange("(c p) f -> p c f", p=128), 128, 512, "a1b")
    load_cast(w2b[:], wd["w2"], 64, 1024, "w2b")
    load_cast(a2b[:], wd["a2"], 64, 1024, "a2b")

    if limit <= 0:
        return []
    xt = [sb([128, D], F32, f"xt{i}") for i in range(2)]
    ss = sb([128, 1], F32, "ss")
    rs = sb([128, 1], F32, "rs")
    ht = sb([128, D], F32, "ht")
    hT = sb([128, 8, 257], F32, "hT")
    dh = [sb([128, 256], F32, f"dh{i}") for i in range(2)]
    xm = sb([128, 6, 8, 256], BF16, "xm")
    la = sb([64, 256], BF16, "la")
    lw = sb([64, 256], BF16, "lw")
    rh = sb([128, 8, 256], BF16, "rh")
    ah = sb([128, 8, 256], BF16, "ah")
    bh = sb([128, 8, 256], BF16, "bh")
    kh = sb([128, 8, 256], BF16, "kh")
    Vt = sb([128, 2, 1024], BF16, "Vt")
    Bt = sb([128, 2, 1024], BF16, "Bt")
    Kt = sb([128, 2, 1024], BF16, "Kt")
    sz = sb([128, 8, 256], BF16, "sz")
    bonus = sb([128, 8, 256], BF16, "bonus")
    gC = sb([128, 8, 2], F32, "gC")
    tmp = {n: sb([128, 256], F32, n) for n in
           ["kf", "kkr", "sq", "rn", "kk", "a", "kp", "bb", "sig", "L", "eL", "enL", "E2", "vf"]}
    for k_, v_ in RN.items():
        tmp[k_] = tmp[v_]
    ST = sb([128, 8, 64], F32, "ST")
    STb = sb([128, 8, 64], BF16, "STb")
    Q = [sb([128, 4, 128], BF16, f"Q{i}") for i in range(2)]
    QT = [sb([128, 4, 128], BF16, f"QT{i}") for i in range(2)]
    Z = sb([128, 4, 128], F32, "Z")
    Zb = sb([128, 4, 128], BF16, "Zb")
    QTf = sb([128, 4, 128], F32, "QTf")
    WT = sb([128, 16, 128], BF16, "WT")
    Mak = sb([128, 16, 128], BF16, "Mak")
    Mrb = sb([128, 16, 128], BF16, "Mrb")
    Mrk = sb([128, 16, 128], BF16, "Mrk")
    Xn = sb([128, 1024], BF16, "Xn")
    Ub = sb([128, 1024], BF16, "Ub")
    of = sb([128, 16, 64], F32, "of")
    mean = sb([128, 16], F32, "mean")
    ex2 = sb([128, 16], F32, "ex2")
    var = sb([128, 16], F32, "var")
    yT = sb([128, 8, 128], BF16, "yT")
    ytmp = sb([128, 128], F32, "ytmp")
    xr = xt[0]
    outt = sb([128, D], F32, "outt")
    osq = outt[:].rearrange("p (a b) -> p a b", a=16)

    G(("memset", C(ST[:], 0.0)), w=["ST"])
    G(("memset", C(STb[:], 0.0)), w=["STb"])
    G(("memset", C(hT[:, :, 0:1], 0.0)), w=["hT"])

    vcol = lambda i, hp: vecs[:, i, hp:hp + 1]
    eps = 1e-6

    for b in range(NB):
        t0 = b * 256
        for i in range(2):
            xb = xt[i]
            P.dma("sync", pfx + f"x{i}", xb[:], x[t0 + i * 128:t0 + (i + 1) * 128, :], writes=[("xt", i)])
            A(("activation", C(out=ht[:], in_=xb[:], func=AF.Square, accum_out=ss[:])), [("xt", i)], ["ht", "ss"])
            A(("activation", C(out=rs[:], in_=ss[:], func=AF.Sqrt, scale=1.0 / D, bias=eps)), ["ss"], ["rs"])
            V(("reciprocal", C(out=rs[:], in_=rs[:])), ["rs"], ["rs"])
            V(("scalar_tensor_tensor", C(out=ht[:], in0=xb[:], scalar=rs[:, 0:1], in1=gb[:], op0=ALU.mult, op1=ALU.mult)),
              [("xt", i), "rs", "gb"], ["ht"])
            for half in range(2):
                for c in range(4):
                    cc = half * 4 + c
                    tr(pb[half][:, c * 128:(c + 1) * 128], ht[:, cc * 128:(cc + 1) * 128], ident[:], ["ht", "ident"], [PB(half)])
                pv = pb[half].rearrange("p (c t) -> p c t", c=4)
                dst = hT[:, half * 4:(half + 1) * 4, 1 + i * 128:1 + (i + 1) * 128]
                if half == 0:
                    V(("tensor_copy", C(out=dst, in_=pv)), [PB(half)], ["hT"])
                else:
                    A(("copy", C(out=dst, in_=pv)), [PB(half)], ["hT"])
        if limit <= 1:
            return []
        n = 0
        for dc in range(8):
            dd = dh[dc % 2]
            V(("tensor_tensor", C(out=dd[:], in0=hT[:, dc, 0:256], in1=hT[:, dc, 1:257], op=ALU.subtract)),
              ["hT"], [("dh", dc % 2)])
            for c in range(6):
                fn = ("scalar_tensor_tensor", C(out=xm[:, c, dc, :], in0=dd[:], scalar=vcol(c, dc),
                                                                        in1=hT[:, dc, 1:257], op0=ALU.mult, op1=ALU.add))
                V(fn, [("dh", dc % 2), "hT", "vecs"], [("xm", c)])
                n += 1
        V(("tensor_copy", C(out=hT[:, :, 0:1], in_=hT[:, :, 256:257])), ["hT"], ["hT"])
        if limit <= 2:
            return []
        for dc in range(8):
            mm(pb[2][0:64, 0:256], a1b[:, dc, :], xm[:, 5, dc, :], dc == 0, dc == 7, [("xm", 5), "a1b"], [PB(2)])
        V(("tensor_copy", C(out=la[:], in_=pb[2][0:64, 0:256])), [PB(2)], ["la"])
        for dc in range(8):
            mm(pb[3][0:64, 0:256], w1b[:, dc, :], xm[:, 4, dc, :], dc == 0, dc == 7, [("xm", 4), "w1b"], [PB(3)])
        A(("activation", C(out=lw[:], in_=pb[3][0:64, 0:256], func=AF.Tanh)), [PB(3)], ["lw"])
        if limit <= 3:
            return []
        for hp in range(8):
            fs = slice(hp * 128, (hp + 1) * 128)
            n_it = b * 8 + hp
            if n_it == 0:
                P.dma("sync", pfx + "ws0", wslot[0][:], wscr[0], reads=WS_ALL, writes=[("wslot", 0)])
            if n_it + 1 < NB * 8:
                sl_ = (n_it + 1) % 2
                P.dma("sync", pfx + f"ws{sl_}", wslot[sl_][:], wscr[(hp + 1) % 8], reads=WS_ALL, writes=[("wslot", sl_)])
            wsl = wslot[n_it % 2]
            pR, pK, pV_, pZ = pb[0][:, 0:256], pb[0][:, 256:512], pb[1][:, 0:256], pb[1][:, 256:512]
            pA, pU = pb[2][:, 0:256], pb[2][:, 256:512]
            pN, pBS = pb[3][:, 0:256], pb[3][:, 256:512]
            for ci, (po, pbi) in enumerate([(pR, 0), (pK, 0), (pV_, 1), (pZ, 1)]):
                for dc in range(8):
                    mm(po, wsl[:, dc, ci, :], xm[:, ci, dc, :], dc == 0, dc == 7,
                       [("xm", ci), ("wslot", n_it % 2)], [PB(pbi)])
            mm(pA, a2b[0:64, fs], la[:], True, True, ["a2b", "la"], [PB(2)])
            mm(pU, w2b[0:64, fs], lw[:], True, True, ["w2b", "lw"], [PB(2)])
            t = tmp
            A(("copy", C(out=t["kf"][:], in_=pK)), [PB(0)], ["kf"])
            V(("tensor_scalar", C(out=t["kkr"][:], in0=pK, scalar1=vcol(I_KK, hp), scalar2=None, op0=ALU.mult)), [PB(0), "vecs"], ["kkr"])
            A(("activation", C(out=t["sq"][:], in_=t["kkr"][:], func=AF.Square)), ["kkr"], ["sq"])
            mm(pN, BD[:], t["sq"][:], True, True, ["BD", "sq"], [PB(3)])
            A(("activation", C(out=t["rn"][:], in_=pN, func=AF.Sqrt)), [PB(3)], ["rn"])
            V(("tensor_scalar", C(out=t["rn"][:], in0=t["rn"][:], scalar1=1e-12, scalar2=None, op0=ALU.max)), ["rn"], ["rn"])
            V(("reciprocal", C(out=t["rn"][:], in_=t["rn"][:])), ["rn"], ["rn"])
            V(("tensor_tensor", C(out=t["kk"][:], in0=t["kkr"][:], in1=t["rn"][:], op=ALU.mult)), ["kkr", "rn"], ["kk"])
            A(("activation", C(out=t["a"][:], in_=pA, func=AF.Sigmoid, bias=vcol(I_A0, hp))), [PB(2), "vecs"], ["a"])
            V(("tensor_scalar", C(out=t["t1"][:], in0=t["a"][:], scalar1=-1.0, scalar2=vcol(I_KA, hp), op0=ALU.add, op1=ALU.mult)),
              ["a", "vecs"], ["t1"])
            V(("scalar_tensor_tensor", C(out=t["kp"][:], in0=t["t1"][:], scalar=1.0, in1=t["kf"][:], op0=ALU.add, op1=ALU.mult)),
              ["t1", "kf"], ["kp"])
            G(("tensor_tensor", C(out=t["bb"][:], in0=t["kk"][:], in1=t["a"][:], op=ALU.mult)), ["kk", "a"], ["bb"])
            A(("activation", C(out=t["sig"][:], in_=pU, func=AF.Sigmoid, bias=vcol(I_W0, hp))), [PB(2), "vecs"], ["sig"])
            G(("tensor_scalar", C(out=t["sig"][:], in0=t["sig"][:], scalar1=DEC, scalar2=None, op0=ALU.mult)), ["sig"], ["sig"])
            V(("tensor_tensor_scan", C(out=t["L"][:], data0=m01[:], data1=t["sig"][:], initial=0.0, op0=ALU.mult, op1=ALU.add)),
              ["m01", "sig"], ["L"])
            G(("tensor_tensor", C(out=t["Lp"][:], in0=t["L"][:], in1=t["sig"][:], op=ALU.subtract)), ["L", "sig"], ["Lp"])
            A(("activation", C(out=t["eL"][:], in_=t["L"][:], func=AF.Exp)), ["L"], ["eL"])
            A(("activation", C(out=t["eLp"][:], in_=t["Lp"][:], func=AF.Exp)), ["Lp"], ["eLp"])
            A(("activation", C(out=t["enL"][:], in_=t["L"][:], func=AF.Exp, scale=-1.0)), ["L"], ["enL"])
            for j in range(2):
                cs = slice(j * 128, (j + 1) * 128)
                A(("activation", C(out=t["E2"][:, cs], in_=t["L"][:, cs], func=AF.Exp, scale=-1.0,
                                                    bias=t["L"][:, j * 128 + 127:j * 128 + 128])), ["L"], ["E2"])
            V(("tensor_tensor", C(out=rh[:, hp, :], in0=pR, in1=t["eL"][:], op=ALU.mult)), [PB(0), "eL"], [("rh", hp)])
            V(("scalar_tensor_tensor", C(out=t["rk"][:], in0=pR, scalar=vcol(I_RK, hp), in1=t["kp"][:], op0=ALU.mult, op1=ALU.mult)),
              [PB(0), "vecs", "kp"], ["rk"])
            mm(pBS, BD[:], t["rk"][:], True, True, ["BD", "rk"], [PB(3)])
            A(("copy", C(out=t["vf"][:], in_=pV_)), [PB(1)], ["vf"])
            V(("tensor_tensor", C(out=bonus[:, hp, :], in0=pBS, in1=t["vf"][:], op=ALU.mult)), [PB(3), "vf"], [("bonus", hp)])
            A(("activation", C(out=sz[:, hp, :], in_=pZ, func=AF.Silu)), [PB(1)], [("sz", hp)])
            G(("tensor_tensor", C(out=ah[:, hp, :], in0=t["kk"][:], in1=t["eLp"][:], op=ALU.mult)), ["kk", "eLp"], [("ah", hp)])
            G(("tensor_tensor", C(out=bh[:, hp, :], in0=t["bb"][:], in1=t["enL"][:], op=ALU.mult)), ["bb", "enL"], [("bh", hp)])
            V(("tensor_tensor", C(out=kh[:, hp, :], in0=t["kp"][:], in1=t["enL"][:], op=ALU.mult)), ["kp", "enL"], [("kh", hp)])
            G(("tensor_tensor", C(out=t["BtT"][:], in0=t["bb"][:], in1=t["E2"][:], op=ALU.mult)), ["bb", "E2"], ["BtT"])
            V(("tensor_tensor", C(out=t["KtT"][:], in0=t["kp"][:], in1=t["E2"][:], op=ALU.mult)), ["kp", "E2"], ["KtT"])
            V(("tensor_copy", C(out=gC[:, hp, :], in_=t["eL"][:, 127:256:128])), ["eL"], ["gC"])
            for j in range(2):
                cs = slice(j * 128, (j + 1) * 128)
                for si, (src, sres) in enumerate([(t["BtT"], "BtT"), (t["KtT"], "KtT"), (t["vf"], "vf")]):
                    tr(pb[4 + j][:, si * 128:(si + 1) * 128], src[:, cs], ident[:], [sres, "ident"], [PB(4 + j)])
                V(("tensor_copy", C(out=Bt[:, j, fs], in_=pb[4 + j][:, 0:128])), [PB(4 + j)], [("Bt", j)])
                A(("copy", C(out=Kt[:, j, fs], in_=pb[4 + j][:, 128:256])), [PB(4 + j)], [("Kt", j)])
                V(("tensor_copy", C(out=Vt[:, j, fs], in_=pb[4 + j][:, 256:384])), [PB(4 + j)], [("Vt", j)])
        if limit <= 4:
            return []
        for j in range(2):
            ts = slice(j * 128, (j + 1) * 128)
            for hg in range(4):
                for q in range(4):
                    h = hg * 4 + q
                    hp, hh = h // 2, h % 2
                    ps_ = slice(hh * 64, hh * 64 + 64)
                    qs = slice(q * 128, (q + 1) * 128)
                    a_, b_, k_, r_ = ah[ps_, hp, ts], bh[ps_, hp, ts], kh[ps_, hp, ts], rh[ps_, hp, ts]
                    mm(pb[0][:, qs], b_, a_, True, True, [("ah", hp), ("bh", hp)], [PB(0)])
                    mm(pb[1][:, qs], a_, b_, True, True, [("ah", hp), ("bh", hp)], [PB(1)])
                    mm(pb[2][:, qs], k_, a_, True, True, [("ah", hp), ("kh", hp)], [PB(2)])
                    mm(pb[3][:, qs], b_, r_, True, True, [("rh", hp), ("bh", hp)], [PB(3)])
                    mm(pb[4][:, qs], k_, r_, True, True, [("rh", hp), ("kh", hp)], [PB(4)])
                p4 = lambda i: pb[i].rearrange("p (a b) -> p a b", a=4)
                hs = slice(hg * 4, hg * 4 + 4)
                V(("scalar_tensor_tensor", C(out=QTf[:], in0=p4(0), scalar=-1.0, in1=triS4[:], op0=ALU.mult, op1=ALU.mult)),
                  [PB(0), "triS4"], ["QTf"])
                V(("scalar_tensor_tensor", C(out=Q[0][:], in0=p4(1), scalar=-1.0, in1=triL4[:], op0=ALU.mult, op1=ALU.mult)),
                  [PB(1), "triL4"], [("Q", 0)])
                G(("tensor_copy", C(out=QT[0][:], in_=QTf[:])), ["QTf"], [("QT", 0)])
                G(("tensor_tensor", C(out=Z[:], in0=QTf[:], in1=ident4[:], op=ALU.add)), ["QTf", "ident4"], ["Z"])
                G(("tensor_copy", C(out=Zb[:], in_=Z[:])), ["Z"], ["Zb"])
                V(("tensor_tensor", C(out=Mak[:, hs, :], in0=p4(2), in1=triS4[:], op=ALU.mult)), [PB(2), "triS4"], ["Mak"])
                V(("tensor_tensor", C(out=Mrb[:, hs, :], in0=p4(3), in1=triI4[:], op=ALU.mult)), [PB(3), "triI4"], ["Mrb"])
                V(("tensor_tensor", C(out=Mrk[:, hs, :], in0=p4(4), in1=triI4[:], op=ALU.mult)), [PB(4), "triI4"], ["Mrk"])
                cur = 0
                for lvl in range(6):
                    nxt = 1 - cur
                    for q in range(4):
                        qs = slice(q * 128, (q + 1) * 128)
                        mm(pb[5][:, qs], QT[cur][:, q, :], Q[cur][:, q, :], True, True, [("Q", cur), ("QT", cur)], [PB(5)])
                        mm(pb[6][:, qs], Q[cur][:, q, :], QT[cur][:, q, :], True, True, [("Q", cur), ("QT", cur)], [PB(6)])
                    V(("tensor_copy", C(out=Q[nxt][:], in_=p4(5))), [PB(5)], [("Q", nxt)])
                    A(("copy", C(out=QT[nxt][:], in_=p4(6))), [PB(6)], [("QT", nxt)])
                    for q in range(4):
                        qs = slice(q * 128, (q + 1) * 128)
                        mm(pb[7][:, qs], Q[nxt][:, q, :], Zb[:, q, :], True, True, [("Q", nxt), "Zb"], [PB(7)])
                    V(("tensor_tensor", C(out=Z[:], in0=p4(7), in1=Z[:], op=ALU.add)), [PB(7), "Z"], ["Z"])
                    if lvl < 5:
                        G(("tensor_copy", C(out=Zb[:], in_=Z[:])), ["Z"], ["Zb"])
                    else:
                        G(("tensor_copy", C(out=WT[:, hs, :], in_=Z[:])), ["Z"], ["WT"])
                    cur = nxt
            if limit <= 5:
                return []
            pX = pball[:, 0:2, :].rearrange("p a b -> p (a b)")
            pUu = pball[:, 2:4, :].rearrange("p a b -> p (a b)")
            pO = pball[:, 4:6, :].rearrange("p a b -> p (a b)")
            pS = pb[6]
            hd = lambda h: (h // 2, slice((h % 2) * 64, (h % 2) * 64 + 64), slice(h * 64, (h + 1) * 64))
            for h in range(16):
                hp, ps_, vs = hd(h)
                mm(pX[:, vs], ah[ps_, hp, ts], STb[ps_, hp, :], True, False, [("ah", hp), "STb"], [PB(h // 8)])
                mm(pX[:, vs], Mak[:, h, :], Vt[:, j, vs], False, True, ["Mak", ("Vt", j)], [PB(h // 8)])
            V(("tensor_scalar", C(out=Xn[:, 0:512], in0=pX[:, 0:512], scalar1=-1.0, scalar2=None, op0=ALU.mult)), [PB(0)], ["Xn"])
            A(("mul", C(out=Xn[:, 512:1024], in_=pX[:, 512:1024], mul=-1.0)), [PB(1)], ["Xn"])
            for h in range(16):
                hp, ps_, vs = hd(h)
                mm(pUu[:, vs], WT[:, h, :], Xn[:, vs], True, True, ["WT", "Xn"], [PB(2 + h // 8)])
            V(("tensor_copy", C(out=Ub[:, 0:512], in_=pUu[:, 0:512])), [PB(2)], ["Ub"])
            A(("copy", C(out=Ub[:, 512:1024], in_=pUu[:, 512:1024])), [PB(3)], ["Ub"])
            for h in range(16):
                hp, ps_, vs = hd(h)
                mm(pO[:, vs], rh[ps_, hp, ts], STb[ps_, hp, :], True, False, [("rh", hp), "STb"], [PB(4 + h // 8)])
                mm(pO[:, vs], Mrb[:, h, :], Ub[:, vs], False, False, ["Mrb", "Ub"], [PB(4 + h // 8)])
                mm(pO[:, vs], Mrk[:, h, :], Vt[:, j, vs], False, True, ["Mrk", ("Vt", j)], [PB(4 + h // 8)])
            for h in range(16):
                hp, ps_, vs = hd(h)
                mm(pS[ps_, hp * 64:(hp + 1) * 64], Bt[:, j, vs], Ub[:, vs], True, False, [("Bt", j), "Ub"], [PB(6)])
                mm(pS[ps_, hp * 64:(hp + 1) * 64], Kt[:, j, vs], Vt[:, j, vs], False, True, [("Kt", j), ("Vt", j)], [PB(6)])
            ofl = of[:].rearrange("p a b -> p (a b)")
            V(("tensor_copy", C(out=ofl[:, 0:512], in_=pO[:, 0:512])), [PB(4)], ["of"])
            A(("copy", C(out=ofl[:, 512:1024], in_=pO[:, 512:1024])), [PB(5)], ["of"])
            for hp in range(8):
                V(("scalar_tensor_tensor", C(out=ST[:, hp, :], in0=ST[:, hp, :], scalar=gC[:, hp, j:j + 1],
                                                         in1=pS[:, hp * 64:(hp + 1) * 64], op0=ALU.mult, op1=ALU.add)),
                  ["ST", "gC", PB(6)], ["ST"])
            G(("tensor_copy", C(out=STb[:], in_=ST[:])), ["ST"], ["STb"])
            if limit <= 6:
                return []
            G(("tensor_tensor", C(out=osq, in0=of[:], in1=of[:], op=ALU.mult)), ["of"], ["outt"])
            V(("tensor_reduce", C(out=mean[:], in_=of[:], axis=AX.X, op=ALU.add)), ["of"], ["mean"])
            V(("tensor_reduce", C(out=ex2[:], in_=osq, axis=AX.X, op=ALU.add)), ["outt"], ["ex2"])
            V(("tensor_scalar", C(out=mean[:], in0=mean[:], scalar1=1.0 / 64, scalar2=None, op0=ALU.mult)), ["mean"], ["mean"])
            V(("tensor_tensor", C(out=var[:], in0=mean[:], in1=mean[:], op=ALU.mult)), ["mean"], ["var"])
            V(("scalar_tensor_tensor", C(out=var[:], in0=ex2[:], scalar=1.0 / 64, in1=var[:], op0=ALU.mult, op1=ALU.subtract)),
              ["ex2", "var"], ["var"])
            A(("activation", C(out=var[:], in_=var[:], func=AF.Sqrt, bias=64e-5)), ["var"], ["var"])
            V(("reciprocal", C(out=var[:], in_=var[:])), ["var"], ["var"])
            for h in range(16):
                eng = V if h % 2 == 0 else G
                eng(("tensor_scalar", C(out=of[:, h, :], in0=of[:, h, :], scalar1=mean[:, h:h + 1], scalar2=var[:, h:h + 1],
                                                   op0=ALU.subtract, op1=ALU.mult)), ["of", "mean", "var"], ["of"])
            for hp in range(8):
                tr(pb[hp // 4][:, (hp % 4) * 128:(hp % 4 + 1) * 128], ofl[:, hp * 128:(hp + 1) * 128], ident[:], ["of", "ident"], [PB(hp // 4)])
            for hp in range(8):
                src = pb[hp // 4][:, (hp % 4) * 128:(hp % 4 + 1) * 128]
                V(("scalar_tensor_tensor", C(out=ytmp[:], in0=src, scalar=vcol(I_LG, hp), in1=bonus[:, hp, ts],
                                                                 op0=ALU.mult, op1=ALU.add)), [PB(hp // 4), "vecs", ("bonus", hp)], ["ytmp"])
                V(("scalar_tensor_tensor", C(out=yT[:, hp, :], in0=ytmp[:], scalar=vcol(I_LB, hp), in1=sz[:, hp, ts],
                                                         op0=ALU.add, op1=ALU.mult)), ["ytmp", "vecs", ("sz", hp)], ["yT"])
            P.dma("gpsimd", pfx + "xr", xr[:], x[t0 + j * 128:t0 + (j + 1) * 128, :], writes=[("xt", 0)])
            for hf in range(2):
                for dc in range(8):
                    mm(pb[2 + hf], yT[:, dc, :], woutb[:, dc, hf * 512:(hf + 1) * 512], dc == 0, dc == 7, ["yT", "woutb"], [PB(2 + hf)])
                V(("tensor_tensor", C(out=outt[:, hf * 512:(hf + 1) * 512], in0=pb[2 + hf], in1=xr[:, hf * 512:(hf + 1) * 512],
                                                  op=ALU.add)), [PB(2 + hf), ("xt", 0)], ["outt"])
            P.dma("sync", pfx + "xo", xo[t0 + j * 128:t0 + (j + 1) * 128, :], outt[:], reads=["outt"], writes=[(pfx + "xo", b * 2 + j)])
    return [(pfx + "xo", i) for i in range(NB * 2)]


NEG = -30000.0
GELU_C = 1.5957691216057308


def nsa_host(inp):
    qg = inp["nsa_q_gain"][0]
    kg = inp["nsa_k_gain"][0]
    gains = np.stack([np.tile(qg, 2), np.tile(kg[0], 2), np.tile(kg[1], 2), np.tile(kg[2], 2)], 1).astype(np.float32)
    pe = inp["nsa_cmp_pe"][0]
    peT = np.concatenate([pe[0].T, pe[1].T], 0).astype(np.float32)
    b1 = inp["nsa_cmp_b1"][0].reshape(2, 2, 128).transpose(2, 0, 1).astype(np.float32)
    w2 = inp["nsa_cmp_w2"][0].reshape(2, 2, 128, 64).transpose(2, 1, 0, 3).astype(np.float32)
    return dict(gains=np.ascontiguousarray(gains), peT=np.ascontiguousarray(peT), b1=np.ascontiguousarray(b1),
                w2=np.ascontiguousarray(w2.reshape(128, 256)))


def emit_nsa(nc, P, st, x, xo, wd, T, pfx="n", limit=99):
    D = 1024
    NB = T // 256
    NT = T // 128
    sbn = [0]

    def sb(shape, dt, name=None):
        sbn[0] += 1
        return st.enter_context(nc.sbuf_tensor(f"{pfx}_{name or 't'}{sbn[0]}", shape, dt))

    pball = st.enter_context(nc.psum_tensor(f"{pfx}_psum", [128, 8, 512], F32))
    pb = [pball[:, i, :] for i in range(8)]
    PB = lambda i: (pfx + "pb", i)

    def V(fn, r=(), w=()): P.op("vector", fn, r, w)
    def G(fn, r=(), w=()): P.op("gpsimd", fn, r, w)
    def A(fn, r=(), w=()): P.op("scalar", fn, r, w)

    def mm(out, lhsT, rhs, start=True, stop=True, r=(), w=()):
        P.op("tensor", ("matmul", C(out=out, lhsT=lhsT, rhs=rhs, start=start, stop=stop)), r, w, pe_accum=not start)

    def tr(out, in_, ident, r=(), w=()):
        P.op("tensor", ("transpose", C(out=out, in_=in_, identity=ident)), r, w)

    ones = sb([128, 512], F32, "ones")
    onesb = sb([64, 2048], BF16, "onesb")
    zerob = sb([128, 4, 128], BF16, "zerob")
    ident = sb([128, 128], F32, "ident")
    identb = sb([128, 128], BF16, "identb")
    BD = sb([128, 128], F32, "BD")
    CB = sb([128, 4, 128], BF16, "CB")
    AB = sb([128, 4, 128], BF16, "AB")
    E = sb([128, 32, 128], BF16, "E")
    ov = sb([128, 2, 64], BF16, "ov")
    ovf = sb([128, 2, 64], F32, "ovf")
    G(("memset", C(ones[:], 1.0)), w=["ones"])
    G(("memset", C(onesb[:], 1.0)), w=["onesb"])
    G(("memset", C(zerob[:], 0.0)), w=["zerob"])
    G(("affine_select", C(out=ident[:], in_=ones[:, 0:128], pattern=[[-1, 128]], compare_op=ALU.is_equal, fill=0.0, base=0,
                          channel_multiplier=1)), ["ones"], ["ident"])
    G(("tensor_copy", C(out=identb[:], in_=ident[:])), ["ident"], ["identb"])
    G(("memset", C(BD[:], 0.0)), w=["BD"])
    G(("memset", C(BD[0:64, 0:64], 1.0)), w=["BD"])
    G(("memset", C(BD[64:128, 64:128], 1.0)), w=["BD"])
    G(("affine_select", C(out=CB[:], in_=zerob[:], pattern=[[0, 4], [1, 128]], compare_op=ALU.is_ge, fill=NEG, base=0,
                          channel_multiplier=-1)), ["zerob"], ["CB"])
    G(("affine_select", C(out=AB[:], in_=zerob[:], pattern=[[0, 4], [-1, 128]], compare_op=ALU.is_gt, fill=NEG, base=0,
                          channel_multiplier=1)), ["zerob"], ["AB"])
    ob3 = onesb[:].rearrange("p (a b) -> p a b", a=32)
    G(("memset", C(E[:], 0.0)), w=["E"])
    G(("affine_select", C(out=E[0:64, :, 0:64], in_=ob3, pattern=[[-2, 32], [0, 64]], compare_op=ALU.is_equal, fill=0.0, base=0,
                          channel_multiplier=1)), ["onesb"], ["E"])
    G(("affine_select", C(out=E[0:64, :, 64:128], in_=ob3, pattern=[[-2, 32], [0, 64]], compare_op=ALU.is_equal, fill=0.0, base=-1,
                          channel_multiplier=1)), ["onesb"], ["E"])
    for c in range(2):
        G(("affine_select", C(out=ovf[:, c, :], in_=ones[:, 0:64], pattern=[[-64, 64]], compare_op=ALU.is_ge, fill=0.0,
                              base=2048 * c + 31, channel_multiplier=16)), ["ones"], ["ovf"])
        G(("affine_select", C(out=ovf[:, c, :], in_=ovf[:, c, :], pattern=[[64, 64]], compare_op=ALU.is_ge, fill=0.0,
                              base=63 - 2048 * c, channel_multiplier=-16)), ["ovf"], ["ovf"])
    G(("tensor_copy", C(out=ov[:], in_=ovf[:])), ["ovf"], ["ov"])

    gb = sb([128, D], F32, "gb")
    gains = sb([128, 4], F32, "gains")
    peT = sb([128, 32], F32, "peT")
    peTb = sb([128, 32], BF16, "peTb")
    b1 = sb([128, 2, 2], F32, "b1")
    cb = sb([128, 2, 2], F32, "cb")
    w2f = sb([128, 256], F32, "w2f")
    w2b = sb([128, 2, 2, 64], BF16, "w2b")
    w1b = sb([128, 32, 256], BF16, "w1b")
    woutb = sb([128, 8, 1024], BF16, "woutb")
    stg = [sb([128, 1024], F32, f"stg{i}") for i in range(2)]
    stgb = [sb([128, 1024], BF16, f"stgb{i}") for i in range(2)]
    wslot = [sb([128, 8, 128], BF16, f"wslot{i}") for i in range(2)]
    NCH = 29
    wscr = nc.dram_tensor(pfx + "_wscr", [NCH, 128, 8, 128], BF16, kind="Internal").ap()
    P.dma("sync", pfx + "gb", gb[:], wd["g"].partition_broadcast(128), writes=["gb"])
    P.dma("sync", pfx + "gains", gains[:], wd["gains"], writes=["gains"])
    P.dma("sync", pfx + "peT", peT[:], wd["peT"], writes=["peT"])
    P.dma("sync", pfx + "b1", b1[:].rearrange("p a b -> p (a b)"), wd["b1"].rearrange("p a b -> p (a b)"), writes=["b1"])
    P.dma("sync", pfx + "w2f", w2f[:], wd["w2"], writes=["w2f"])
    V(("tensor_copy", C(out=w2b[:].rearrange("p a b c -> p (a b c)"), in_=w2f[:])), ["w2f"], ["w2b"])
    V(("tensor_copy", C(out=peTb[:], in_=peT[:])), ["peT"], ["peTb"])
    V(("tensor_scalar", C(out=gains[:, 0:1], in0=gains[:, 0:1], scalar1=0.125, scalar2=None, op0=ALU.mult)), ["gains"], ["gains"])
    nst = [0]

    def load_cast(dst_ap, src_ap, np_, ncols, wres, p0=0):
        i = nst[0] % 2
        nst[0] += 1
        q = "sync" if i == 0 else "gpsimd"
        P.dma(q, pfx + f"stg{i}", stg[i][p0:p0 + np_, 0:ncols], src_ap, writes=[("stg", i)])
        eng = "vector" if i == 0 else "gpsimd"
        P.op(eng, ("tensor_copy", C(out=dst_ap, in_=stg[i][p0:p0 + np_, 0:ncols])), [("stg", i)], [wres])
        return i

    CQ, CKS, CKW, CZ, CCV, CVS, CVW, CG = 0, 8, 10, 12, 20, 24, 26, 28
    win_v = wd["w_in"].rearrange("(c p) f -> p c f", p=128)
    WS_ALL = []

    def scr_write(i, dst, src, key):
        P.dma("sync" if i == 0 else "gpsimd", pfx + f"wscr{i}", dst, src, reads=[("stgb", i)], writes=[key])
        WS_ALL.append(key)

    for dc in range(8):
        i = nst[0] % 2
        load_cast(stgb[i][:], win_v[:, dc, 0:1024], 128, 1024, ("stgb", i))
        srcv = stgb[i][:].rearrange("p (m e j n) -> p m e j n", m=2, e=2, j=4)
        for m in range(2):
            for e in range(2):
                dst = wscr[CQ + m * 4:CQ + m * 4 + 4, :, dc, e * 64:(e + 1) * 64].rearrange("j p n -> p j n")
                scr_write(i, dst, srcv[:, m, e, :, :], ("wscr", "q", dc, m, e))
        i = nst[0] % 2
        load_cast(stgb[i][:], win_v[:, dc, 1024:2048], 128, 1024, ("stgb", i))
        s4 = stgb[i][:].rearrange("p (a g n) -> p a g n", a=4, g=4)
        for a_ in range(2):
            dst = wscr[CCV:CCV + 4, :, dc, a_ * 64:(a_ + 1) * 64].rearrange("g p n -> p g n")
            scr_write(i, dst, s4[:, a_, :, :], ("wscr", "cv", dc, a_))
        s2 = stgb[i][:].rearrange("p (a c f) -> p a c f", a=4, c=2)
        scr_write(i, wscr[CKS:CKS + 2, :, dc, :].rearrange("c p f -> p c f"), s2[:, 2, :, :], ("wscr", "ks", dc))
        scr_write(i, wscr[CVS:CVS + 2, :, dc, :].rearrange("c p f -> p c f"), s2[:, 3, :, :], ("wscr", "vs", dc))
        i = nst[0] % 2
        load_cast(stgb[i][:], win_v[:, dc, 2048:3072], 128, 1024, ("stgb", i))
        s8 = stgb[i][:].rearrange("p (c f) -> p c f", c=8)
        scr_write(i, wscr[CKW:CKW + 2, :, dc, :].rearrange("c p f -> p c f"), s8[:, 0:2, :], ("wscr", "kw", dc))
        scr_write(i, wscr[CVW:CVW + 2, :, dc, :].rearrange("c p f -> p c f"), s8[:, 2:4, :], ("wscr", "vw", dc))
        scr_write(i, wscr[CZ:CZ + 4, :, dc, :].rearrange("c p f -> p c f"), s8[:, 4:8, :], ("wscr", "z0", dc))
        i = nst[0] % 2
        load_cast(stgb[i][:, 0:560], win_v[:, dc, 3072:3632], 128, 560, ("stgb", i))
        s5 = stgb[i][:, 0:512].rearrange("p (c f) -> p c f", c=4)
        scr_write(i, wscr[CZ + 4:CZ + 8, :, dc, :].rearrange("c p f -> p c f"), s5, ("wscr", "z1", dc))
        scr_write(i, wscr[CG, :, dc, 0:48], stgb[i][:, 512:560], ("wscr", "g", dc))
    wout_v = wd["w_out"].rearrange("(c p) f -> p c f", p=128)
    for c in range(8):
        load_cast(woutb[:, c, :], wout_v[:, c, :], 128, 1024, "woutb")
    for kv in range(2):
        w1v = wd["w1"][kv].rearrange("(l d) f -> d l f", d=64)
        for l4 in range(0, 32, 4):
            i = nst[0] % 2
            P.dma("sync" if i == 0 else "gpsimd", pfx + f"stg{i}", stg[i][kv * 64:(kv + 1) * 64, :].rearrange("p (l f) -> p l f", l=4),
                  w1v[:, l4:l4 + 4, :], writes=[("stg", i)])
            nst[0] += 1
            P.op("vector" if i == 0 else "gpsimd",
                 ("tensor_copy", C(out=w1b[kv * 64:(kv + 1) * 64, l4:l4 + 4, :].rearrange("p l f -> p (l f)"),
                                   in_=stg[i][kv * 64:(kv + 1) * 64, :])), [("stg", i)], ["w1b"])
    for kv in range(2):
        ps_ = slice(kv * 64, (kv + 1) * 64)
        for fc in range(2):
            for l in range(32):
                mm(pb[0][:, (kv * 2 + fc):(kv * 2 + fc) + 1], w1b[ps_, l, fc * 128:(fc + 1) * 128], peTb[ps_, l:l + 1], l == 0, l == 31,
                   ["w1b", "peTb"], [PB(0)])
    V(("tensor_tensor", C(out=cb[:].rearrange("p a b -> p (a b)"), in0=pb[0][:, 0:4], in1=b1[:].rearrange("p a b -> p (a b)"), op=ALU.add)),
      [PB(0), "b1"], ["cb"])
    if limit <= 0:
        return []

    ksT = sb([128, 2, T], BF16, "ksT")
    kwT = sb([128, 2, T], BF16, "kwT")
    vs_tok = sb([128, NT, 4, 65], BF16, "vs_tok")
    vw_tok = sb([128, NT, 4, 65], BF16, "vw_tok")
    kcT = sb([128, 2, 256], BF16, "kcT")
    vcT = sb([128, 2, 256], F32, "vcT")
    vc_tok = sb([128, 2, 4, 65], BF16, "vc_tok")
    G(("memset", C(vs_tok[:], 1.0)), w=["vtok"])
    G(("memset", C(vw_tok[:], 1.0)), w=["vtok"])
    G(("memset", C(vc_tok[:], 1.0)), w=["vc_tok"])
    G(("memset", C(kcT[:], 0.0)), w=["kcT"])
    G(("memset", C(vcT[:], 0.0)), w=["vcT"])
    xt = [sb([128, D], F32, f"xt{i}") for i in range(2)]
    ss = sb([128, 1], F32, "ss")
    rs = sb([128, 1], F32, "rs")
    ht = sb([128, D], F32, "ht")
    hT = sb([128, 8, 256], BF16, "hT")
    qT = sb([128, 2, 4, 256], BF16, "qT")
    szT = sb([128, 8, 256], BF16, "szT")
    craw = sb([128, 4, 272], BF16, "craw")
    gsb = sb([128, 2, 48], F32, "gsb")
    sq = sb([128, 256], F32, "sq")
    rstd = sb([128, 256], F32, "rstd")
    xs = sb([128, 256], F32, "xs")
    g1 = sb([128, 256], F32, "g1")
    hid = sb([128, 4, 64], BF16, "hid")
    kcv = sb([128, 2, 2, 16], F32, "kcv")
    Pc = sb([128, 2, 4, 128], BF16, "Pc")
    Ps = [sb([128, 4, 128], BF16, f"Ps{i}") for i in range(2)]
    cmpb = sb([128, 2, 4, 128], BF16, "cmpb")
    Aadd = sb([128, 64], F32, "Aadd")
    A0 = sb([128, 64], F32, "A0")
    imp = sb([128, 64], F32, "imp")
    imp2 = sb([128, 64], F32, "imp2")
    mx = sb([128, 16], F32, "mx")
    selm = sb([128, 64], F32, "selm")
    selmT = sb([128, 4, 128], BF16, "selmT")
    rc = sb([128, 4], F32, "rc")
    cf = sb([128, 4], F32, "cf")
    oacc = sb([128, 16, 64], F32, "oacc")
    yT = sb([128, 8, 128], BF16, "yT")
    outt = sb([128, D], F32, "outt")
    G(("memset", C(craw[:], 0.0)), w=["craw"])
    G(("memset", C(selmT[:], 0.0)), w=["selmT"])
    G(("memset", C(A0[:], 0.0)), w=["A0"])
    G(("memset", C(A0[:, 0:1], 10000.0)), w=["A0"])
    D0 = sb([128, 64], F32, "D0")
    Dd = sb([128, 64], F32, "Dd")
    At = sb([128, 64], F32, "At")
    G(("iota", C(D0[:], pattern=[[-1, 64]], base=0, channel_multiplier=0, allow_small_or_imprecise_dtypes=True)), w=["D0"])
    G(("tensor_scalar", C(out=D0[64:128, :], in0=D0[64:128, :], scalar1=1.0, scalar2=None, op0=ALU.add)), ["D0"], ["D0"])
    eps = 1e-6
    wcnt = [0]

    def wload(ch):
        s_ = wcnt[0] % 2
        wcnt[0] += 1
        P.dma("sync", pfx + f"ws{s_}", wslot[s_][:], wscr[ch], reads=WS_ALL, writes=[("wslot", s_)])
        return s_

    def rmsnorm_evac(ps_ap, gcol, dst_ap, pbi, wres, ncols=256):
        A(("activation", C(out=sq[:, 0:ncols], in_=ps_ap, func=AF.Square)), [PB(pbi)], ["sq"])
        mm(pb[7][:, 0:ncols], BD[:], sq[:, 0:ncols], True, True, ["BD", "sq"], [PB(7)])
        A(("activation", C(out=rstd[:, 0:ncols], in_=pb[7][:, 0:ncols], func=AF.Sqrt, scale=1.0 / 64, bias=eps)), [PB(7)], ["rstd"])
        V(("reciprocal", C(out=rstd[:, 0:ncols], in_=rstd[:, 0:ncols])), ["rstd"], ["rstd"])
        V(("scalar_tensor_tensor", C(out=dst_ap, in0=ps_ap, scalar=gains[:, gcol:gcol + 1], in1=rstd[:, 0:ncols], op0=ALU.mult, op1=ALU.mult)),
          [PB(pbi), "gains", "rstd"], [wres])

    outs = []
    for b in range(NB):
        t0 = b * 256
        for i in range(2):
            xb = xt[i]
            P.dma("sync", pfx + f"x{i}", xb[:], x[t0 + i * 128:t0 + (i + 1) * 128, :], writes=[("xt", i)])
            A(("activation", C(out=ht[:], in_=xb[:], func=AF.Square, accum_out=ss[:])), [("xt", i)], ["ht", "ss"])
            A(("activation", C(out=rs[:], in_=ss[:], func=AF.Sqrt, scale=1.0 / D, bias=eps)), ["ss"], ["rs"])
            V(("reciprocal", C(out=rs[:], in_=rs[:])), ["rs"], ["rs"])
            V(("scalar_tensor_tensor", C(out=ht[:], in0=xb[:], scalar=rs[:, 0:1], in1=gb[:], op0=ALU.mult, op1=ALU.mult)),
              [("xt", i), "rs", "gb"], ["ht"])
            for half in range(2):
                for c in range(4):
                    cc = half * 4 + c
                    tr(pb[half][:, c * 128:(c + 1) * 128], ht[:, cc * 128:(cc + 1) * 128], ident[:], ["ht", "ident"], [PB(half)])
                pv = pb[half].rearrange("p (c t) -> p c t", c=4)
                dst = hT[:, half * 4:(half + 1) * 4, i * 128:(i + 1) * 128]
                if half == 0:
                    V(("tensor_copy", C(out=dst, in_=pv)), [PB(half)], ["hT"])
                else:
                    A(("copy", C(out=dst, in_=pv)), [PB(half)], ["hT"])
        if limit <= 1:
            return []
        def proj_fm(ch, pbi, M=128):
            s_ = wload(ch)
            for dc in range(8):
                mm(pb[pbi][0:M, 0:256], wslot[s_][:, dc, 0:M], hT[:, dc, :], dc == 0, dc == 7, [("wslot", s_), "hT"], [PB(pbi)])
        for m in range(2):
            for j in range(4):
                pbi = (m * 4 + j) % 2
                proj_fm(CQ + m * 4 + j, pbi)
                rmsnorm_evac(pb[pbi][:, 0:256], 0, qT[:, m, j, :], pbi, "qT")
        for m in range(2):
            proj_fm(CKS + m, m)
            rmsnorm_evac(pb[m][:, 0:256], 2, ksT[:, m, t0:t0 + 256], m, "kT")
        for m in range(2):
            proj_fm(CKW + m, m)
            rmsnorm_evac(pb[m][:, 0:256], 3, kwT[:, m, t0:t0 + 256], m, "kT")
        for c in range(8):
            proj_fm(CZ + c, c % 2)
            A(("activation", C(out=szT[:, c, :], in_=pb[c % 2][:, 0:256], func=AF.Silu)), [PB(c % 2)], ["szT"])
        for g in range(4):
            proj_fm(CCV + g, g % 2)
            A(("copy", C(out=craw[:, g, 16:272], in_=pb[g % 2][:, 0:256])), [PB(g % 2)], ["craw"])
        for (ch0, vtok) in ((CVS, vs_tok), (CVW, vw_tok)):
            for c2 in range(2):
                s_ = wload(ch0 + c2)
                for i in range(2):
                    for dc in range(8):
                        mm(pb[i][:, 0:128], hT[:, dc, i * 128:(i + 1) * 128], wslot[s_][:, dc, :], dc == 0, dc == 7,
                           [("wslot", s_), "hT"], [PB(i)])
                    V(("tensor_copy", C(out=vtok[:, 2 * b + i, 2 * c2:2 * c2 + 2, 0:64],
                                        in_=pb[i][:, 0:128].rearrange("p (g n) -> p g n", g=2))), [PB(i)], ["vtok"])
        s_ = wload(CG)
        for i in range(2):
            for dc in range(8):
                mm(pb[i][:, 0:48], hT[:, dc, i * 128:(i + 1) * 128], wslot[s_][:, dc, 0:48], dc == 0, dc == 7, [("wslot", s_), "hT"], [PB(i)])
            A(("activation", C(out=gsb[:, i, :], in_=pb[i][:, 0:48], func=AF.Sigmoid)), [PB(i)], ["gsb"])
        if limit <= 2:
            return []
        i0 = 1 if b == 0 else 0
        ni = 16 - i0
        n0 = 16 * b - 1 + i0
        ph = pb[2]
        for kv in range(2):
            ps_ = slice(kv * 64, (kv + 1) * 64)
            for fc in range(2):
                reg = (kv * 2 + fc) * 64
                for l in range(32):
                    rhs = craw[ps_, :, l + 16 * i0:l + 16 * i0 + 16 * (ni - 1) + 1:16]
                    outp = ph[:, reg:reg + 64].rearrange("p (g i) -> p g i", g=4)[:, :, 0:ni]
                    mm(outp, w1b[ps_, l, fc * 128:(fc + 1) * 128], rhs, l == 0, l == 31, ["w1b", "craw"], [PB(2)])
        for kv in range(2):
            for fc in range(2):
                reg = (kv * 2 + fc) * 64
                A(("activation", C(out=xs[:, reg:reg + 64], in_=ph[:, reg:reg + 64], func=AF.Identity, bias=cb[:, kv, fc:fc + 1])),
                  [PB(2), "cb"], ["xs"])
        V(("tensor_tensor", C(out=g1[:], in0=xs[:], in1=xs[:], op=ALU.mult)), ["xs"], ["g1"])
        V(("tensor_scalar", C(out=g1[:], in0=g1[:], scalar1=0.044715, scalar2=1.0, op0=ALU.mult, op1=ALU.add)), ["g1"], ["g1"])
        V(("tensor_tensor", C(out=g1[:], in0=g1[:], in1=xs[:], op=ALU.mult)), ["g1", "xs"], ["g1"])
        A(("activation", C(out=g1[:], in_=g1[:], func=AF.Sigmoid, scale=GELU_C)), ["g1"], ["g1"])
        V(("tensor_tensor", C(out=hid[:].rearrange("p a b -> p (a b)"), in0=g1[:], in1=xs[:], op=ALU.mult)), ["g1", "xs"], ["hid"])
        pk = pb[3][:, 0:64].rearrange("p (kv m i) -> p kv m i", kv=2, m=2)
        for kv in range(2):
            for g in range(4):
                m_, e_ = g // 2, g % 2
                for fc in range(2):
                    mm(pk[e_ * 64:(e_ + 1) * 64, kv, m_, 0:ni], w2b[:, fc, kv, :], hid[:, kv * 2 + fc, g * 16:g * 16 + ni], fc == 0, fc == 1,
                       ["w2b", "hid"], [PB(3)])
        V(("tensor_copy", C(out=kcv[:, :, :, 0:ni], in_=pk[:, :, :, 0:ni])), [PB(3)], ["kcv"])
        for m_ in range(2):
            G(("tensor_copy", C(out=vcT[:, m_, n0:n0 + ni], in_=kcv[:, 1, m_, 0:ni])), ["kcv"], ["vcT"])
        kflat = kcv[:, 0, :, :].rearrange("p m i -> p (m i)")
        A(("activation", C(out=sq[:, 0:32], in_=kflat, func=AF.Square)), ["kcv"], ["sq"])
        mm(pb[7][:, 0:32], BD[:], sq[:, 0:32], True, True, ["BD", "sq"], [PB(7)])
        A(("activation", C(out=rstd[:, 0:32], in_=pb[7][:, 0:32], func=AF.Sqrt, scale=1.0 / 64, bias=eps)), [PB(7)], ["rstd"])
        V(("reciprocal", C(out=rstd[:, 0:32], in_=rstd[:, 0:32])), ["rstd"], ["rstd"])
        V(("scalar_tensor_tensor", C(out=sq[:, 0:32], in0=kflat, scalar=gains[:, 1:2], in1=rstd[:, 0:32], op0=ALU.mult, op1=ALU.mult)),
          ["kcv", "gains", "rstd"], ["sq"])
        for m_ in range(2):
            G(("tensor_copy", C(out=kcT[:, m_, n0:n0 + ni], in_=sq[:, m_ * 16:m_ * 16 + ni])), ["sq"], ["kcT"])
        for c in range(2):
            for m_ in range(2):
                tr(pb[4][:, (c * 2 + m_) * 128:(c * 2 + m_ + 1) * 128], vcT[:, m_, c * 128:(c + 1) * 128], ident[:], ["vcT", "ident"], [PB(4)])
        V(("tensor_copy", C(out=vc_tok[:, :, :, 0:64], in_=pb[4].rearrange("p (c g n) -> p c g n", c=2, g=4))), [PB(4)], ["vc_tok"])
        G(("tensor_copy", C(out=craw[:, :, 0:16], in_=craw[:, :, 256:272])), ["craw"], ["craw"])
        if limit <= 3:
            return []
        for il in range(2):
            i = 2 * b + il
            tl = slice(il * 128, (il + 1) * 128)
            V(("tensor_scalar", C(out=Dd[:], in0=D0[:], scalar1=float(2 * i), scalar2=None, op0=ALU.add)), ["D0"], ["Dd"])
            V(("tensor_scalar", C(out=Aadd[:], in0=Dd[:], scalar1=0.0, scalar2=None, op0=ALU.is_ge)), ["Dd"], ["Aadd"])
            V(("tensor_scalar", C(out=At[:], in0=Dd[:], scalar1=1.0, scalar2=10000.0, op0=ALU.is_le, op1=ALU.mult)), ["Dd"], ["At"])
            V(("tensor_tensor", C(out=Aadd[:], in0=Aadd[:], in1=At[:], op=ALU.mult)), ["Aadd", "At"], ["Aadd"])
            V(("tensor_tensor", C(out=Aadd[:], in0=Aadd[:], in1=A0[:], op=ALU.max)), ["Aadd", "A0"], ["Aadd"])
            V(("tensor_scalar", C(out=At[:], in0=Dd[:], scalar1=0.0, scalar2=-1e30, op0=ALU.is_lt, op1=ALU.mult)), ["Dd"], ["At"])
            V(("tensor_tensor", C(out=Aadd[:], in0=Aadd[:], in1=At[:], op=ALU.add)), ["Aadd", "At"], ["Aadd"])
            cts = []
            for c in range(2):
                base = 128 * i - 2048 * c - 31
                if base + 127 < 0:
                    continue
                need_bias = base - 16 * 127 < 0
                cts.append((c, need_bias))
                if need_bias:
                    G(("affine_select", C(out=cmpb[:, c, :, :], in_=zerob[:], pattern=[[0, 4], [1, 128]], compare_op=ALU.is_ge, fill=NEG,
                                          base=base, channel_multiplier=-16)), ["zerob"], ["cmpb"])
            for g in range(4):
                m_, e_ = g // 2, g % 2
                ps_ = slice(e_ * 64, (e_ + 1) * 64)
                rq = qT[ps_, m_, :, tl]
                pO4 = pball[:, 4:8, :]
                first = [True]

                def combine(br, pO_of_h, srcres):
                    V(("tensor_scalar", C(out=rc[:], in0=pO_of_h(None), scalar1=1e-30, scalar2=None, op0=ALU.max)), srcres, ["rc"])
                    V(("reciprocal", C(out=rc[:], in_=rc[:])), ["rc"], ["rc"])
                    V(("tensor_tensor", C(out=cf[:], in0=rc[:], in1=gsb[:, il, br * 16 + 4 * g:br * 16 + 4 * g + 4], op=ALU.mult)),
                      ["rc", "gsb"], ["cf"])
                    for h in range(4):
                        if first[0]:
                            V(("tensor_scalar", C(out=oacc[:, 4 * g + h, :], in0=pO_of_h(h), scalar1=cf[:, h:h + 1], scalar2=None, op0=ALU.mult)),
                              srcres + ["cf"], ["oacc"])
                        else:
                            V(("scalar_tensor_tensor", C(out=oacc[:, 4 * g + h, :], in0=pO_of_h(h), scalar=cf[:, h:h + 1], in1=oacc[:, 4 * g + h, :],
                                                         op0=ALU.mult, op1=ALU.add)), srcres + ["cf", "oacc"], ["oacc"])
                    first[0] = False

                for ci, (c, nb_) in enumerate(cts):
                    psS = pb[ci % 2]
                    mm(psS, kcT[ps_, m_, c * 128:(c + 1) * 128], rq, True, not nb_, ["kcT", "qT"], [PB(ci % 2)])
                    if nb_:
                        mm(psS, identb[:], cmpb[:, c, :, :], False, True, ["identb", "cmpb"], [PB(ci % 2)])
                    A(("activation", C(out=Pc[:, c, :, :], in_=psS.rearrange("p (h t) -> p h t", h=4), func=AF.Exp)), [PB(ci % 2)], ["Pc"])
                pOc = pb[4]
                for h in range(4):
                    for ci, (c, nb_) in enumerate(cts):
                        mm(pOc[:, h * 128:h * 128 + 65], Pc[:, c, h, :], vc_tok[:, c, g, :], ci == 0, ci == len(cts) - 1, ["Pc", "vc_tok"], [PB(4)])
                pI = pb[5]
                for h in range(4):
                    for ci, (c, nb_) in enumerate(cts):
                        mm(pI[:, h * 64:(h + 1) * 64], Pc[:, c, h, :], ov[:, c, :], ci == 0, ci == len(cts) - 1, ["Pc", "ov"], [PB(5)])
                pOc3 = pOc.rearrange("p (h n) -> p h n", h=4)
                combine(0, lambda h: pOc3[:, :, 64] if h is None else pOc[:, h * 128:h * 128 + 64], [PB(4)])
                for h in range(4):
                    if h == 0:
                        V(("tensor_scalar", C(out=imp[:], in0=pI[:, 0:64], scalar1=rc[:, 0:1], scalar2=None, op0=ALU.mult)), [PB(5), "rc"], ["imp"])
                    else:
                        V(("scalar_tensor_tensor", C(out=imp[:], in0=pI[:, h * 64:(h + 1) * 64], scalar=rc[:, h:h + 1], in1=imp[:], op0=ALU.mult,
                                                     op1=ALU.add)), [PB(5), "rc", "imp"], ["imp"])
                need_sel = i >= 8
                if need_sel:
                    V(("tensor_tensor", C(out=imp[:], in0=imp[:], in1=Aadd[:], op=ALU.add)), ["imp", "Aadd"], ["imp"])
                    V(("max", C(out=mx[:, 0:8], in_=imp[:])), ["imp"], ["mx"])
                    V(("match_replace", C(out=imp2[:], in_to_replace=mx[:, 0:8], in_values=imp[:], imm_value=-3e38)), ["imp", "mx"], ["imp2"])
                    V(("max", C(out=mx[:, 8:16], in_=imp2[:])), ["imp2"], ["mx"])
                    V(("tensor_scalar", C(out=selm[:], in0=imp[:], scalar1=mx[:, 15:16], scalar2=-NEG, op0=ALU.is_lt, op1=ALU.mult)),
                      ["imp", "mx"], ["selm"])
                    V(("tensor_scalar", C(out=selm[:], in0=selm[:], scalar1=-1.0, scalar2=None, op0=ALU.mult)), ["selm"], ["selm"])
                    tr(pb[6][0:64, 0:128], selm[:], ident[:], ["selm", "ident"], [PB(6)])
                    V(("tensor_copy", C(out=selmT[0:64, :, :], in_=pb[6][0:64, 0:128].unsqueeze(1).broadcast_to([64, 4, 128]))), [PB(6)], ["selmT"])
                for br, (kT, vtok, tiles) in ((1, (ksT, vs_tok, list(range(0, i + 1)))),
                                              (2, (kwT, vw_tok, list(range(max(0, i - 4), i + 1))))):
                    for ti, s_t in enumerate(tiles):
                        par = ti % 2
                        psS = pb[par]
                        extra = []
                        if br == 1 and need_sel:
                            extra.append((E[:, s_t, :], selmT[:], ["E", "selmT"]))
                        if s_t == i:
                            extra.append((identb[:], CB[:], ["identb", "CB"]))
                        if br == 2 and s_t == i - 4:
                            extra.append((identb[:], AB[:], ["identb", "AB"]))
                        mm(psS, kT[ps_, m_, s_t * 128:(s_t + 1) * 128], rq, True, len(extra) == 0, ["kT", "qT"], [PB(par)])
                        for xi, (l_, r_, rr) in enumerate(extra):
                            mm(psS, l_, r_, False, xi == len(extra) - 1, rr, [PB(par)])
                        A(("activation", C(out=Ps[par][:], in_=psS.rearrange("p (h t) -> p h t", h=4), func=AF.Exp)), [PB(par)], [("Ps", par)])
                        for h in range(4):
                            mm(pO4[:, h, 0:65], Ps[par][:, h, :], vtok[:, s_t, g, :], ti == 0, ti == len(tiles) - 1, [("Ps", par), "vtok"], [PB(4 + h)])
                    combine(br, lambda h: pO4[:, :, 64] if h is None else pO4[:, h, 0:64], [PB(4), PB(5), PB(6), PB(7)])
            if limit <= 4:
                continue
            ofl = oacc[:].rearrange("p a b -> p (a b)")
            for c in range(8):
                tr(pb[c // 4][:, (c % 4) * 128:(c % 4 + 1) * 128], ofl[:, c * 128:(c + 1) * 128], ident[:], ["oacc", "ident"], [PB(c // 4)])
            for hf in range(2):
                V(("tensor_tensor", C(out=yT[:, hf * 4:(hf + 1) * 4, :], in0=pb[hf].rearrange("p (c t) -> p c t", c=4),
                                      in1=szT[:, hf * 4:(hf + 1) * 4, tl], op=ALU.mult)), [PB(hf), "szT"], ["yT"])
            P.dma("gpsimd", pfx + "xr", xt[0][:], x[t0 + il * 128:t0 + (il + 1) * 128, :], writes=[("xt", 0)])
            for hf in range(2):
                for dc in range(8):
                    mm(pb[2 + hf], yT[:, dc, :], woutb[:, dc, hf * 512:(hf + 1) * 512], dc == 0, dc == 7, ["yT", "woutb"], [PB(2 + hf)])
                V(("tensor_tensor", C(out=outt[:, hf * 512:(hf + 1) * 512], in0=pb[2 + hf], in1=xt[0][:, hf * 512:(hf + 1) * 512], op=ALU.add)),
                  [PB(2 + hf), ("xt", 0)], ["outt"])
            P.dma("sync", pfx + "xo", xo[t0 + il * 128:t0 + (il + 1) * 128, :], outt[:], reads=["outt"], writes=[(pfx + "xo", i)])
            outs.append((pfx + "xo", i))
    return outs


T_SEQ = 4096
FUSED = False
_CACHE = {}


def _dt(nc, n, s):
    return nc.dram_tensor(n, s, F32, kind="ExternalInput").ap()


def _rwkv_wd(nc):
    return dict(g=_dt(nc, "r_g", [1, 1024]), vecs=_dt(nc, "r_vecs", [128, NV, 8]), w_in=_dt(nc, "r_w_in", [1024, 4096]),
                w1=_dt(nc, "r_w1", [1024, 64]), a1=_dt(nc, "r_a1", [1024, 64]), w2=_dt(nc, "r_w2", [64, 1024]),
                a2=_dt(nc, "r_a2", [64, 1024]), w_out=_dt(nc, "r_w_out", [1024, 1024]))


def _nsa_wd(nc):
    return dict(g=_dt(nc, "n_g", [1, 1024]), w_in=_dt(nc, "n_w_in", [1024, 3632]), w_out=_dt(nc, "n_w_out", [1024, 1024]),
                w1=_dt(nc, "n_w1", [2, 2048, 256]), gains=_dt(nc, "n_gains", [128, 4]), peT=_dt(nc, "n_peT", [128, 32]),
                b1=_dt(nc, "n_b1", [128, 2, 2]), w2=_dt(nc, "n_w2", [128, 256]))


def _build(which):
    T = T_SEQ
    nc = bass.Bass("TRN2", target_bir_lowering=False)
    x = _dt(nc, "x", [T, 1024])
    xo = nc.dram_tensor("xo", [T, 1024], F32, kind="ExternalOutput").ap()
    if which == "fused":
        x1 = nc.dram_tensor("x1_scr", [T, 1024], F32, kind="Internal").ap()
        rwd = _rwkv_wd(nc)
        nwd = _nsa_wd(nc)
        with ExitStack() as st:
            P = Prog(nc, st)
            outs = emit_rwkv(nc, P, st, x, x1, rwd, T)
            P.final_wait("sync", outs)
            P.emit()
        with ExitStack() as st:
            P = Prog(nc, st)
            outs = emit_nsa(nc, P, st, x1, xo, nwd, T)
            P.final_wait("sync", outs)
            P.emit()
    else:
        wd = _rwkv_wd(nc) if which == "rwkv" else _nsa_wd(nc)
        with ExitStack() as st:
            P = Prog(nc, st)
            outs = (emit_rwkv if which == "rwkv" else emit_nsa)(nc, P, st, x, xo, wd, T)
            P.final_wait("sync", outs)
            P.emit()
    return nc


def _get(which):
    if which not in _CACHE:
        _CACHE[which] = _build(which)
    return _CACHE[which]


def kernel(**inputs):
    inp = {k: np.asarray(v) for k, v in inputs.items()}
    x = np.ascontiguousarray(inp["x"], dtype=np.float32)
    B = x.shape[0]
    f32 = lambda a: np.ascontiguousarray(a, dtype=np.float32)
    rmap = {"r_g": f32(inp["norm_g"][0:1]), "r_vecs": rwkv_host_vecs(inp), "r_w_in": f32(inp["rwkv_w_in"][0]),
            "r_w1": f32(inp["rwkv_w1"][0]), "r_a1": f32(inp["rwkv_a1"][0]), "r_w2": f32(inp["rwkv_w2"][0]),
            "r_a2": f32(inp["rwkv_a2"][0]), "r_w_out": f32(inp["rwkv_w_out"][0])}
    hp = nsa_host(inp)
    nmap = {"n_g": f32(inp["norm_g"][1:2]), "n_w_in": f32(inp["nsa_w_in"][0]), "n_w_out": f32(inp["nsa_w_out"][0]),
            "n_w1": f32(inp["nsa_cmp_w1"][0]), "n_gains": hp["gains"], "n_peT": hp["peT"], "n_b1": hp["b1"], "n_w2": hp["w2"]}
    cores = list(range(B))
    if FUSED:
        nc = _get("fused")
        res = run_bass_kernel_spmd(nc, [{"x": x[i], **rmap, **nmap} for i in cores], core_ids=cores)
        return np.stack([np.asarray(res.results[i]["xo"], dtype=np.float32) for i in cores], 0)
    nc = _get("rwkv")
    res = run_bass_kernel_spmd(nc, [{"x": x[i], **rmap} for i in cores], core_ids=cores)
    x1 = [np.ascontiguousarray(res.results[i]["xo"], dtype=np.float32) for i in cores]
    nc = _get("nsa")
    res = run_bass_kernel_spmd(nc, [{"x": x1[i], **nmap} for i in cores], core_ids=cores)
    return np.stack([np.asarray(res.results[i]["xo"], dtype=np.float32) for i in cores], 0)
```

```python
from contextlib import ExitStack
from concourse.bass_utils import run_bass_kernel_spmd
import numpy as np
import concourse.bass as bass
import concourse.mybir as mybir

F32 = mybir.dt.float32
BF16 = mybir.dt.bfloat16
ALU = mybir.AluOpType
AF = mybir.ActivationFunctionType
AX = mybir.AxisListType

ENGINES = ("tensor", "vector", "scalar", "gpsimd", "sync")
CH = 30000


class Prog:
    def __init__(self, nc, stack, same_engine_sync=True):
        self.nc = nc
        self.stack = stack
        self.ops = {e: [] for e in ENGINES}
        self.cnt = {e: 0 for e in ENGINES}
        self.sems = {}
        self.res_w = {}
        self.res_r = {}
        self.dma_cnt = {}
        self.seen = {e: {} for e in ENGINES}
        self.same_engine_sync = same_engine_sync
        self.nwaits = 0
        self.max_ops = 10**9
        self.nops = 0
        self.last_line = None

    def sem(self, key):
        if key not in self.sems:
            name = "s_" + "_".join(str(k) for k in (key if isinstance(key, tuple) else (key,)))
            self.sems[key] = self.stack.enter_context(self.nc.semaphore(name))
        return self.sems[key]

    def _deps(self, eng, reads, writes, pe_accum=False):
        waits = {}

        def need(dep):
            if dep is None:
                return
            semkey, val, deng = dep
            if deng == eng and semkey[0] == "c":
                if not self.same_engine_sync:
                    return
                if eng == "tensor" and pe_accum:
                    return
            if self.seen[eng].get(semkey, 0) >= val:
                return
            if waits.get(semkey, 0) < val:
                waits[semkey] = val

        for r in reads:
            need(self.res_w.get(r))
        for w in writes:
            need(self.res_w.get(w))
            for rd in self.res_r.get(w, ()):
                need(rd)
        for k, v in waits.items():
            self.seen[eng][k] = v
        self.nwaits += len(waits)
        return list(waits.items())

    def _record(self, dep, reads, writes):
        for r in reads:
            self.res_r.setdefault(r, []).append(dep)
        for w in writes:
            self.res_w[w] = dep
            self.res_r[w] = []

    def op(self, eng, fn, reads=(), writes=(), pe_accum=False):
        isps = lambda r: isinstance(r, tuple) and isinstance(r[0], str) and r[0].endswith("pb")
        writes = list(writes) + [r for r in reads if isps(r)]
        reads = [r for r in reads if not isps(r)]
        waits = self._deps(eng, reads, writes, pe_accum)
        i = self.cnt[eng]
        self.cnt[eng] += 1
        semkey = ("c", eng, i // CH)
        self.sem(semkey)
        for k, _ in waits:
            self.sem(k)
        dep = (semkey, i % CH + 1, eng)
        self.ops[eng].append((waits, fn, semkey, 1))
        self._record(dep, reads, writes)

    def dma(self, eng, semname, out, in_, reads=(), writes=(), **kw):
        waits = self._deps(eng, reads, writes)
        semkey = ("d", semname)
        self.sem(semkey)
        for k, _ in waits:
            self.sem(k)
        n = self.dma_cnt.get(semname, 0) + 1
        self.dma_cnt[semname] = n
        dep = (semkey, 16 * n, eng)
        self.ops[eng].append((waits, lambda e: e.dma_start(out=out, in_=in_, **kw), semkey, 16))
        self._record(dep, reads, writes)

    def final_wait(self, eng, resources):
        waits = self._deps(eng, resources, ())
        for k, _ in waits:
            self.sem(k)
        self.ops[eng].append((waits, None, None, 0))

    def emit(self):
        nc = self.nc
        with nc.Block() as block:
            def mk(engname):
                def body(e):
                    for waits, fn, semkey, inc in self.ops[engname]:
                        for k, v in waits:
                            e.wait_ge(self.sems[k], v)
                        if fn is not None:
                            try:
                                ins = getattr(e, fn[0])(*fn[1][0], **fn[1][1]) if isinstance(fn, tuple) else fn(e)
                            except Exception:
                                print("EMIT FAIL", engname, fn[0] if isinstance(fn, tuple) else fn, {k: (v if not hasattr(v, "shape") else ("AP", v.shape)) for k, v in fn[1][1].items()} if isinstance(fn, tuple) else "")
                                raise
                            ins.then_inc(self.sems[semkey], inc)
                return body
            block.tensor(mk("tensor"))
            block.vector(mk("vector"))
            block.scalar(mk("scalar"))
            block.gpsimd(mk("gpsimd"))
            block.sync(mk("sync"))


def C(*a, **k):
    return (a, k)


D = 1024
NV = 13
I_W0, I_A0, I_KK, I_KA, I_RK, I_LG, I_LB = 6, 7, 8, 9, 10, 11, 12
DEC = -float(np.exp(-0.5))


def rwkv_host_vecs(inp):
    rows = [inp["rwkv_mu"][0][i] for i in range(6)] + [inp[k][0].reshape(-1) for k in
            ["rwkv_w0", "rwkv_a0", "rwkv_k_k", "rwkv_k_a"]]
    rows.append(np.tile(inp["rwkv_r_k"][0].reshape(16, 64), 1).reshape(-1))
    rows += [inp["rwkv_lnx_g"][0], inp["rwkv_lnx_b"][0]]
    v = np.stack([np.asarray(r, np.float32).reshape(8, 128) for r in rows], 0)
    return np.ascontiguousarray(v.transpose(2, 0, 1))


def emit_rwkv(nc, P, st, x, xo, wd, T, pfx="r", limit=99):
    NB = T // 256
    sbn = [0]

    def sb(shape, dt, name=None):
        sbn[0] += 1
        return st.enter_context(nc.sbuf_tensor(f"{pfx}_{name or 't'}{sbn[0]}", shape, dt))

    pball = st.enter_context(nc.psum_tensor(f"{pfx}_psum", [128, 8, 512], F32))
    pb = [pball[:, i, :] for i in range(8)]
    PB = lambda i: (pfx + "pb", i)

    RN = {"t1": "sq", "rk": "sq", "Lp": "rn", "eLp": "kkr", "BtT": "kf", "KtT": "a"}
    cn = lambda l: [RN.get(x, x) if isinstance(x, str) else x for x in l]

    def V(fn, r=(), w=()): P.op("vector", fn, cn(r), cn(w))
    def G(fn, r=(), w=()): P.op("gpsimd", fn, cn(r), cn(w))
    def A(fn, r=(), w=()): P.op("scalar", fn, cn(r), cn(w))

    def mm(out, lhsT, rhs, start=True, stop=True, r=(), w=()):
        P.op("tensor", ("matmul", C(out=out, lhsT=lhsT, rhs=rhs, start=start, stop=stop)), cn(r), cn(w), pe_accum=not start)

    def tr(out, in_, ident, r=(), w=()):
        P.op("tensor", ("transpose", C(out=out, in_=in_, identity=ident)), cn(r), cn(w))

    ones = sb([128, 512], F32, "ones")
    ident = sb([128, 128], F32, "ident")
    ident4 = sb([128, 4, 128], F32, "ident4")
    triS4 = sb([128, 4, 128], F32, "triS4")
    triI4 = sb([128, 4, 128], F32, "triI4")
    triL4 = sb([128, 4, 128], F32, "triL4")
    BD = sb([128, 128], F32, "BD")
    m01 = sb([128, 256], F32, "m01")
    G(("memset", C(ones[:], 1.0)), w=["ones"])
    G(("affine_select", C(out=ident[:], in_=ones[:, 0:128], pattern=[[-1, 128]], compare_op=ALU.is_equal,
                                fill=0.0, base=0, channel_multiplier=1)), r=["ones"], w=["ident"])
    o4 = ones[:].rearrange("p (a b) -> p a b", a=4)
    G(("affine_select", C(out=ident4[:], in_=o4, pattern=[[0, 4], [-1, 128]], compare_op=ALU.is_equal,
                                fill=0.0, base=0, channel_multiplier=1)), r=["ones"], w=["ident4"])
    G(("affine_select", C(out=triS4[:], in_=o4, pattern=[[0, 4], [1, 128]], compare_op=ALU.is_gt,
                                fill=0.0, base=0, channel_multiplier=-1)), r=["ones"], w=["triS4"])
    G(("affine_select", C(out=triI4[:], in_=o4, pattern=[[0, 4], [1, 128]], compare_op=ALU.is_ge,
                                fill=0.0, base=0, channel_multiplier=-1)), r=["ones"], w=["triI4"])
    G(("affine_select", C(out=triL4[:], in_=o4, pattern=[[0, 4], [-1, 128]], compare_op=ALU.is_gt,
                                fill=0.0, base=0, channel_multiplier=1)), r=["ones"], w=["triL4"])
    G(("memset", C(BD[:], 0.0)), w=["BD"])
    G(("memset", C(BD[0:64, 0:64], 1.0)), w=["BD"])
    G(("memset", C(BD[64:128, 64:128], 1.0)), w=["BD"])
    G(("memset", C(m01[:], 1.0)), w=["m01"])
    G(("memset", C(m01[:, 0:1], 0.0)), w=["m01"])
    G(("memset", C(m01[:, 128:129], 0.0)), w=["m01"])

    vecs = sb([128, NV, 8], F32, "vecs")
    gb = sb([128, D], F32, "gb")
    wslot = [sb([128, 8, 4, 128], BF16, f"wslot{i}") for i in range(2)]
    wscr = nc.dram_tensor(pfx + "_wscr", [8, 128, 8, 4, 128], BF16, kind="Internal").ap()
    wscr_w = wscr.rearrange("h p d c f -> p h d c f")
    woutb = sb([128, 8, 1024], BF16, "woutb")
    w1b = sb([128, 8, 64], BF16, "w1b")
    a1b = sb([128, 8, 64], BF16, "a1b")
    w2b = sb([64, 1024], BF16, "w2b")
    a2b = sb([64, 1024], BF16, "a2b")
    stg = [sb([128, 1024], F32, f"stg{i}") for i in range(2)]
    stgb = [sb([128, 1024], BF16, f"stgb{i}") for i in range(2)]
    P.dma("sync", pfx + "vecs", vecs[:], wd["vecs"], writes=["vecs"])
    P.dma("sync", pfx + "gb", gb[:], wd["g"].partition_broadcast(128), writes=["gb"])
    nst = [0]
    WS_ALL = [("wscr", c, ci) for c in range(8) for ci in range(4)]

    def load_cast(dst_ap, src_ap, np_, ncols, wres):
        i = nst[0] % 2
        nst[0] += 1
        q = "sync" if i == 0 else "gpsimd"
        P.dma(q, pfx + f"stg{i}", stg[i][0:np_, 0:ncols], src_ap, writes=[("stg", i)])
        eng = "vector" if i == 0 else "gpsimd"
        P.op(eng, ("tensor_copy", C(out=dst_ap, in_=stg[i][0:np_, 0:ncols])), [("stg", i)], [wres])

    win_v = wd["w_in"].rearrange("(c p) f -> p c f", p=128)
    for c in range(8):
        for ci in range(4):
            i = nst[0] % 2
            load_cast(stgb[i][:], win_v[:, c, ci * 1024:(ci + 1) * 1024], 128, 1024, ("stgb", i))
            P.dma("sync" if i == 0 else "gpsimd", pfx + f"wscr{i}", wscr_w[:, :, c, ci, :],
                  stgb[i][:].rearrange("p (h f) -> p h f", h=8), reads=[("stgb", i)], writes=[("wscr", c, ci)])
    wout_v = wd["w_out"].rearrange("(c p) f -> p c f", p=128)
    for c in range(8):
        load_cast(woutb[:, c, :], wout_v[:, c, :], 128, 1024, "woutb")
    load_cast(w1b[:], wd["w1"].rearrange("(c p) f -> p c f", p=128), 128, 512, "w1b")
    load_cast(a1b[:], wd["a1"].rearrange("(c p) f -> p c f", p=128), 128, 512, "a1b")
    load_cast(w2b[:], wd["w2"], 64, 1024, "w2b")
    load_cast(a2b[:], wd["a2"], 64, 1024, "a2b")

    if limit <= 0:
        return []
    xt = [sb([128, D], F32, f"xt{i}") for i in range(2)]
    ss = sb([128, 1], F32, "ss")
    rs = sb([128, 1], F32, "rs")
    ht = sb([128, D], F32, "ht")
    hT = sb([128, 8, 257], F32, "hT")
    dh = [sb([128, 256], F32, f"dh{i}") for i in range(2)]
    xm = sb([128, 6, 8, 256], BF16, "xm")
    la = sb([64, 256], BF16, "la")
    lw = sb([64, 256], BF16, "lw")
    rh = sb([128, 8, 256], BF16, "rh")
    ah = sb([128, 8, 256], BF16, "ah")
    bh = sb([128, 8, 256], BF16, "bh")
    kh = sb([128, 8, 256], BF16, "kh")
    Vt = sb([128, 2, 1024], BF16, "Vt")
    Bt = sb([128, 2, 1024], BF16, "Bt")
    Kt = sb([128, 2, 1024], BF16, "Kt")
    sz = sb([128, 8, 256], BF16, "sz")
    bonus = sb([128, 8, 256], BF16, "bonus")
    gC = sb([128, 8, 2], F32, "gC")
    tmp = {n: sb([128, 256], F32, n) for n in
           ["kf", "kkr", "sq", "rn", "kk", "a", "kp", "bb", "sig", "L", "eL", "enL", "E2", "vf"]}
    for k_, v_ in RN.items():
        tmp[k_] = tmp[v_]
    ST = sb([128, 8, 64], F32, "ST")
    STb = sb([128, 8, 64], BF16, "STb")
    Q = [sb([128, 4, 128], BF16, f"Q{i}") for i in range(2)]
    QT = [sb([128, 4, 128], BF16, f"QT{i}") for i in range(2)]
    Z = sb([128, 4, 128], F32, "Z")
    Zb = sb([128, 4, 128], BF16, "Zb")
    QTf = sb([128, 4, 128], F32, "QTf")
    WT = sb([128, 16, 128], BF16, "WT")
    Mak = sb([128, 16, 128], BF16, "Mak")
    Mrb = sb([128, 16, 128], BF16, "Mrb")
    Mrk = sb([128, 16, 128], BF16, "Mrk")
    Xn = sb([128, 1024], BF16, "Xn")
    Ub = sb([128, 1024], BF16, "Ub")
    of = sb([128, 16, 64], F32, "of")
    mean = sb([128, 16], F32, "mean")
    ex2 = sb([128, 16], F32, "ex2")
    var = sb([128, 16], F32, "var")
    yT = sb([128, 8, 128], BF16, "yT")
    ytmp = sb([128, 128], F32, "ytmp")
    xr = xt[0]
    outt = sb([128, D], F32, "outt")
    osq = outt[:].rearrange("p (a b) -> p a b", a=16)

    G(("memset", C(ST[:], 0.0)), w=["ST"])
    G(("memset", C(STb[:], 0.0)), w=["STb"])
    G(("memset", C(hT[:, :, 0:1], 0.0)), w=["hT"])

    vcol = lambda i, hp: vecs[:, i, hp:hp + 1]
    eps = 1e-6

    for b in range(NB):
        t0 = b * 256
        for i in range(2):
            xb = xt[i]
            P.dma("sync", pfx + f"x{i}", xb[:], x[t0 + i * 128:t0 + (i + 1) * 128, :], writes=[("xt", i)])
            A(("activation", C(out=ht[:], in_=xb[:], func=AF.Square, accum_out=ss[:])), [("xt", i)], ["ht", "ss"])
            A(("activation", C(out=rs[:], in_=ss[:], func=AF.Sqrt, scale=1.0 / D, bias=eps)), ["ss"], ["rs"])
            V(("reciprocal", C(out=rs[:], in_=rs[:])), ["rs"], ["rs"])
            V(("scalar_tensor_tensor", C(out=ht[:], in0=xb[:], scalar=rs[:, 0:1], in1=gb[:], op0=ALU.mult, op1=ALU.mult)),
              [("xt", i), "rs", "gb"], ["ht"])
            for half in range(2):
                for c in range(4):
                    cc = half * 4 + c
                    tr(pb[half][:, c * 128:(c + 1) * 128], ht[:, cc * 128:(cc + 1) * 128], ident[:], ["ht", "ident"], [PB(half)])
                pv = pb[half].rearrange("p (c t) -> p c t", c=4)
                dst = hT[:, half * 4:(half + 1) * 4, 1 + i * 128:1 + (i + 1) * 128]
                if half == 0:
                    V(("tensor_copy", C(out=dst, in_=pv)), [PB(half)], ["hT"])
                else:
                    A(("copy", C(out=dst, in_=pv)), [PB(half)], ["hT"])
        if limit <= 1:
            return []
        n = 0
        for dc in range(8):
            dd = dh[dc % 2]
            V(("tensor_tensor", C(out=dd[:], in0=hT[:, dc, 0:256], in1=hT[:, dc, 1:257], op=ALU.subtract)),
              ["hT"], [("dh", dc % 2)])
            for c in range(6):
                fn = ("scalar_tensor_tensor", C(out=xm[:, c, dc, :], in0=dd[:], scalar=vcol(c, dc),
                                                                        in1=hT[:, dc, 1:257], op0=ALU.mult, op1=ALU.add))
                V(fn, [("dh", dc % 2), "hT", "vecs"], [("xm", c)])
                n += 1
        V(("tensor_copy", C(out=hT[:, :, 0:1], in_=hT[:, :, 256:257])), ["hT"], ["hT"])
        if limit <= 2:
            return []
        for dc in range(8):
            mm(pb[2][0:64, 0:256], a1b[:, dc, :], xm[:, 5, dc, :], dc == 0, dc == 7, [("xm", 5), "a1b"], [PB(2)])
        V(("tensor_copy", C(out=la[:], in_=pb[2][0:64, 0:256])), [PB(2)], ["la"])
        for dc in range(8):
            mm(pb[3][0:64, 0:256], w1b[:, dc, :], xm[:, 4, dc, :], dc == 0, dc == 7, [("xm", 4), "w1b"], [PB(3)])
        A(("activation", C(out=lw[:], in_=pb[3][0:64, 0:256], func=AF.Tanh)), [PB(3)], ["lw"])
        if limit <= 3:
            return []
        for hp in range(8):
            fs = slice(hp * 128, (hp + 1) * 128)
            n_it = b * 8 + hp
            if n_it == 0:
                P.dma("sync", pfx + "ws0", wslot[0][:], wscr[0], reads=WS_ALL, writes=[("wslot", 0)])
            if n_it + 1 < NB * 8:
                sl_ = (n_it + 1) % 2
                P.dma("sync", pfx + f"ws{sl_}", wslot[sl_][:], wscr[(hp + 1) % 8], reads=WS_ALL, writes=[("wslot", sl_)])
            wsl = wslot[n_it % 2]
            pR, pK, pV_, pZ = pb[0][:, 0:256], pb[0][:, 256:512], pb[1][:, 0:256], pb[1][:, 256:512]
            pA, pU = pb[2][:, 0:256], pb[2][:, 256:512]
            pN, pBS = pb[3][:, 0:256], pb[3][:, 256:512]
            for ci, (po, pbi) in enumerate([(pR, 0), (pK, 0), (pV_, 1), (pZ, 1)]):
                for dc in range(8):
                    mm(po, wsl[:, dc, ci, :], xm[:, ci, dc, :], dc == 0, dc == 7,
                       [("xm", ci), ("wslot", n_it % 2)], [PB(pbi)])
            mm(pA, a2b[0:64, fs], la[:], True, True, ["a2b", "la"], [PB(2)])
            mm(pU, w2b[0:64, fs], lw[:], True, True, ["w2b", "lw"], [PB(2)])
            t = tmp
            A(("copy", C(out=t["kf"][:], in_=pK)), [PB(0)], ["kf"])
            V(("tensor_scalar", C(out=t["kkr"][:], in0=pK, scalar1=vcol(I_KK, hp), scalar2=None, op0=ALU.mult)), [PB(0), "vecs"], ["kkr"])
            A(("activation", C(out=t["sq"][:], in_=t["kkr"][:], func=AF.Square)), ["kkr"], ["sq"])
            mm(pN, BD[:], t["sq"][:], True, True, ["BD", "sq"], [PB(3)])
            A(("activation", C(out=t["rn"][:], in_=pN, func=AF.Sqrt)), [PB(3)], ["rn"])
            V(("tensor_scalar", C(out=t["rn"][:], in0=t["rn"][:], scalar1=1e-12, scalar2=None, op0=ALU.max)), ["rn"], ["rn"])
            V(("reciprocal", C(out=t["rn"][:], in_=t["rn"][:])), ["rn"], ["rn"])
            V(("tensor_tensor", C(out=t["kk"][:], in0=t["kkr"][:], in1=t["rn"][:], op=ALU.mult)), ["kkr", "rn"], ["kk"])
            A(("activation", C(out=t["a"][:], in_=pA, func=AF.Sigmoid, bias=vcol(I_A0, hp))), [PB(2), "vecs"], ["a"])
            V(("tensor_scalar", C(out=t["t1"][:], in0=t["a"][:], scalar1=-1.0, scalar2=vcol(I_KA, hp), op0=ALU.add, op1=ALU.mult)),
              ["a", "vecs"], ["t1"])
            V(("scalar_tensor_tensor", C(out=t["kp"][:], in0=t["t1"][:], scalar=1.0, in1=t["kf"][:], op0=ALU.add, op1=ALU.mult)),
              ["t1", "kf"], ["kp"])
            G(("tensor_tensor", C(out=t["bb"][:], in0=t["kk"][:], in1=t["a"][:], op=ALU.mult)), ["kk", "a"], ["bb"])
            A(("activation", C(out=t["sig"][:], in_=pU, func=AF.Sigmoid, bias=vcol(I_W0, hp))), [PB(2), "vecs"], ["sig"])
            G(("tensor_scalar", C(out=t["sig"][:], in0=t["sig"][:], scalar1=DEC, scalar2=None, op0=ALU.mult)), ["sig"], ["sig"])
            V(("tensor_tensor_scan", C(out=t["L"][:], data0=m01[:], data1=t["sig"][:], initial=0.0, op0=ALU.mult, op1=ALU.add)),
              ["m01", "sig"], ["L"])
            G(("tensor_tensor", C(out=t["Lp"][:], in0=t["L"][:], in1=t["sig"][:], op=ALU.subtract)), ["L", "sig"], ["Lp"])
            A(("activation", C(out=t["eL"][:], in_=t["L"][:], func=AF.Exp)), ["L"], ["eL"])
            A(("activation", C(out=t["eLp"][:], in_=t["Lp"][:], func=AF.Exp)), ["Lp"], ["eLp"])
            A(("activation", C(out=t["enL"][:], in_=t["L"][:], func=AF.Exp, scale=-1.0)), ["L"], ["enL"])
            for j in range(2):
                cs = slice(j * 128, (j + 1) * 128)
                A(("activation", C(out=t["E2"][:, cs], in_=t["L"][:, cs], func=AF.Exp, scale=-1.0,
                                                    bias=t["L"][:, j * 128 + 127:j * 128 + 128])), ["L"], ["E2"])
            V(("tensor_tensor", C(out=rh[:, hp, :], in0=pR, in1=t["eL"][:], op=ALU.mult)), [PB(0), "eL"], [("rh", hp)])
            V(("scalar_tensor_tensor", C(out=t["rk"][:], in0=pR, scalar=vcol(I_RK, hp), in1=t["kp"][:], op0=ALU.mult, op1=ALU.mult)),
              [PB(0), "vecs", "kp"], ["rk"])
            mm(pBS, BD[:], t["rk"][:], True, True, ["BD", "rk"], [PB(3)])
            A(("copy", C(out=t["vf"][:], in_=pV_)), [PB(1)], ["vf"])
            V(("tensor_tensor", C(out=bonus[:, hp, :], in0=pBS, in1=t["vf"][:], op=ALU.mult)), [PB(3), "vf"], [("bonus", hp)])
            A(("activation", C(out=sz[:, hp, :], in_=pZ, func=AF.Silu)), [PB(1)], [("sz", hp)])
            G(("tensor_tensor", C(out=ah[:, hp, :], in0=t["kk"][:], in1=t["eLp"][:], op=ALU.mult)), ["kk", "eLp"], [("ah", hp)])
            G(("tensor_tensor", C(out=bh[:, hp, :], in0=t["bb"][:], in1=t["enL"][:], op=ALU.mult)), ["bb", "enL"], [("bh", hp)])
            V(("tensor_tensor", C(out=kh[:, hp, :], in0=t["kp"][:], in1=t["enL"][:], op=ALU.mult)), ["kp", "enL"], [("kh", hp)])
            G(("tensor_tensor", C(out=t["BtT"][:], in0=t["bb"][:], in1=t["E2"][:], op=ALU.mult)), ["bb", "E2"], ["BtT"])
            V(("tensor_tensor", C(out=t["KtT"][:], in0=t["kp"][:], in1=t["E2"][:], op=ALU.mult)), ["kp", "E2"], ["KtT"])
            V(("tensor_copy", C(out=gC[:, hp, :], in_=t["eL"][:, 127:256:128])), ["eL"], ["gC"])
            for j in range(2):
                cs = slice(j * 128, (j + 1) * 128)
                for si, (src, sres) in enumerate([(t["BtT"], "BtT"), (t["KtT"], "KtT"), (t["vf"], "vf")]):
                    tr(pb[4 + j][:, si * 128:(si + 1) * 128], src[:, cs], ident[:], [sres, "ident"], [PB(4 + j)])
                V(("tensor_copy", C(out=Bt[:, j, fs], in_=pb[4 + j][:, 0:128])), [PB(4 + j)], [("Bt", j)])
                A(("copy", C(out=Kt[:, j, fs], in_=pb[4 + j][:, 128:256])), [PB(4 + j)], [("Kt", j)])
                V(("tensor_copy", C(out=Vt[:, j, fs], in_=pb[4 + j][:, 256:384])), [PB(4 + j)], [("Vt", j)])
        if limit <= 4:
            return []
        for j in range(2):
            ts = slice(j * 128, (j + 1) * 128)
            for hg in range(4):
                for q in range(4):
                    h = hg * 4 + q
                    hp, hh = h // 2, h % 2
                    ps_ = slice(hh * 64, hh * 64 + 64)
                    qs = slice(q * 128, (q + 1) * 128)
                    a_, b_, k_, r_ = ah[ps_, hp, ts], bh[ps_, hp, ts], kh[ps_, hp, ts], rh[ps_, hp, ts]
                    mm(pb[0][:, qs], b_, a_, True, True, [("ah", hp), ("bh", hp)], [PB(0)])
                    mm(pb[1][:, qs], a_, b_, True, True, [("ah", hp), ("bh", hp)], [PB(1)])
                    mm(pb[2][:, qs], k_, a_, True, True, [("ah", hp), ("kh", hp)], [PB(2)])
                    mm(pb[3][:, qs], b_, r_, True, True, [("rh", hp), ("bh", hp)], [PB(3)])
                    mm(pb[4][:, qs], k_, r_, True, True, [("rh", hp), ("kh", hp)], [PB(4)])
                p4 = lambda i: pb[i].rearrange("p (a b) -> p a b", a=4)
                hs = slice(hg * 4, hg * 4 + 4)
                V(("scalar_tensor_tensor", C(out=QTf[:], in0=p4(0), scalar=-1.0, in1=triS4[:], op0=ALU.mult, op1=ALU.mult)),
                  [PB(0), "triS4"], ["QTf"])
                V(("scalar_tensor_tensor", C(out=Q[0][:], in0=p4(1), scalar=-1.0, in1=triL4[:], op0=ALU.mult, op1=ALU.mult)),
                  [PB(1), "triL4"], [("Q", 0)])
                G(("tensor_copy", C(out=QT[0][:], in_=QTf[:])), ["QTf"], [("QT", 0)])
                G(("tensor_tensor", C(out=Z[:], in0=QTf[:], in1=ident4[:], op=ALU.add)), ["QTf", "ident4"], ["Z"])
                G(("tensor_copy", C(out=Zb[:], in_=Z[:])), ["Z"], ["Zb"])
                V(("tensor_tensor", C(out=Mak[:, hs, :], in0=p4(2), in1=triS4[:], op=ALU.mult)), [PB(2), "triS4"], ["Mak"])
                V(("tensor_tensor", C(out=Mrb[:, hs, :], in0=p4(3), in1=triI4[:], op=ALU.mult)), [PB(3), "triI4"], ["Mrb"])
                V(("tensor_tensor", C(out=Mrk[:, hs, :], in0=p4(4), in1=triI4[:], op=ALU.mult)), [PB(4), "triI4"], ["Mrk"])
                cur = 0
                for lvl in range(6):
                    nxt = 1 - cur
                    for q in range(4):
                        qs = slice(q * 128, (q + 1) * 128)
                        mm(pb[5][:, qs], QT[cur][:, q, :], Q[cur][:, q, :], True, True, [("Q", cur), ("QT", cur)], [PB(5)])
                        mm(pb[6][:, qs], Q[cur][:, q, :], QT[cur][:, q, :], True, True, [("Q", cur), ("QT", cur)], [PB(6)])
                    V(("tensor_copy", C(out=Q[nxt][:], in_=p4(5))), [PB(5)], [("Q", nxt)])
                    A(("copy", C(out=QT[nxt][:], in_=p4(6))), [PB(6)], [("QT", nxt)])
                    for q in range(4):
                        qs = slice(q * 128, (q + 1) * 128)
                        mm(pb[7][:, qs], Q[nxt][:, q, :], Zb[:, q, :], True, True, [("Q", nxt), "Zb"], [PB(7)])
                    V(("tensor_tensor", C(out=Z[:], in0=p4(7), in1=Z[:], op=ALU.add)), [PB(7), "Z"], ["Z"])
                    if lvl < 5:
                        G(("tensor_copy", C(out=Zb[:], in_=Z[:])), ["Z"], ["Zb"])
                    else:
                        G(("tensor_copy", C(out=WT[:, hs, :], in_=Z[:])), ["Z"], ["WT"])
                    cur = nxt
            if limit <= 5:
                return []
            pX = pball[:, 0:2, :].rearrange("p a b -> p (a b)")
            pUu = pball[:, 2:4, :].rearrange("p a b -> p (a b)")
            pO = pball[:, 4:6, :].rearrange("p a b -> p (a b)")
            pS = pb[6]
            hd = lambda h: (h // 2, slice((h % 2) * 64, (h % 2) * 64 + 64), slice(h * 64, (h + 1) * 64))
            for h in range(16):
                hp, ps_, vs = hd(h)
                mm(pX[:, vs], ah[ps_, hp, ts], STb[ps_, hp, :], True, False, [("ah", hp), "STb"], [PB(h // 8)])
                mm(pX[:, vs], Mak[:, h, :], Vt[:, j, vs], False, True, ["Mak", ("Vt", j)], [PB(h // 8)])
            V(("tensor_scalar", C(out=Xn[:, 0:512], in0=pX[:, 0:512], scalar1=-1.0, scalar2=None, op0=ALU.mult)), [PB(0)], ["Xn"])
            A(("mul", C(out=Xn[:, 512:1024], in_=pX[:, 512:1024], mul=-1.0)), [PB(1)], ["Xn"])
            for h in range(16):
                hp, ps_, vs = hd(h)
                mm(pUu[:, vs], WT[:, h, :], Xn[:, vs], True, True, ["WT", "Xn"], [PB(2 + h // 8)])
            V(("tensor_copy", C(out=Ub[:, 0:512], in_=pUu[:, 0:512])), [PB(2)], ["Ub"])
            A(("copy", C(out=Ub[:, 512:1024], in_=pUu[:, 512:1024])), [PB(3)], ["Ub"])
            for h in range(16):
                hp, ps_, vs = hd(h)
                mm(pO[:, vs], rh[ps_, hp, ts], STb[ps_, hp, :], True, False, [("rh", hp), "STb"], [PB(4 + h // 8)])
                mm(pO[:, vs], Mrb[:, h, :], Ub[:, vs], False, False, ["Mrb", "Ub"], [PB(4 + h // 8)])
                mm(pO[:, vs], Mrk[:, h, :], Vt[:, j, vs], False, True, ["Mrk", ("Vt", j)], [PB(4 + h // 8)])
            for h in range(16):
                hp, ps_, vs = hd(h)
                mm(pS[ps_, hp * 64:(hp + 1) * 64], Bt[:, j, vs], Ub[:, vs], True, False, [("Bt", j), "Ub"], [PB(6)])
                mm(pS[ps_, hp * 64:(hp + 1) * 64], Kt[:, j, vs], Vt[:, j, vs], False, True, [("Kt", j), ("Vt", j)], [PB(6)])
            ofl = of[:].rearrange("p a b -> p (a b)")
            V(("tensor_copy", C(out=ofl[:, 0:512], in_=pO[:, 0:512])), [PB(4)], ["of"])
            A(("copy", C(out=ofl[:, 512:1024], in_=pO[:, 512:1024])), [PB(5)], ["of"])
            for hp in range(8):
                V(("scalar_tensor_tensor", C(out=ST[:, hp, :], in0=ST[:, hp, :], scalar=gC[:, hp, j:j + 1],
                                                         in1=pS[:, hp * 64:(hp + 1) * 64], op0=ALU.mult, op1=ALU.add)),
                  ["ST", "gC", PB(6)], ["ST"])
            G(("tensor_copy", C(out=STb[:], in_=ST[:])), ["ST"], ["STb"])
            if limit <= 6:
                return []
            G(("tensor_tensor", C(out=osq, in0=of[:], in1=of[:], op=ALU.mult)), ["of"], ["outt"])
            V(("tensor_reduce", C(out=mean[:], in_=of[:], axis=AX.X, op=ALU.add)), ["of"], ["mean"])
            V(("tensor_reduce", C(out=ex2[:], in_=osq, axis=AX.X, op=ALU.add)), ["outt"], ["ex2"])
            V(("tensor_scalar", C(out=mean[:], in0=mean[:], scalar1=1.0 / 64, scalar2=None, op0=ALU.mult)), ["mean"], ["mean"])
            V(("tensor_tensor", C(out=var[:], in0=mean[:], in1=mean[:], op=ALU.mult)), ["mean"], ["var"])
            V(("scalar_tensor_tensor", C(out=var[:], in0=ex2[:], scalar=1.0 / 64, in1=var[:], op0=ALU.mult, op1=ALU.subtract)),
              ["ex2", "var"], ["var"])
            A(("activation", C(out=var[:], in_=var[:], func=AF.Sqrt, bias=64e-5)), ["var"], ["var"])
            V(("reciprocal", C(out=var[:], in_=var[:])), ["var"], ["var"])
            for h in range(16):
                eng = V if h % 2 == 0 else G
                eng(("tensor_scalar", C(out=of[:, h, :], in0=of[:, h, :], scalar1=mean[:, h:h + 1], scalar2=var[:, h:h + 1],
                                                   op0=ALU.subtract, op1=ALU.mult)), ["of", "mean", "var"], ["of"])
            for hp in range(8):
                tr(pb[hp // 4][:, (hp % 4) * 128:(hp % 4 + 1) * 128], ofl[:, hp * 128:(hp + 1) * 128], ident[:], ["of", "ident"], [PB(hp // 4)])
            for hp in range(8):
                src = pb[hp // 4][:, (hp % 4) * 128:(hp % 4 + 1) * 128]
                V(("scalar_tensor_tensor", C(out=ytmp[:], in0=src, scalar=vcol(I_LG, hp), in1=bonus[:, hp, ts],
                                                                 op0=ALU.mult, op1=ALU.add)), [PB(hp // 4), "vecs", ("bonus", hp)], ["ytmp"])
                V(("scalar_tensor_tensor", C(out=yT[:, hp, :], in0=ytmp[:], scalar=vcol(I_LB, hp), in1=sz[:, hp, ts],
                                                         op0=ALU.add, op1=ALU.mult)), ["ytmp", "vecs", ("sz", hp)], ["yT"])
            P.dma("gpsimd", pfx + "xr", xr[:], x[t0 + j * 128:t0 + (j + 1) * 128, :], writes=[("xt", 0)])
            for hf in range(2):
                for dc in range(8):
                    mm(pb[2 + hf], yT[:, dc, :], woutb[:, dc, hf * 512:(hf + 1) * 512], dc == 0, dc == 7, ["yT", "woutb"], [PB(2 + hf)])
                V(("tensor_tensor", C(out=outt[:, hf * 512:(hf + 1) * 512], in0=pb[2 + hf], in1=xr[:, hf * 512:(hf + 1) * 512],
                                                  op=ALU.add)), [PB(2 + hf), ("xt", 0)], ["outt"])
            P.dma("sync", pfx + "xo", xo[t0 + j * 128:t0 + (j + 1) * 128, :], outt[:], reads=["outt"], writes=[(pfx + "xo", b * 2 + j)])
    return [(pfx + "xo", i) for i in range(NB * 2)]


NEG = -30000.0
GELU_C = 1.5957691216057308


def nsa_host(inp):
    qg = inp["nsa_q_gain"][0]
    kg = inp["nsa_k_gain"][0]
    gains = np.stack([np.tile(qg, 2), np.tile(kg[0], 2), np.tile(kg[1], 2), np.tile(kg[2], 2)], 1).astype(np.float32)
    pe = inp["nsa_cmp_pe"][0]
    peT = np.concatenate([pe[0].T, pe[1].T], 0).astype(np.float32)
    b1 = inp["nsa_cmp_b1"][0].reshape(2, 2, 128).transpose(2, 0, 1).astype(np.float32)
    w2 = inp["nsa_cmp_w2"][0].reshape(2, 2, 128, 64).transpose(2, 1, 0, 3).astype(np.float32)
    return dict(gains=np.ascontiguousarray(gains), peT=np.ascontiguousarray(peT), b1=np.ascontiguousarray(b1),
                w2=np.ascontiguousarray(w2.reshape(128, 256)))


def emit_nsa(nc, P, st, x, xo, wd, T, pfx="n", limit=99):
    D = 1024
    NB = T // 256
    NT = T // 128
    sbn = [0]

    def sb(shape, dt, name=None):
        sbn[0] += 1
        return st.enter_context(nc.sbuf_tensor(f"{pfx}_{name or 't'}{sbn[0]}", shape, dt))

    pball = st.enter_context(nc.psum_tensor(f"{pfx}_psum", [128, 8, 512], F32))
    pb = [pball[:, i, :] for i in range(8)]
    PB = lambda i: (pfx + "pb", i)

    def V(fn, r=(), w=()): P.op("vector", fn, r, w)
    def G(fn, r=(), w=()): P.op("gpsimd", fn, r, w)
    def A(fn, r=(), w=()): P.op("scalar", fn, r, w)

    def mm(out, lhsT, rhs, start=True, stop=True, r=(), w=()):
        P.op("tensor", ("matmul", C(out=out, lhsT=lhsT, rhs=rhs, start=start, stop=stop)), r, w, pe_accum=not start)

    def tr(out, in_, ident, r=(), w=()):
        P.op("tensor", ("transpose", C(out=out, in_=in_, identity=ident)), r, w)

    ones = sb([128, 512], F32, "ones")
    onesb = sb([64, 2048], BF16, "onesb")
    zerob = sb([128, 4, 128], BF16, "zerob")
    ident = sb([128, 128], F32, "ident")
    identb = sb([128, 128], BF16, "identb")
    BD = sb([128, 128], F32, "BD")
    CB = sb([128, 4, 128], BF16, "CB")
    AB = sb([128, 4, 128], BF16, "AB")
    E = sb([128, 32, 128], BF16, "E")
    ov = sb([128, 2, 64], BF16, "ov")
    ovf = sb([128, 2, 64], F32, "ovf")
    G(("memset", C(ones[:], 1.0)), w=["ones"])
    G(("memset", C(onesb[:], 1.0)), w=["onesb"])
    G(("memset", C(zerob[:], 0.0)), w=["zerob"])
    G(("affine_select", C(out=ident[:], in_=ones[:, 0:128], pattern=[[-1, 128]], compare_op=ALU.is_equal, fill=0.0, base=0,
                          channel_multiplier=1)), ["ones"], ["ident"])
    G(("tensor_copy", C(out=identb[:], in_=ident[:])), ["ident"], ["identb"])
    G(("memset", C(BD[:], 0.0)), w=["BD"])
    G(("memset", C(BD[0:64, 0:64], 1.0)), w=["BD"])
    G(("memset", C(BD[64:128, 64:128], 1.0)), w=["BD"])
    G(("affine_select", C(out=CB[:], in_=zerob[:], pattern=[[0, 4], [1, 128]], compare_op=ALU.is_ge, fill=NEG, base=0,
                          channel_multiplier=-1)), ["zerob"], ["CB"])
    G(("affine_select", C(out=AB[:], in_=zerob[:], pattern=[[0, 4], [-1, 128]], compare_op=ALU.is_gt, fill=NEG, base=0,
                          channel_multiplier=1)), ["zerob"], ["AB"])
    ob3 = onesb[:].rearrange("p (a b) -> p a b", a=32)
    G(("memset", C(E[:], 0.0)), w=["E"])
    G(("affine_select", C(out=E[0:64, :, 0:64], in_=ob3, pattern=[[-2, 32], [0, 64]], compare_op=ALU.is_equal, fill=0.0, base=0,
                          channel_multiplier=1)), ["onesb"], ["E"])
    G(("affine_select", C(out=E[0:64, :, 64:128], in_=ob3, pattern=[[-2, 32], [0, 64]], compare_op=ALU.is_equal, fill=0.0, base=-1,
                          channel_multiplier=1)), ["onesb"], ["E"])
    for c in range(2):
        G(("affine_select", C(out=ovf[:, c, :], in_=ones[:, 0:64], pattern=[[-64, 64]], compare_op=ALU.is_ge, fill=0.0,
                              base=2048 * c + 31, channel_multiplier=16)), ["ones"], ["ovf"])
        G(("affine_select", C(out=ovf[:, c, :], in_=ovf[:, c, :], pattern=[[64, 64]], compare_op=ALU.is_ge, fill=0.0,
                              base=63 - 2048 * c, channel_multiplier=-16)), ["ovf"], ["ovf"])
    G(("tensor_copy", C(out=ov[:], in_=ovf[:])), ["ovf"], ["ov"])

    gb = sb([128, D], F32, "gb")
    gains = sb([128, 4], F32, "gains")
    peT = sb([128, 32], F32, "peT")
    peTb = sb([128, 32], BF16, "peTb")
    b1 = sb([128, 2, 2], F32, "b1")
    cb = sb([128, 2, 2], F32, "cb")
    w2f = sb([128, 256], F32, "w2f")
    w2b = sb([128, 2, 2, 64], BF16, "w2b")
    w1b = sb([128, 32, 256], BF16, "w1b")
    woutb = sb([128, 8, 1024], BF16, "woutb")
    stg = [sb([128, 1024], F32, f"stg{i}") for i in range(2)]
    stgb = [sb([128, 1024], BF16, f"stgb{i}") for i in range(2)]
    wslot = [sb([128, 8, 128], BF16, f"wslot{i}") for i in range(2)]
    NCH = 29
    wscr = nc.dram_tensor(pfx + "_wscr", [NCH, 128, 8, 128], BF16, kind="Internal").ap()
    P.dma("sync", pfx + "gb", gb[:], wd["g"].partition_broadcast(128), writes=["gb"])
    P.dma("sync", pfx + "gains", gains[:], wd["gains"], writes=["gains"])
    P.dma("sync", pfx + "peT", peT[:], wd["peT"], writes=["peT"])
    P.dma("sync", pfx + "b1", b1[:].rearrange("p a b -> p (a b)"), wd["b1"].rearrange("p a b -> p (a b)"), writes=["b1"])
    P.dma("sync", pfx + "w2f", w2f[:], wd["w2"], writes=["w2f"])
    V(("tensor_copy", C(out=w2b[:].rearrange("p a b c -> p (a b c)"), in_=w2f[:])), ["w2f"], ["w2b"])
    V(("tensor_copy", C(out=peTb[:], in_=peT[:])), ["peT"], ["peTb"])
    V(("tensor_scalar", C(out=gains[:, 0:1], in0=gains[:, 0:1], scalar1=0.125, scalar2=None, op0=ALU.mult)), ["gains"], ["gains"])
    nst = [0]

    def load_cast(dst_ap, src_ap, np_, ncols, wres, p0=0):
        i = nst[0] % 2
        nst[0] += 1
        q = "sync" if i == 0 else "gpsimd"
        P.dma(q, pfx + f"stg{i}", stg[i][p0:p0 + np_, 0:ncols], src_ap, writes=[("stg", i)])
        eng = "vector" if i == 0 else "gpsimd"
        P.op(eng, ("tensor_copy", C(out=dst_ap, in_=stg[i][p0:p0 + np_, 0:ncols])), [("stg", i)], [wres])
        return i

    CQ, CKS, CKW, CZ, CCV, CVS, CVW, CG = 0, 8, 10, 12, 20, 24, 26, 28
    win_v = wd["w_in"].rearrange("(c p) f -> p c f", p=128)
    WS_ALL = []

    def scr_write(i, dst, src, key):
        P.dma("sync" if i == 0 else "gpsimd", pfx + f"wscr{i}", dst, src, reads=[("stgb", i)], writes=[key])
        WS_ALL.append(key)

    for dc in range(8):
        i = nst[0] % 2
        load_cast(stgb[i][:], win_v[:, dc, 0:1024], 128, 1024, ("stgb", i))
        srcv = stgb[i][:].rearrange("p (m e j n) -> p m e j n", m=2, e=2, j=4)
        for m in range(2):
            for e in range(2):
                dst = wscr[CQ + m * 4:CQ + m * 4 + 4, :, dc, e * 64:(e + 1) * 64].rearrange("j p n -> p j n")
                scr_write(i, dst, srcv[:, m, e, :, :], ("wscr", "q", dc, m, e))
        i = nst[0] % 2
        load_cast(stgb[i][:], win_v[:, dc, 1024:2048], 128, 1024, ("stgb", i))
        s4 = stgb[i][:].rearrange("p (a g n) -> p a g n", a=4, g=4)
        for a_ in range(2):
            dst = wscr[CCV:CCV + 4, :, dc, a_ * 64:(a_ + 1) * 64].rearrange("g p n -> p g n")
            scr_write(i, dst, s4[:, a_, :, :], ("wscr", "cv", dc, a_))
        s2 = stgb[i][:].rearrange("p (a c f) -> p a c f", a=4, c=2)
        scr_write(i, wscr[CKS:CKS + 2, :, dc, :].rearrange("c p f -> p c f"), s2[:, 2, :, :], ("wscr", "ks", dc))
        scr_write(i, wscr[CVS:CVS + 2, :, dc, :].rearrange("c p f -> p c f"), s2[:, 3, :, :], ("wscr", "vs", dc))
        i = nst[0] % 2
        load_cast(stgb[i][:], win_v[:, dc, 2048:3072], 128, 1024, ("stgb", i))
        s8 = stgb[i][:].rearrange("p (c f) -> p c f", c=8)
        scr_write(i, wscr[CKW:CKW + 2, :, dc, :].rearrange("c p f -> p c f"), s8[:, 0:2, :], ("wscr", "kw", dc))
        scr_write(i, wscr[CVW:CVW + 2, :, dc, :].rearrange("c p f -> p c f"), s8[:, 2:4, :], ("wscr", "vw", dc))
        scr_write(i, wscr[CZ:CZ + 4, :, dc, :].rearrange("c p f -> p c f"), s8[:, 4:8, :], ("wscr", "z0", dc))
        i = nst[0] % 2
        load_cast(stgb[i][:, 0:560], win_v[:, dc, 3072:3632], 128, 560, ("stgb", i))
        s5 = stgb[i][:, 0:512].rearrange("p (c f) -> p c f", c=4)
        scr_write(i, wscr[CZ + 4:CZ + 8, :, dc, :].rearrange("c p f -> p c f"), s5, ("wscr", "z1", dc))
        scr_write(i, wscr[CG, :, dc, 0:48], stgb[i][:, 512:560], ("wscr", "g", dc))
    wout_v = wd["w_out"].rearrange("(c p) f -> p c f", p=128)
    for c in range(8):
        load_cast(woutb[:, c, :], wout_v[:, c, :], 128, 1024, "woutb")
    for kv in range(2):
        w1v = wd["w1"][kv].rearrange("(l d) f -> d l f", d=64)
        for l4 in range(0, 32, 4):
            i = nst[0] % 2
            P.dma("sync" if i == 0 else "gpsimd", pfx + f"stg{i}", stg[i][kv * 64:(kv + 1) * 64, :].rearrange("p (l f) -> p l f", l=4),
                  w1v[:, l4:l4 + 4, :], writes=[("stg", i)])
            nst[0] += 1
            P.op("vector" if i == 0 else "gpsimd",
                 ("tensor_copy", C(out=w1b[kv * 64:(kv + 1) * 64, l4:l4 + 4, :].rearrange("p l f -> p (l f)"),
                                   in_=stg[i][kv * 64:(kv + 1) * 64, :])), [("stg", i)], ["w1b"])
    for kv in range(2):
        ps_ = slice(kv * 64, (kv + 1) * 64)
        for fc in range(2):
            for l in range(32):
                mm(pb[0][:, (kv * 2 + fc):(kv * 2 + fc) + 1], w1b[ps_, l, fc * 128:(fc + 1) * 128], peTb[ps_, l:l + 1], l == 0, l == 31,
                   ["w1b", "peTb"], [PB(0)])
    V(("tensor_tensor", C(out=cb[:].rearrange("p a b -> p (a b)"), in0=pb[0][:, 0:4], in1=b1[:].rearrange("p a b -> p (a b)"), op=ALU.add)),
      [PB(0), "b1"], ["cb"])
    if limit <= 0:
        return []

    ksT = sb([128, 2, T], BF16, "ksT")
    kwT = sb([128, 2, T], BF16, "kwT")
    vs_tok = sb([128, NT, 4, 65], BF16, "vs_tok")
    vw_tok = sb([128, NT, 4, 65], BF16, "vw_tok")
    kcT = sb([128, 2, 256], BF16, "kcT")
    vcT = sb([128, 2, 256], F32, "vcT")
    vc_tok = sb([128, 2, 4, 65], BF16, "vc_tok")
    G(("memset", C(vs_tok[:], 1.0)), w=["vtok"])
    G(("memset", C(vw_tok[:], 1.0)), w=["vtok"])
    G(("memset", C(vc_tok[:], 1.0)), w=["vc_tok"])
    G(("memset", C(kcT[:], 0.0)), w=["kcT"])
    G(("memset", C(vcT[:], 0.0)), w=["vcT"])
    xt = [sb([128, D], F32, f"xt{i}") for i in range(2)]
    ss = sb([128, 1], F32, "ss")
    rs = sb([128, 1], F32, "rs")
    ht = sb([128, D], F32, "ht")
    hT = sb([128, 8, 256], BF16, "hT")
    qT = sb([128, 2, 4, 256], BF16, "qT")
    szT = sb([128, 8, 256], BF16, "szT")
    craw = sb([128, 4, 272], BF16, "craw")
    gsb = sb([128, 2, 48], F32, "gsb")
    sq = sb([128, 256], F32, "sq")
    rstd = sb([128, 256], F32, "rstd")
    xs = sb([128, 256], F32, "xs")
    g1 = sb([128, 256], F32, "g1")
    hid = sb([128, 4, 64], BF16, "hid")
    kcv = sb([128, 2, 2, 16], F32, "kcv")
    Pc = sb([128, 2, 4, 128], BF16, "Pc")
    Ps = [sb([128, 4, 128], BF16, f"Ps{i}") for i in range(2)]
    cmpb = sb([128, 2, 4, 128], BF16, "cmpb")
    Aadd = sb([128, 64], F32, "Aadd")
    A0 = sb([128, 64], F32, "A0")
    imp = sb([128, 64], F32, "imp")
    imp2 = sb([128, 64], F32, "imp2")
    mx = sb([128, 16], F32, "mx")
    selm = sb([128, 64], F32, "selm")
    selmT = sb([128, 4, 128], BF16, "selmT")
    rc = sb([128, 4], F32, "rc")
    cf = sb([128, 4], F32, "cf")
    oacc = sb([128, 16, 64], F32, "oacc")
    yT = sb([128, 8, 128], BF16, "yT")
    outt = sb([128, D], F32, "outt")
    G(("memset", C(craw[:], 0.0)), w=["craw"])
    G(("memset", C(selmT[:], 0.0)), w=["selmT"])
    G(("memset", C(A0[:], 0.0)), w=["A0"])
    G(("memset", C(A0[:, 0:1], 10000.0)), w=["A0"])
    D0 = sb([128, 64], F32, "D0")
    Dd = sb([128, 64], F32, "Dd")
    At = sb([128, 64], F32, "At")
    G(("iota", C(D0[:], pattern=[[-1, 64]], base=0, channel_multiplier=0, allow_small_or_imprecise_dtypes=True)), w=["D0"])
    G(("tensor_scalar", C(out=D0[64:128, :], in0=D0[64:128, :], scalar1=1.0, scalar2=None, op0=ALU.add)), ["D0"], ["D0"])
    eps = 1e-6
    wcnt = [0]

    def wload(ch):
        s_ = wcnt[0] % 2
        wcnt[0] += 1
        P.dma("sync", pfx + f"ws{s_}", wslot[s_][:], wscr[ch], reads=WS_ALL, writes=[("wslot", s_)])
        return s_

    def rmsnorm_evac(ps_ap, gcol, dst_ap, pbi, wres, ncols=256):
        A(("activation", C(out=sq[:, 0:ncols], in_=ps_ap, func=AF.Square)), [PB(pbi)], ["sq"])
        mm(pb[7][:, 0:ncols], BD[:], sq[:, 0:ncols], True, True, ["BD", "sq"], [PB(7)])
        A(("activation", C(out=rstd[:, 0:ncols], in_=pb[7][:, 0:ncols], func=AF.Sqrt, scale=1.0 / 64, bias=eps)), [PB(7)], ["rstd"])
        V(("reciprocal", C(out=rstd[:, 0:ncols], in_=rstd[:, 0:ncols])), ["rstd"], ["rstd"])
        V(("scalar_tensor_tensor", C(out=dst_ap, in0=ps_ap, scalar=gains[:, gcol:gcol + 1], in1=rstd[:, 0:ncols], op0=ALU.mult, op1=ALU.mult)),
          [PB(pbi), "gains", "rstd"], [wres])

    outs = []
    for b in range(NB):
        t0 = b * 256
        for i in range(2):
            xb = xt[i]
            P.dma("sync", pfx + f"x{i}", xb[:], x[t0 + i * 128:t0 + (i + 1) * 128, :], writes=[("xt", i)])
            A(("activation", C(out=ht[:], in_=xb[:], func=AF.Square, accum_out=ss[:])), [("xt", i)], ["ht", "ss"])
            A(("activation", C(out=rs[:], in_=ss[:], func=AF.Sqrt, scale=1.0 / D, bias=eps)), ["ss"], ["rs"])
            V(("reciprocal", C(out=rs[:], in_=rs[:])), ["rs"], ["rs"])
            V(("scalar_tensor_tensor", C(out=ht[:], in0=xb[:], scalar=rs[:, 0:1], in1=gb[:], op0=ALU.mult, op1=ALU.mult)),
              [("xt", i), "rs", "gb"], ["ht"])
            for half in range(2):
                for c in range(4):
                    cc = half * 4 + c
                    tr(pb[half][:, c * 128:(c + 1) * 128], ht[:, cc * 128:(cc + 1) * 128], ident[:], ["ht", "ident"], [PB(half)])
                pv = pb[half].rearrange("p (c t) -> p c t", c=4)
                dst = hT[:, half * 4:(half + 1) * 4, i * 128:(i + 1) * 128]
                if half == 0:
                    V(("tensor_copy", C(out=dst, in_=pv)), [PB(half)], ["hT"])
                else:
                    A(("copy", C(out=dst, in_=pv)), [PB(half)], ["hT"])
        if limit <= 1:
            return []
        def proj_fm(ch, pbi, M=128):
            s_ = wload(ch)
            for dc in range(8):
                mm(pb[pbi][0:M, 0:256], wslot[s_][:, dc, 0:M], hT[:, dc, :], dc == 0, dc == 7, [("wslot", s_), "hT"], [PB(pbi)])
        for m in range(2):
            for j in range(4):
                pbi = (m * 4 + j) % 2
                proj_fm(CQ + m * 4 + j, pbi)
                rmsnorm_evac(pb[pbi][:, 0:256], 0, qT[:, m, j, :], pbi, "qT")
        for m in range(2):
            proj_fm(CKS + m, m)
            rmsnorm_evac(pb[m][:, 0:256], 2, ksT[:, m, t0:t0 + 256], m, "kT")
        for m in range(2):
            proj_fm(CKW + m, m)
            rmsnorm_evac(pb[m][:, 0:256], 3, kwT[:, m, t0:t0 + 256], m, "kT")
        for c in range(8):
            proj_fm(CZ + c, c % 2)
            A(("activation", C(out=szT[:, c, :], in_=pb[c % 2][:, 0:256], func=AF.Silu)), [PB(c % 2)], ["szT"])
        for g in range(4):
            proj_fm(CCV + g, g % 2)
            A(("copy", C(out=craw[:, g, 16:272], in_=pb[g % 2][:, 0:256])), [PB(g % 2)], ["craw"])
        for (ch0, vtok) in ((CVS, vs_tok), (CVW, vw_tok)):
            for c2 in range(2):
                s_ = wload(ch0 + c2)
                for i in range(2):
                    for dc in range(8):
                        mm(pb[i][:, 0:128], hT[:, dc, i * 128:(i + 1) * 128], wslot[s_][:, dc, :], dc == 0, dc == 7,
                           [("wslot", s_), "hT"], [PB(i)])
                    V(("tensor_copy", C(out=vtok[:, 2 * b + i, 2 * c2:2 * c2 + 2, 0:64],
                                        in_=pb[i][:, 0:128].rearrange("p (g n) -> p g n", g=2))), [PB(i)], ["vtok"])
        s_ = wload(CG)
        for i in range(2):
            for dc in range(8):
                mm(pb[i][:, 0:48], hT[:, dc, i * 128:(i + 1) * 128], wslot[s_][:, dc, 0:48], dc == 0, dc == 7, [("wslot", s_), "hT"], [PB(i)])
            A(("activation", C(out=gsb[:, i, :], in_=pb[i][:, 0:48], func=AF.Sigmoid)), [PB(i)], ["gsb"])
        if limit <= 2:
            return []
        i0 = 1 if b == 0 else 0
        ni = 16 - i0
        n0 = 16 * b - 1 + i0
        ph = pb[2]
        for kv in range(2):
            ps_ = slice(kv * 64, (kv + 1) * 64)
            for fc in range(2):
                reg = (kv * 2 + fc) * 64
                for l in range(32):
                    rhs = craw[ps_, :, l + 16 * i0:l + 16 * i0 + 16 * (ni - 1) + 1:16]
                    outp = ph[:, reg:reg + 64].rearrange("p (g i) -> p g i", g=4)[:, :, 0:ni]
                    mm(outp, w1b[ps_, l, fc * 128:(fc + 1) * 128], rhs, l == 0, l == 31, ["w1b", "craw"], [PB(2)])
        for kv in range(2):
            for fc in range(2):
                reg = (kv * 2 + fc) * 64
                A(("activation", C(out=xs[:, reg:reg + 64], in_=ph[:, reg:reg + 64], func=AF.Identity, bias=cb[:, kv, fc:fc + 1])),
                  [PB(2), "cb"], ["xs"])
        V(("tensor_tensor", C(out=g1[:], in0=xs[:], in1=xs[:], op=ALU.mult)), ["xs"], ["g1"])
        V(("tensor_scalar", C(out=g1[:], in0=g1[:], scalar1=0.044715, scalar2=1.0, op0=ALU.mult, op1=ALU.add)), ["g1"], ["g1"])
        V(("tensor_tensor", C(out=g1[:], in0=g1[:], in1=xs[:], op=ALU.mult)), ["g1", "xs"], ["g1"])
        A(("activation", C(out=g1[:], in_=g1[:], func=AF.Sigmoid, scale=GELU_C)), ["g1"], ["g1"])
        V(("tensor_tensor", C(out=hid[:].rearrange("p a b -> p (a b)"), in0=g1[:], in1=xs[:], op=ALU.mult)), ["g1", "xs"], ["hid"])
        pk = pb[3][:, 0:64].rearrange("p (kv m i) -> p kv m i", kv=2, m=2)
        for kv in range(2):
            for g in range(4):
                m_, e_ = g // 2, g % 2
                for fc in range(2):
                    mm(pk[e_ * 64:(e_ + 1) * 64, kv, m_, 0:ni], w2b[:, fc, kv, :], hid[:, kv * 2 + fc, g * 16:g * 16 + ni], fc == 0, fc == 1,
                       ["w2b", "hid"], [PB(3)])
        V(("tensor_copy", C(out=kcv[:, :, :, 0:ni], in_=pk[:, :, :, 0:ni])), [PB(3)], ["kcv"])
        for m_ in range(2):
            G(("tensor_copy", C(out=vcT[:, m_, n0:n0 + ni], in_=kcv[:, 1, m_, 0:ni])), ["kcv"], ["vcT"])
        kflat = kcv[:, 0, :, :].rearrange("p m i -> p (m i)")
        A(("activation", C(out=sq[:, 0:32], in_=kflat, func=AF.Square)), ["kcv"], ["sq"])
        mm(pb[7][:, 0:32], BD[:], sq[:, 0:32], True, True, ["BD", "sq"], [PB(7)])
        A(("activation", C(out=rstd[:, 0:32], in_=pb[7][:, 0:32], func=AF.Sqrt, scale=1.0 / 64, bias=eps)), [PB(7)], ["rstd"])
        V(("reciprocal", C(out=rstd[:, 0:32], in_=rstd[:, 0:32])), ["rstd"], ["rstd"])
        V(("scalar_tensor_tensor", C(out=sq[:, 0:32], in0=kflat, scalar=gains[:, 1:2], in1=rstd[:, 0:32], op0=ALU.mult, op1=ALU.mult)),
          ["kcv", "gains", "rstd"], ["sq"])
        for m_ in range(2):
            G(("tensor_copy", C(out=kcT[:, m_, n0:n0 + ni], in_=sq[:, m_ * 16:m_ * 16 + ni])), ["sq"], ["kcT"])
        for c in range(2):
            for m_ in range(2):
                tr(pb[4][:, (c * 2 + m_) * 128:(c * 2 + m_ + 1) * 128], vcT[:, m_, c * 128:(c + 1) * 128], ident[:], ["vcT", "ident"], [PB(4)])
        V(("tensor_copy", C(out=vc_tok[:, :, :, 0:64], in_=pb[4].rearrange("p (c g n) -> p c g n", c=2, g=4))), [PB(4)], ["vc_tok"])
        G(("tensor_copy", C(out=craw[:, :, 0:16], in_=craw[:, :, 256:272])), ["craw"], ["craw"])
        if limit <= 3:
            return []
        for il in range(2):
            i = 2 * b + il
            tl = slice(il * 128, (il + 1) * 128)
            V(("tensor_scalar", C(out=Dd[:], in0=D0[:], scalar1=float(2 * i), scalar2=None, op0=ALU.add)), ["D0"], ["Dd"])
            V(("tensor_scalar", C(out=Aadd[:], in0=Dd[:], scalar1=0.0, scalar2=None, op0=ALU.is_ge)), ["Dd"], ["Aadd"])
            V(("tensor_scalar", C(out=At[:], in0=Dd[:], scalar1=1.0, scalar2=10000.0, op0=ALU.is_le, op1=ALU.mult)), ["Dd"], ["At"])
            V(("tensor_tensor", C(out=Aadd[:], in0=Aadd[:], in1=At[:], op=ALU.mult)), ["Aadd", "At"], ["Aadd"])
            V(("tensor_tensor", C(out=Aadd[:], in0=Aadd[:], in1=A0[:], op=ALU.max)), ["Aadd", "A0"], ["Aadd"])
            V(("tensor_scalar", C(out=At[:], in0=Dd[:], scalar1=0.0, scalar2=-1e30, op0=ALU.is_lt, op1=ALU.mult)), ["Dd"], ["At"])
            V(("tensor_tensor", C(out=Aadd[:], in0=Aadd[:], in1=At[:], op=ALU.add)), ["Aadd", "At"], ["Aadd"])
            cts = []
            for c in range(2):
                base = 128 * i - 2048 * c - 31
                if base + 127 < 0:
                    continue
                need_bias = base - 16 * 127 < 0
                cts.append((c, need_bias))
                if need_bias:
                    G(("affine_select", C(out=cmpb[:, c, :, :], in_=zerob[:], pattern=[[0, 4], [1, 128]], compare_op=ALU.is_ge, fill=NEG,
                                          base=base, channel_multiplier=-16)), ["zerob"], ["cmpb"])
            for g in range(4):
                m_, e_ = g // 2, g % 2
                ps_ = slice(e_ * 64, (e_ + 1) * 64)
                rq = qT[ps_, m_, :, tl]
                pO4 = pball[:, 4:8, :]
                first = [True]

                def combine(br, pO_of_h, srcres):
                    V(("tensor_scalar", C(out=rc[:], in0=pO_of_h(None), scalar1=1e-30, scalar2=None, op0=ALU.max)), srcres, ["rc"])
                    V(("reciprocal", C(out=rc[:], in_=rc[:])), ["rc"], ["rc"])
                    V(("tensor_tensor", C(out=cf[:], in0=rc[:], in1=gsb[:, il, br * 16 + 4 * g:br * 16 + 4 * g + 4], op=ALU.mult)),
                      ["rc", "gsb"], ["cf"])
                    for h in range(4):
                        if first[0]:
                            V(("tensor_scalar", C(out=oacc[:, 4 * g + h, :], in0=pO_of_h(h), scalar1=cf[:, h:h + 1], scalar2=None, op0=ALU.mult)),
                              srcres + ["cf"], ["oacc"])
                        else:
                            V(("scalar_tensor_tensor", C(out=oacc[:, 4 * g + h, :], in0=pO_of_h(h), scalar=cf[:, h:h + 1], in1=oacc[:, 4 * g + h, :],
                                                         op0=ALU.mult, op1=ALU.add)), srcres + ["cf", "oacc"], ["oacc"])
                    first[0] = False

                for ci, (c, nb_) in enumerate(cts):
                    psS = pb[ci % 2]
                    mm(psS, kcT[ps_, m_, c * 128:(c + 1) * 128], rq, True, not nb_, ["kcT", "qT"], [PB(ci % 2)])
                    if nb_:
                        mm(psS, identb[:], cmpb[:, c, :, :], False, True, ["identb", "cmpb"], [PB(ci % 2)])
                    A(("activation", C(out=Pc[:, c, :, :], in_=psS.rearrange("p (h t) -> p h t", h=4), func=AF.Exp)), [PB(ci % 2)], ["Pc"])
                pOc = pb[4]
                for h in range(4):
                    for ci, (c, nb_) in enumerate(cts):
                        mm(pOc[:, h * 128:h * 128 + 65], Pc[:, c, h, :], vc_tok[:, c, g, :], ci == 0, ci == len(cts) - 1, ["Pc", "vc_tok"], [PB(4)])
                pI = pb[5]
                for h in range(4):
                    for ci, (c, nb_) in enumerate(cts):
                        mm(pI[:, h * 64:(h + 1) * 64], Pc[:, c, h, :], ov[:, c, :], ci == 0, ci == len(cts) - 1, ["Pc", "ov"], [PB(5)])
                pOc3 = pOc.rearrange("p (h n) -> p h n", h=4)
                combine(0, lambda h: pOc3[:, :, 64] if h is None else pOc[:, h * 128:h * 128 + 64], [PB(4)])
                for h in range(4):
                    if h == 0:
                        V(("tensor_scalar", C(out=imp[:], in0=pI[:, 0:64], scalar1=rc[:, 0:1], scalar2=None, op0=ALU.mult)), [PB(5), "rc"], ["imp"])
                    else:
                        V(("scalar_tensor_tensor", C(out=imp[:], in0=pI[:, h * 64:(h + 1) * 64], scalar=rc[:, h:h + 1], in1=imp[:], op0=ALU.mult,
                                                     op1=ALU.add)), [PB(5), "rc", "imp"], ["imp"])
                need_sel = i >= 8
                if need_sel:
                    V(("tensor_tensor", C(out=imp[:], in0=imp[:], in1=Aadd[:], op=ALU.add)), ["imp", "Aadd"], ["imp"])
                    V(("max", C(out=mx[:, 0:8], in_=imp[:])), ["imp"], ["mx"])
                    V(("match_replace", C(out=imp2[:], in_to_replace=mx[:, 0:8], in_values=imp[:], imm_value=-3e38)), ["imp", "mx"], ["imp2"])
                    V(("max", C(out=mx[:, 8:16], in_=imp2[:])), ["imp2"], ["mx"])
                    V(("tensor_scalar", C(out=selm[:], in0=imp[:], scalar1=mx[:, 15:16], scalar2=-NEG, op0=ALU.is_lt, op1=ALU.mult)),
                      ["imp", "mx"], ["selm"])
                    V(("tensor_scalar", C(out=selm[:], in0=selm[:], scalar1=-1.0, scalar2=None, op0=ALU.mult)), ["selm"], ["selm"])
                    tr(pb[6][0:64, 0:128], selm[:], ident[:], ["selm", "ident"], [PB(6)])
                    V(("tensor_copy", C(out=selmT[0:64, :, :], in_=pb[6][0:64, 0:128].unsqueeze(1).broadcast_to([64, 4, 128]))), [PB(6)], ["selmT"])
                for br, (kT, vtok, tiles) in ((1, (ksT, vs_tok, list(range(0, i + 1)))),
                                              (2, (kwT, vw_tok, list(range(max(0, i - 4), i + 1))))):
                    for ti, s_t in enumerate(tiles):
                        par = ti % 2
                        psS = pb[par]
                        extra = []
                        if br == 1 and need_sel:
                            extra.append((E[:, s_t, :], selmT[:], ["E", "selmT"]))
                        if s_t == i:
                            extra.append((identb[:], CB[:], ["identb", "CB"]))
                        if br == 2 and s_t == i - 4:
                            extra.append((identb[:], AB[:], ["identb", "AB"]))
                        mm(psS, kT[ps_, m_, s_t * 128:(s_t + 1) * 128], rq, True, len(extra) == 0, ["kT", "qT"], [PB(par)])
                        for xi, (l_, r_, rr) in enumerate(extra):
                            mm(psS, l_, r_, False, xi == len(extra) - 1, rr, [PB(par)])
                        A(("activation", C(out=Ps[par][:], in_=psS.rearrange("p (h t) -> p h t", h=4), func=AF.Exp)), [PB(par)], [("Ps", par)])
                        for h in range(4):
                            mm(pO4[:, h, 0:65], Ps[par][:, h, :], vtok[:, s_t, g, :], ti == 0, ti == len(tiles) - 1, [("Ps", par), "vtok"], [PB(4 + h)])
                    combine(br, lambda h: pO4[:, :, 64] if h is None else pO4[:, h, 0:64], [PB(4), PB(5), PB(6), PB(7)])
            if limit <= 4:
                continue
            ofl = oacc[:].rearrange("p a b -> p (a b)")
            for c in range(8):
                tr(pb[c // 4][:, (c % 4) * 128:(c % 4 + 1) * 128], ofl[:, c * 128:(c + 1) * 128], ident[:], ["oacc", "ident"], [PB(c // 4)])
            for hf in range(2):
                V(("tensor_tensor", C(out=yT[:, hf * 4:(hf + 1) * 4, :], in0=pb[hf].rearrange("p (c t) -> p c t", c=4),
                                      in1=szT[:, hf * 4:(hf + 1) * 4, tl], op=ALU.mult)), [PB(hf), "szT"], ["yT"])
            P.dma("gpsimd", pfx + "xr", xt[0][:], x[t0 + il * 128:t0 + (il + 1) * 128, :], writes=[("xt", 0)])
            for hf in range(2):
                for dc in range(8):
                    mm(pb[2 + hf], yT[:, dc, :], woutb[:, dc, hf * 512:(hf + 1) * 512], dc == 0, dc == 7, ["yT", "woutb"], [PB(2 + hf)])
                V(("tensor_tensor", C(out=outt[:, hf * 512:(hf + 1) * 512], in0=pb[2 + hf], in1=xt[0][:, hf * 512:(hf + 1) * 512], op=ALU.add)),
                  [PB(2 + hf), ("xt", 0)], ["outt"])
            P.dma("sync", pfx + "xo", xo[t0 + il * 128:t0 + (il + 1) * 128, :], outt[:], reads=["outt"], writes=[(pfx + "xo", i)])
            outs.append((pfx + "xo", i))
    return outs


T_SEQ = 4096
FUSED = True
_CACHE = {}


def _dt(nc, n, s):
    return nc.dram_tensor(n, s, F32, kind="ExternalInput").ap()


def _rwkv_wd(nc):
    return dict(g=_dt(nc, "r_g", [1, 1024]), vecs=_dt(nc, "r_vecs", [128, NV, 8]), w_in=_dt(nc, "r_w_in", [1024, 4096]),
                w1=_dt(nc, "r_w1", [1024, 64]), a1=_dt(nc, "r_a1", [1024, 64]), w2=_dt(nc, "r_w2", [64, 1024]),
                a2=_dt(nc, "r_a2", [64, 1024]), w_out=_dt(nc, "r_w_out", [1024, 1024]))


def _nsa_wd(nc):
    return dict(g=_dt(nc, "n_g", [1, 1024]), w_in=_dt(nc, "n_w_in", [1024, 3632]), w_out=_dt(nc, "n_w_out", [1024, 1024]),
                w1=_dt(nc, "n_w1", [2, 2048, 256]), gains=_dt(nc, "n_gains", [128, 4]), peT=_dt(nc, "n_peT", [128, 32]),
                b1=_dt(nc, "n_b1", [128, 2, 2]), w2=_dt(nc, "n_w2", [128, 256]))


def _build(which):
    T = T_SEQ
    nc = bass.Bass("TRN2", target_bir_lowering=False)
    x = _dt(nc, "x", [T, 1024])
    xo = nc.dram_tensor("xo", [T, 1024], F32, kind="ExternalOutput").ap()
    if which == "fused":
        x1 = nc.dram_tensor("x1_scr", [T, 1024], F32, kind="Internal").ap()
        rwd = _rwkv_wd(nc)
        nwd = _nsa_wd(nc)
        with ExitStack() as st:
            P = Prog(nc, st)
            outs = emit_rwkv(nc, P, st, x, x1, rwd, T)
            P.final_wait("sync", outs)
            P.emit()
        with ExitStack() as st:
            P = Prog(nc, st)
            outs = emit_nsa(nc, P, st, x1, xo, nwd, T)
            P.final_wait("sync", outs)
            P.emit()
    else:
        wd = _rwkv_wd(nc) if which == "rwkv" else _nsa_wd(nc)
        with ExitStack() as st:
            P = Prog(nc, st)
            outs = (emit_rwkv if which == "rwkv" else emit_nsa)(nc, P, st, x, xo, wd, T)
            P.final_wait("sync", outs)
            P.emit()
    return nc


def _get(which):
    if which not in _CACHE:
        _CACHE[which] = _build(which)
    return _CACHE[which]


def kernel(**inputs):
    inp = {k: np.asarray(v) for k, v in inputs.items()}
    x = np.ascontiguousarray(inp["x"], dtype=np.float32)
    B = x.shape[0]
    f32 = lambda a: np.ascontiguousarray(a, dtype=np.float32)
    rmap = {"r_g": f32(inp["norm_g"][0:1]), "r_vecs": rwkv_host_vecs(inp), "r_w_in": f32(inp["rwkv_w_in"][0]),
            "r_w1": f32(inp["rwkv_w1"][0]), "r_a1": f32(inp["rwkv_a1"][0]), "r_w2": f32(inp["rwkv_w2"][0]),
            "r_a2": f32(inp["rwkv_a2"][0]), "r_w_out": f32(inp["rwkv_w_out"][0])}
    hp = nsa_host(inp)
    nmap = {"n_g": f32(inp["norm_g"][1:2]), "n_w_in": f32(inp["nsa_w_in"][0]), "n_w_out": f32(inp["nsa_w_out"][0]),
            "n_w1": f32(inp["nsa_cmp_w1"][0]), "n_gains": hp["gains"], "n_peT": hp["peT"], "n_b1": hp["b1"], "n_w2": hp["w2"]}
    cores = list(range(B))
    if FUSED:
        nc = _get("fused")
        res = run_bass_kernel_spmd(nc, [{"x": x[i], **rmap, **nmap} for i in cores], core_ids=cores)
        return np.stack([np.asarray(res.results[i]["xo"], dtype=np.float32) for i in cores], 0)
    nc = _get("rwkv")
    res = run_bass_kernel_spmd(nc, [{"x": x[i], **rmap} for i in cores], core_ids=cores)
    x1 = [np.ascontiguousarray(res.results[i]["xo"], dtype=np.float32) for i in cores]
    nc = _get("nsa")
    res = run_bass_kernel_spmd(nc, [{"x": x1[i], **nmap} for i in cores], core_ids=cores)
    return np.stack([np.asarray(res.results[i]["xo"], dtype=np.float32) for i in cores], 0)
```

```python
from contextlib import ExitStack
from concourse.bass_utils import run_bass_kernel_spmd
import numpy as np
import concourse.bass as bass
import concourse.mybir as mybir

F32 = mybir.dt.float32
BF16 = mybir.dt.bfloat16
ALU = mybir.AluOpType
AF = mybir.ActivationFunctionType
AX = mybir.AxisListType

ENGINES = ("tensor", "vector", "scalar", "gpsimd", "sync")
CH = 30000


class Prog:
    def __init__(self, nc, stack, same_engine_sync=True):
        self.nc = nc
        self.stack = stack
        self.ops = {e: [] for e in ENGINES}
        self.cnt = {e: 0 for e in ENGINES}
        self.sems = {}
        self.res_w = {}
        self.res_r = {}
        self.dma_cnt = {}
        self.seen = {e: {} for e in ENGINES}
        self.same_engine_sync = same_engine_sync
        self.nwaits = 0
        self.max_ops = 10**9
        self.nops = 0
        self.last_line = None

    def sem(self, key):
        if key not in self.sems:
            name = "s_" + "_".join(str(k) for k in (key if isinstance(key, tuple) else (key,)))
            self.sems[key] = self.stack.enter_context(self.nc.semaphore(name))
        return self.sems[key]

    def _deps(self, eng, reads, writes, pe_accum=False):
        waits = {}

        def need(dep):
            if dep is None:
                return
            semkey, val, deng = dep
            if deng == eng and semkey[0] == "c":
                if not self.same_engine_sync:
                    return
                if eng == "tensor" and pe_accum:
                    return
            if self.seen[eng].get(semkey, 0) >= val:
                return
            if waits.get(semkey, 0) < val:
                waits[semkey] = val

        for r in reads:
            need(self.res_w.get(r))
        for w in writes:
            need(self.res_w.get(w))
            for rd in self.res_r.get(w, ()):
                need(rd)
        for k, v in waits.items():
            self.seen[eng][k] = v
        self.nwaits += len(waits)
        return list(waits.items())

    def _record(self, dep, reads, writes):
        for r in reads:
            self.res_r.setdefault(r, []).append(dep)
        for w in writes:
            self.res_w[w] = dep
            self.res_r[w] = []

    def op(self, eng, fn, reads=(), writes=(), pe_accum=False):
        isps = lambda r: isinstance(r, tuple) and isinstance(r[0], str) and r[0].endswith("pb")
        writes = list(writes) + [r for r in reads if isps(r)]
        reads = [r for r in reads if not isps(r)]
        waits = self._deps(eng, reads, writes, pe_accum)
        i = self.cnt[eng]
        self.cnt[eng] += 1
        semkey = ("c", eng, i // CH)
        self.sem(semkey)
        for k, _ in waits:
            self.sem(k)
        dep = (semkey, i % CH + 1, eng)
        self.ops[eng].append((waits, fn, semkey, 1))
        self._record(dep, reads, writes)

    def dma(self, eng, semname, out, in_, reads=(), writes=(), **kw):
        waits = self._deps(eng, reads, writes)
        semkey = ("d", semname)
        self.sem(semkey)
        for k, _ in waits:
            self.sem(k)
        n = self.dma_cnt.get(semname, 0) + 1
        self.dma_cnt[semname] = n
        dep = (semkey, 16 * n, eng)
        self.ops[eng].append((waits, lambda e: e.dma_start(out=out, in_=in_, **kw), semkey, 16))
        self._record(dep, reads, writes)

    def final_wait(self, eng, resources):
        waits = self._deps(eng, resources, ())
        for k, _ in waits:
            self.sem(k)
        self.ops[eng].append((waits, None, None, 0))

    def emit(self):
        nc = self.nc
        with nc.Block() as block:
            def mk(engname):
                def body(e):
                    for waits, fn, semkey, inc in self.ops[engname]:
                        for k, v in waits:
                            e.wait_ge(self.sems[k], v)
                        if fn is not None:
                            try:
                                ins = getattr(e, fn[0])(*fn[1][0], **fn[1][1]) if isinstance(fn, tuple) else fn(e)
                            except Exception:
                                print("EMIT FAIL", engname, fn[0] if isinstance(fn, tuple) else fn, {k: (v if not hasattr(v, "shape") else ("AP", v.shape)) for k, v in fn[1][1].items()} if isinstance(fn, tuple) else "")
                                raise
                            ins.then_inc(self.sems[semkey], inc)
                return body
            block.tensor(mk("tensor"))
            block.vector(mk("vector"))
            block.scalar(mk("scalar"))
            block.gpsimd(mk("gpsimd"))
            block.sync(mk("sync"))


def C(*a, **k):
    return (a, k)


D = 1024
NV = 13
I_W0, I_A0, I_KK, I_KA, I_RK, I_LG, I_LB = 6, 7, 8, 9, 10, 11, 12
DEC = -float(np.exp(-0.5))


def rwkv_host_vecs(inp):
    rows = [inp["rwkv_mu"][0][i] for i in range(6)] + [inp[k][0].reshape(-1) for k in
            ["rwkv_w0", "rwkv_a0", "rwkv_k_k", "rwkv_k_a"]]
    rows.append(np.tile(inp["rwkv_r_k"][0].reshape(16, 64), 1).reshape(-1))
    rows += [inp["rwkv_lnx_g"][0], inp["rwkv_lnx_b"][0]]
    v = np.stack([np.asarray(r, np.float32).reshape(8, 128) for r in rows], 0)
    return np.ascontiguousarray(v.transpose(2, 0, 1))


def emit_rwkv(nc, P, st, x, xo, wd, T, pfx="r", limit=99):
    NB = T // 256
    sbn = [0]

    def sb(shape, dt, name=None):
        sbn[0] += 1
        return st.enter_context(nc.sbuf_tensor(f"{pfx}_{name or 't'}{sbn[0]}", shape, dt))

    pball = st.enter_context(nc.psum_tensor(f"{pfx}_psum", [128, 8, 512], F32))
    pb = [pball[:, i, :] for i in range(8)]
    PB = lambda i: (pfx + "pb", i)

    RN = {"t1": "sq", "rk": "sq", "Lp": "rn", "eLp": "kkr", "BtT": "kf", "KtT": "a"}
    cn = lambda l: [RN.get(x, x) if isinstance(x, str) else x for x in l]

    def V(fn, r=(), w=()): P.op("vector", fn, cn(r), cn(w))
    def G(fn, r=(), w=()): P.op("gpsimd", fn, cn(r), cn(w))
    def A(fn, r=(), w=()): P.op("scalar", fn, cn(r), cn(w))

    def mm(out, lhsT, rhs, start=True, stop=True, r=(), w=()):
        P.op("tensor", ("matmul", C(out=out, lhsT=lhsT, rhs=rhs, start=start, stop=stop)), cn(r), cn(w), pe_accum=not start)

    def tr(out, in_, ident, r=(), w=()):
        P.op("tensor", ("transpose", C(out=out, in_=in_, identity=ident)), cn(r), cn(w))

    ones = sb([128, 512], F32, "ones")
    ident = sb([128, 128], F32, "ident")
    ident4 = sb([128, 4, 128], F32, "ident4")
    triS4 = sb([128, 4, 128], F32, "triS4")
    triI4 = sb([128, 4, 128], F32, "triI4")
    triL4 = sb([128, 4, 128], F32, "triL4")
    BD = sb([128, 128], F32, "BD")
    m01 = sb([128, 256], F32, "m01")
    G(("memset", C(ones[:], 1.0)), w=["ones"])
    G(("affine_select", C(out=ident[:], in_=ones[:, 0:128], pattern=[[-1, 128]], compare_op=ALU.is_equal,
                                fill=0.0, base=0, channel_multiplier=1)), r=["ones"], w=["ident"])
    o4 = ones[:].rearrange("p (a b) -> p a b", a=4)
    G(("affine_select", C(out=ident4[:], in_=o4, pattern=[[0, 4], [-1, 128]], compare_op=ALU.is_equal,
                                fill=0.0, base=0, channel_multiplier=1)), r=["ones"], w=["ident4"])
    G(("affine_select", C(out=triS4[:], in_=o4, pattern=[[0, 4], [1, 128]], compare_op=ALU.is_gt,
                                fill=0.0, base=0, channel_multiplier=-1)), r=["ones"], w=["triS4"])
    G(("affine_select", C(out=triI4[:], in_=o4, pattern=[[0, 4], [1, 128]], compare_op=ALU.is_ge,
                                fill=0.0, base=0, channel_multiplier=-1)), r=["ones"], w=["triI4"])
    G(("affine_select", C(out=triL4[:], in_=o4, pattern=[[0, 4], [-1, 128]], compare_op=ALU.is_gt,
                                fill=0.0, base=0, channel_multiplier=1)), r=["ones"], w=["triL4"])
    G(("memset", C(BD[:], 0.0)), w=["BD"])
    G(("memset", C(BD[0:64, 0:64], 1.0)), w=["BD"])
    G(("memset", C(BD[64:128, 64:128], 1.0)), w=["BD"])
    G(("memset", C(m01[:], 1.0)), w=["m01"])
    G(("memset", C(m01[:, 0:1], 0.0)), w=["m01"])
    G(("memset", C(m01[:, 128:129], 0.0)), w=["m01"])

    vecs = sb([128, NV, 8], F32, "vecs")
    gb = sb([128, D], F32, "gb")
    wslot = [sb([128, 8, 4, 128], BF16, f"wslot{i}") for i in range(2)]
    wscr = nc.dram_tensor(pfx + "_wscr", [8, 128, 8, 4, 128], BF16, kind="Internal").ap()
    wscr_w = wscr.rearrange("h p d c f -> p h d c f")
    woutb = sb([128, 8, 1024], BF16, "woutb")
    w1b = sb([128, 8, 64], BF16, "w1b")
    a1b = sb([128, 8, 64], BF16, "a1b")
    w2b = sb([64, 1024], BF16, "w2b")
    a2b = sb([64, 1024], BF16, "a2b")
    stg = [sb([128, 1024], F32, f"stg{i}") for i in range(2)]
    stgb = [sb([128, 1024], BF16, f"stgb{i}") for i in range(2)]
    P.dma("sync", pfx + "vecs", vecs[:], wd["vecs"], writes=["vecs"])
    P.dma("sync", pfx + "gb", gb[:], wd["g"].partition_broadcast(128), writes=["gb"])
    nst = [0]
    WS_ALL = [("wscr", c, ci) for c in range(8) for ci in range(4)]

    def load_cast(dst_ap, src_ap, np_, ncols, wres):
        i = nst[0] % 2
        nst[0] += 1
        q = "sync" if i == 0 else "gpsimd"
        P.dma(q, pfx + f"stg{i}", stg[i][0:np_, 0:ncols], src_ap, writes=[("stg", i)])
        eng = "vector" if i == 0 else "gpsimd"
        P.op(eng, ("tensor_copy", C(out=dst_ap, in_=stg[i][0:np_, 0:ncols])), [("stg", i)], [wres])

    win_v = wd["w_in"].rearrange("(c p) f -> p c f", p=128)
    for c in range(8):
        for ci in range(4):
            i = nst[0] % 2
            load_cast(stgb[i][:], win_v[:, c, ci * 1024:(ci + 1) * 1024], 128, 1024, ("stgb", i))
            P.dma("sync" if i == 0 else "gpsimd", pfx + f"wscr{i}", wscr_w[:, :, c, ci, :],
                  stgb[i][:].rearrange("p (h f) -> p h f", h=8), reads=[("stgb", i)], writes=[("wscr", c, ci)])
    wout_v = wd["w_out"].rearrange("(c p) f -> p c f", p=128)
    for c in range(8):
        load_cast(woutb[:, c, :], wout_v[:, c, :], 128, 1024, "woutb")
    load_cast(w1b[:], wd["w1"].rearrange("(c p) f -> p c f", p=128), 128, 512, "w1b")
    load_cast(a1b[:], wd["a1"].rearrange("(c p) f -> p c f", p=128), 128, 512, "a1b")
    load_cast(w2b[:], wd["w2"], 64, 1024, "w2b")
    load_cast(a2b[:], wd["a2"], 64, 1024, "a2b")

    if limit <= 0:
        return []
    xt = [sb([128, D], F32, f"xt{i}") for i in range(2)]
    ss = sb([128, 1], F32, "ss")
    rs = sb([128, 1], F32, "rs")
    ht = sb([128, D], F32, "ht")
    hT = sb([128, 8, 257], F32, "hT")
    dh = [sb([128, 256], F32, f"dh{i}") for i in range(2)]
    xm = sb([128, 6, 8, 256], BF16, "xm")
    la = sb([64, 256], BF16, "la")
    lw = sb([64, 256], BF16, "lw")
    rh = sb([128, 8, 256], BF16, "rh")
    ah = sb([128, 8, 256], BF16, "ah")
    bh = sb([128, 8, 256], BF16, "bh")
    kh = sb([128, 8, 256], BF16, "kh")
    Vt = sb([128, 2, 1024], BF16, "Vt")
    Bt = sb([128, 2, 1024], BF16, "Bt")
    Kt = sb([128, 2, 1024], BF16, "Kt")
    sz = sb([128, 8, 256], BF16, "sz")
    bonus = sb([128, 8, 256], BF16, "bonus")
    gC = sb([128, 8, 2], F32, "gC")
    tmp = {n: sb([128, 256], F32, n) for n in
           ["kf", "kkr", "sq", "rn", "kk", "a", "kp", "bb", "sig", "L", "eL", "enL", "E2", "vf"]}
    for k_, v_ in RN.items():
        tmp[k_] = tmp[v_]
    ST = sb([128, 8, 64], F32, "ST")
    STb = sb([128, 8, 64], BF16, "STb")
    Q = [sb([128, 4, 128], BF16, f"Q{i}") for i in range(2)]
    QT = [sb([128, 4, 128], BF16, f"QT{i}") for i in range(2)]
    Z = sb([128, 4, 128], F32, "Z")
    Zb = sb([128, 4, 128], BF16, "Zb")
    QTf = sb([128, 4, 128], F32, "QTf")
    WT = sb([128, 16, 128], BF16, "WT")
    Mak = sb([128, 16, 128], BF16, "Mak")
    Mrb = sb([128, 16, 128], BF16, "Mrb")
    Mrk = sb([128, 16, 128], BF16, "Mrk")
    Xn = sb([128, 1024], BF16, "Xn")
    Ub = sb([128, 1024], BF16, "Ub")
    of = sb([128, 16, 64], F32, "of")
    mean = sb([128, 16], F32, "mean")
    ex2 = sb([128, 16], F32, "ex2")
    var = sb([128, 16], F32, "var")
    yT = sb([128, 8, 128], BF16, "yT")
    ytmp = sb([128, 128], F32, "ytmp")
    xr = xt[0]
    outt = sb([128, D], F32, "outt")
    osq = outt[:].rearrange("p (a b) -> p a b", a=16)

    G(("memset", C(ST[:], 0.0)), w=["ST"])
    G(("memset", C(STb[:], 0.0)), w=["STb"])
    G(("memset", C(hT[:, :, 0:1], 0.0)), w=["hT"])

    vcol = lambda i, hp: vecs[:, i, hp:hp + 1]
    eps = 1e-6

    for b in range(NB):
        t0 = b * 256
        for i in range(2):
            xb = xt[i]
            P.dma("sync", pfx + f"x{i}", xb[:], x[t0 + i * 128:t0 + (i + 1) * 128, :], writes=[("xt", i)])
            A(("activation", C(out=ht[:], in_=xb[:], func=AF.Square, accum_out=ss[:])), [("xt", i)], ["ht", "ss"])
            A(("activation", C(out=rs[:], in_=ss[:], func=AF.Sqrt, scale=1.0 / D, bias=eps)), ["ss"], ["rs"])
            V(("reciprocal", C(out=rs[:], in_=rs[:])), ["rs"], ["rs"])
            V(("scalar_tensor_tensor", C(out=ht[:], in0=xb[:], scalar=rs[:, 0:1], in1=gb[:], op0=ALU.mult, op1=ALU.mult)),
              [("xt", i), "rs", "gb"], ["ht"])
            for half in range(2):
                for c in range(4):
                    cc = half * 4 + c
                    tr(pb[half][:, c * 128:(c + 1) * 128], ht[:, cc * 128:(cc + 1) * 128], ident[:], ["ht", "ident"], [PB(half)])
                pv = pb[half].rearrange("p (c t) -> p c t", c=4)
                dst = hT[:, half * 4:(half + 1) * 4, 1 + i * 128:1 + (i + 1) * 128]
                if half == 0:
                    V(("tensor_copy", C(out=dst, in_=pv)), [PB(half)], ["hT"])
                else:
                    A(("copy", C(out=dst, in_=pv)), [PB(half)], ["hT"])
        if limit <= 1:
            return []
        n = 0
        for dc in range(8):
            dd = dh[dc % 2]
            V(("tensor_tensor", C(out=dd[:], in0=hT[:, dc, 0:256], in1=hT[:, dc, 1:257], op=ALU.subtract)),
              ["hT"], [("dh", dc % 2)])
            for c in range(6):
                fn = ("scalar_tensor_tensor", C(out=xm[:, c, dc, :], in0=dd[:], scalar=vcol(c, dc),
                                                                        in1=hT[:, dc, 1:257], op0=ALU.mult, op1=ALU.add))
                V(fn, [("dh", dc % 2), "hT", "vecs"], [("xm", c)])
                n += 1
        V(("tensor_copy", C(out=hT[:, :, 0:1], in_=hT[:, :, 256:257])), ["hT"], ["hT"])
        if limit <= 2:
            return []
        for dc in range(8):
            mm(pb[2][0:64, 0:256], a1b[:, dc, :], xm[:, 5, dc, :], dc == 0, dc == 7, [("xm", 5), "a1b"], [PB(2)])
        V(("tensor_copy", C(out=la[:], in_=pb[2][0:64, 0:256])), [PB(2)], ["la"])
        for dc in range(8):
            mm(pb[3][0:64, 0:256], w1b[:, dc, :], xm[:, 4, dc, :], dc == 0, dc == 7, [("xm", 4), "w1b"], [PB(3)])
        A(("activation", C(out=lw[:], in_=pb[3][0:64, 0:256], func=AF.Tanh)), [PB(3)], ["lw"])
        if limit <= 3:
            return []
        for hp in range(8):
            fs = slice(hp * 128, (hp + 1) * 128)
            n_it = b * 8 + hp
            if n_it == 0:
                P.dma("sync", pfx + "ws0", wslot[0][:], wscr[0], reads=WS_ALL, writes=[("wslot", 0)])
            if n_it + 1 < NB * 8:
                sl_ = (n_it + 1) % 2
                P.dma("sync", pfx + f"ws{sl_}", wslot[sl_][:], wscr[(hp + 1) % 8], reads=WS_ALL, writes=[("wslot", sl_)])
            wsl = wslot[n_it % 2]
            pR, pK, pV_, pZ = pb[0][:, 0:256], pb[0][:, 256:512], pb[1][:, 0:256], pb[1][:, 256:512]
            pA, pU = pb[2][:, 0:256], pb[2][:, 256:512]
            pN, pBS = pb[3][:, 0:256], pb[3][:, 256:512]
            for ci, (po, pbi) in enumerate([(pR, 0), (pK, 0), (pV_, 1), (pZ, 1)]):
                for dc in range(8):
                    mm(po, wsl[:, dc, ci, :], xm[:, ci, dc, :], dc == 0, dc == 7,
                       [("xm", ci), ("wslot", n_it % 2)], [PB(pbi)])
            mm(pA, a2b[0:64, fs], la[:], True, True, ["a2b", "la"], [PB(2)])
            mm(pU, w2b[0:64, fs], lw[:], True, True, ["w2b", "lw"], [PB(2)])
            t = tmp
            A(("copy", C(out=t["kf"][:], in_=pK)), [PB(0)], ["kf"])
            V(("tensor_scalar", C(out=t["kkr"][:], in0=pK, scalar1=vcol(I_KK, hp), scalar2=None, op0=ALU.mult)), [PB(0), "vecs"], ["kkr"])
            A(("activation", C(out=t["sq"][:], in_=t["kkr"][:], func=AF.Square)), ["kkr"], ["sq"])
            mm(pN, BD[:], t["sq"][:], True, True, ["BD", "sq"], [PB(3)])
            A(("activation", C(out=t["rn"][:], in_=pN, func=AF.Sqrt)), [PB(3)], ["rn"])
            V(("tensor_scalar", C(out=t["rn"][:], in0=t["rn"][:], scalar1=1e-12, scalar2=None, op0=ALU.max)), ["rn"], ["rn"])
            V(("reciprocal", C(out=t["rn"][:], in_=t["rn"][:])), ["rn"], ["rn"])
            V(("tensor_tensor", C(out=t["kk"][:], in0=t["kkr"][:], in1=t["rn"][:], op=ALU.mult)), ["kkr", "rn"], ["kk"])
            A(("activation", C(out=t["a"][:], in_=pA, func=AF.Sigmoid, bias=vcol(I_A0, hp))), [PB(2), "vecs"], ["a"])
            V(("tensor_scalar", C(out=t["t1"][:], in0=t["a"][:], scalar1=-1.0, scalar2=vcol(I_KA, hp), op0=ALU.add, op1=ALU.mult)),
              ["a", "vecs"], ["t1"])
            V(("scalar_tensor_tensor", C(out=t["kp"][:], in0=t["t1"][:], scalar=1.0, in1=t["kf"][:], op0=ALU.add, op1=ALU.mult)),
              ["t1", "kf"], ["kp"])
            G(("tensor_tensor", C(out=t["bb"][:], in0=t["kk"][:], in1=t["a"][:], op=ALU.mult)), ["kk", "a"], ["bb"])
            A(("activation", C(out=t["sig"][:], in_=pU, func=AF.Sigmoid, bias=vcol(I_W0, hp))), [PB(2), "vecs"], ["sig"])
            G(("tensor_scalar", C(out=t["sig"][:], in0=t["sig"][:], scalar1=DEC, scalar2=None, op0=ALU.mult)), ["sig"], ["sig"])
            V(("tensor_tensor_scan", C(out=t["L"][:], data0=m01[:], data1=t["sig"][:], initial=0.0, op0=ALU.mult, op1=ALU.add)),
              ["m01", "sig"], ["L"])
            G(("tensor_tensor", C(out=t["Lp"][:], in0=t["L"][:], in1=t["sig"][:], op=ALU.subtract)), ["L", "sig"], ["Lp"])
            A(("activation", C(out=t["eL"][:], in_=t["L"][:], func=AF.Exp)), ["L"], ["eL"])
            A(("activation", C(out=t["eLp"][:], in_=t["Lp"][:], func=AF.Exp)), ["Lp"], ["eLp"])
            A(("activation", C(out=t["enL"][:], in_=t["L"][:], func=AF.Exp, scale=-1.0)), ["L"], ["enL"])
            for j in range(2):
                cs = slice(j * 128, (j + 1) * 128)
                A(("activation", C(out=t["E2"][:, cs], in_=t["L"][:, cs], func=AF.Exp, scale=-1.0,
                                                    bias=t["L"][:, j * 128 + 127:j * 128 + 128])), ["L"], ["E2"])
            V(("tensor_tensor", C(out=rh[:, hp, :], in0=pR, in1=t["eL"][:], op=ALU.mult)), [PB(0), "eL"], [("rh", hp)])
            V(("scalar_tensor_tensor", C(out=t["rk"][:], in0=pR, scalar=vcol(I_RK, hp), in1=t["kp"][:], op0=ALU.mult, op1=ALU.mult)),
              [PB(0), "vecs", "kp"], ["rk"])
            mm(pBS, BD[:], t["rk"][:], True, True, ["BD", "rk"], [PB(3)])
            A(("copy", C(out=t["vf"][:], in_=pV_)), [PB(1)], ["vf"])
            V(("tensor_tensor", C(out=bonus[:, hp, :], in0=pBS, in1=t["vf"][:], op=ALU.mult)), [PB(3), "vf"], [("bonus", hp)])
            A(("activation", C(out=sz[:, hp, :], in_=pZ, func=AF.Silu)), [PB(1)], [("sz", hp)])
            G(("tensor_tensor", C(out=ah[:, hp, :], in0=t["kk"][:], in1=t["eLp"][:], op=ALU.mult)), ["kk", "eLp"], [("ah", hp)])
            G(("tensor_tensor", C(out=bh[:, hp, :], in0=t["bb"][:], in1=t["enL"][:], op=ALU.mult)), ["bb", "enL"], [("bh", hp)])
            V(("tensor_tensor", C(out=kh[:, hp, :], in0=t["kp"][:], in1=t["enL"][:], op=ALU.mult)), ["kp", "enL"], [("kh", hp)])
            G(("tensor_tensor", C(out=t["BtT"][:], in0=t["bb"][:], in1=t["E2"][:], op=ALU.mult)), ["bb", "E2"], ["BtT"])
            V(("tensor_tensor", C(out=t["KtT"][:], in0=t["kp"][:], in1=t["E2"][:], op=ALU.mult)), ["kp", "E2"], ["KtT"])
            V(("tensor_copy", C(out=gC[:, hp, :], in_=t["eL"][:, 127:256:128])), ["eL"], ["gC"])
            for j in range(2):
                cs = slice(j * 128, (j + 1) * 128)
                for si, (src, sres) in enumerate([(t["BtT"], "BtT"), (t["KtT"], "KtT"), (t["vf"], "vf")]):
                    tr(pb[4 + j][:, si * 128:(si + 1) * 128], src[:, cs], ident[:], [sres, "ident"], [PB(4 + j)])
                V(("tensor_copy", C(out=Bt[:, j, fs], in_=pb[4 + j][:, 0:128])), [PB(4 + j)], [("Bt", j)])
                A(("copy", C(out=Kt[:, j, fs], in_=pb[4 + j][:, 128:256])), [PB(4 + j)], [("Kt", j)])
                V(("tensor_copy", C(out=Vt[:, j, fs], in_=pb[4 + j][:, 256:384])), [PB(4 + j)], [("Vt", j)])
        if limit <= 4:
            continue
        for j in range(2):
            ts = slice(j * 128, (j + 1) * 128)
            for hg in range(4):
                for q in range(4):
                    h = hg * 4 + q
                    hp, hh = h // 2, h % 2
                    ps_ = slice(hh * 64, hh * 64 + 64)
                    qs = slice(q * 128, (q + 1) * 128)
                    a_, b_, k_, r_ = ah[ps_, hp, ts], bh[ps_, hp, ts], kh[ps_, hp, ts], rh[ps_, hp, ts]
                    mm(pb[0][:, qs], b_, a_, True, True, [("ah", hp), ("bh", hp)], [PB(0)])
                    mm(pb[1][:, qs], a_, b_, True, True, [("ah", hp), ("bh", hp)], [PB(1)])
                    mm(pb[2][:, qs], k_, a_, True, True, [("ah", hp), ("kh", hp)], [PB(2)])
                    mm(pb[3][:, qs], b_, r_, True, True, [("rh", hp), ("bh", hp)], [PB(3)])
                    mm(pb[4][:, qs], k_, r_, True, True, [("rh", hp), ("kh", hp)], [PB(4)])
                p4 = lambda i: pb[i].rearrange("p (a b) -> p a b", a=4)
                hs = slice(hg * 4, hg * 4 + 4)
                V(("scalar_tensor_tensor", C(out=QTf[:], in0=p4(0), scalar=-1.0, in1=triS4[:], op0=ALU.mult, op1=ALU.mult)),
                  [PB(0), "triS4"], ["QTf"])
                V(("scalar_tensor_tensor", C(out=Q[0][:], in0=p4(1), scalar=-1.0, in1=triL4[:], op0=ALU.mult, op1=ALU.mult)),
                  [PB(1), "triL4"], [("Q", 0)])
                A(("copy", C(out=QT[0][:], in_=QTf[:])), ["QTf"], [("QT", 0)])
                V(("tensor_tensor", C(out=Z[:], in0=QTf[:], in1=ident4[:], op=ALU.add)), ["QTf", "ident4"], ["Z"])
                A(("copy", C(out=Zb[:], in_=Z[:])), ["Z"], ["Zb"])
                V(("tensor_tensor", C(out=Mak[:, hs, :], in0=p4(2), in1=triS4[:], op=ALU.mult)), [PB(2), "triS4"], ["Mak"])
                V(("tensor_tensor", C(out=Mrb[:, hs, :], in0=p4(3), in1=triI4[:], op=ALU.mult)), [PB(3), "triI4"], ["Mrb"])
                V(("tensor_tensor", C(out=Mrk[:, hs, :], in0=p4(4), in1=triI4[:], op=ALU.mult)), [PB(4), "triI4"], ["Mrk"])
                cur = 0
                for lvl in range(6):
                    nxt = 1 - cur
                    for q in range(4):
                        qs = slice(q * 128, (q + 1) * 128)
                        mm(pb[5][:, qs], QT[cur][:, q, :], Q[cur][:, q, :], True, True, [("Q", cur), ("QT", cur)], [PB(5)])
                        mm(pb[6][:, qs], Q[cur][:, q, :], QT[cur][:, q, :], True, True, [("Q", cur), ("QT", cur)], [PB(6)])
                    V(("tensor_copy", C(out=Q[nxt][:], in_=p4(5))), [PB(5)], [("Q", nxt)])
                    A(("copy", C(out=QT[nxt][:], in_=p4(6))), [PB(6)], [("QT", nxt)])
                    for q in range(4):
                        qs = slice(q * 128, (q + 1) * 128)
                        mm(pb[7][:, qs], Q[nxt][:, q, :], Zb[:, q, :], True, True, [("Q", nxt), "Zb"], [PB(7)])
                    V(("tensor_tensor", C(out=Z[:], in0=p4(7), in1=Z[:], op=ALU.add)), [PB(7), "Z"], ["Z"])
                    if lvl < 5:
                        A(("copy", C(out=Zb[:], in_=Z[:])), ["Z"], ["Zb"])
                    else:
                        A(("copy", C(out=WT[:, hs, :], in_=Z[:])), ["Z"], ["WT"])
                    cur = nxt
            if limit <= 5:
                continue
            pX = pball[:, 0:2, :].rearrange("p a b -> p (a b)")
            pUu = pball[:, 2:4, :].rearrange("p a b -> p (a b)")
            pO = pball[:, 4:6, :].rearrange("p a b -> p (a b)")
            pS = pb[6]
            hd = lambda h: (h // 2, slice((h % 2) * 64, (h % 2) * 64 + 64), slice(h * 64, (h + 1) * 64))
            for h in range(16):
                hp, ps_, vs = hd(h)
                mm(pX[:, vs], ah[ps_, hp, ts], STb[ps_, hp, :], True, False, [("ah", hp), "STb"], [PB(h // 8)])
                mm(pX[:, vs], Mak[:, h, :], Vt[:, j, vs], False, True, ["Mak", ("Vt", j)], [PB(h // 8)])
            V(("tensor_scalar", C(out=Xn[:, 0:512], in0=pX[:, 0:512], scalar1=-1.0, scalar2=None, op0=ALU.mult)), [PB(0)], ["Xn"])
            A(("mul", C(out=Xn[:, 512:1024], in_=pX[:, 512:1024], mul=-1.0)), [PB(1)], ["Xn"])
            for h in range(16):
                hp, ps_, vs = hd(h)
                mm(pUu[:, vs], WT[:, h, :], Xn[:, vs], True, True, ["WT", "Xn"], [PB(2 + h // 8)])
            V(("tensor_copy", C(out=Ub[:, 0:512], in_=pUu[:, 0:512])), [PB(2)], ["Ub"])
            A(("copy", C(out=Ub[:, 512:1024], in_=pUu[:, 512:1024])), [PB(3)], ["Ub"])
            for h in range(16):
                hp, ps_, vs = hd(h)
                mm(pO[:, vs], rh[ps_, hp, ts], STb[ps_, hp, :], True, False, [("rh", hp), "STb"], [PB(4 + h // 8)])
                mm(pO[:, vs], Mrb[:, h, :], Ub[:, vs], False, False, ["Mrb", "Ub"], [PB(4 + h // 8)])
                mm(pO[:, vs], Mrk[:, h, :], Vt[:, j, vs], False, True, ["Mrk", ("Vt", j)], [PB(4 + h // 8)])
            for h in range(16):
                hp, ps_, vs = hd(h)
                mm(pS[ps_, hp * 64:(hp + 1) * 64], Bt[:, j, vs], Ub[:, vs], True, False, [("Bt", j), "Ub"], [PB(6)])
                mm(pS[ps_, hp * 64:(hp + 1) * 64], Kt[:, j, vs], Vt[:, j, vs], False, True, [("Kt", j), ("Vt", j)], [PB(6)])
            ofl = of[:].rearrange("p a b -> p (a b)")
            V(("tensor_copy", C(out=ofl[:, 0:512], in_=pO[:, 0:512])), [PB(4)], ["of"])
            A(("copy", C(out=ofl[:, 512:1024], in_=pO[:, 512:1024])), [PB(5)], ["of"])
            for hp in range(8):
                V(("scalar_tensor_tensor", C(out=ST[:, hp, :], in0=ST[:, hp, :], scalar=gC[:, hp, j:j + 1],
                                                         in1=pS[:, hp * 64:(hp + 1) * 64], op0=ALU.mult, op1=ALU.add)),
                  ["ST", "gC", PB(6)], ["ST"])
            A(("copy", C(out=STb[:], in_=ST[:])), ["ST"], ["STb"])
            if limit <= 6:
                continue
            A(("activation", C(out=osq, in_=of[:], func=AF.Square)), ["of"], ["outt"])
            V(("tensor_reduce", C(out=mean[:], in_=of[:], axis=AX.X, op=ALU.add)), ["of"], ["mean"])
            V(("tensor_reduce", C(out=ex2[:], in_=osq, axis=AX.X, op=ALU.add)), ["outt"], ["ex2"])
            V(("tensor_scalar", C(out=mean[:], in0=mean[:], scalar1=1.0 / 64, scalar2=None, op0=ALU.mult)), ["mean"], ["mean"])
            V(("tensor_tensor", C(out=var[:], in0=mean[:], in1=mean[:], op=ALU.mult)), ["mean"], ["var"])
            V(("scalar_tensor_tensor", C(out=var[:], in0=ex2[:], scalar=1.0 / 64, in1=var[:], op0=ALU.mult, op1=ALU.subtract)),
              ["ex2", "var"], ["var"])
            A(("activation", C(out=var[:], in_=var[:], func=AF.Sqrt, bias=64e-5)), ["var"], ["var"])
            V(("reciprocal", C(out=var[:], in_=var[:])), ["var"], ["var"])
            for h in range(16):
                eng = V if h % 2 == 0 else G
                eng(("tensor_scalar", C(out=of[:, h, :], in0=of[:, h, :], scalar1=mean[:, h:h + 1], scalar2=var[:, h:h + 1],
                                                   op0=ALU.subtract, op1=ALU.mult)), ["of", "mean", "var"], ["of"])
            for hp in range(8):
                tr(pb[hp // 4][:, (hp % 4) * 128:(hp % 4 + 1) * 128], ofl[:, hp * 128:(hp + 1) * 128], ident[:], ["of", "ident"], [PB(hp // 4)])
            for hp in range(8):
                src = pb[hp // 4][:, (hp % 4) * 128:(hp % 4 + 1) * 128]
                V(("scalar_tensor_tensor", C(out=ytmp[:], in0=src, scalar=vcol(I_LG, hp), in1=bonus[:, hp, ts],
                                                                 op0=ALU.mult, op1=ALU.add)), [PB(hp // 4), "vecs", ("bonus", hp)], ["ytmp"])
                V(("scalar_tensor_tensor", C(out=yT[:, hp, :], in0=ytmp[:], scalar=vcol(I_LB, hp), in1=sz[:, hp, ts],
                                                         op0=ALU.add, op1=ALU.mult)), ["ytmp", "vecs", ("sz", hp)], ["yT"])
            P.dma("gpsimd", pfx + "xr", xr[:], x[t0 + j * 128:t0 + (j + 1) * 128, :], writes=[("xt", 0)])
            for hf in range(2):
                for dc in range(8):
                    mm(pb[2 + hf], yT[:, dc, :], woutb[:, dc, hf * 512:(hf + 1) * 512], dc == 0, dc == 7, ["yT", "woutb"], [PB(2 + hf)])
                V(("tensor_tensor", C(out=outt[:, hf * 512:(hf + 1) * 512], in0=pb[2 + hf], in1=xr[:, hf * 512:(hf + 1) * 512],
                                                  op=ALU.add)), [PB(2 + hf), ("xt", 0)], ["outt"])
            P.dma("sync", pfx + "xo", xo[t0 + j * 128:t0 + (j + 1) * 128, :], outt[:], reads=["outt"], writes=[(pfx + "xo", b * 2 + j)])
    return [(pfx + "xo", i) for i in range(NB * 2)]


NEG = -30000.0
GELU_C = 1.5957691216057308


def nsa_host(inp):
    qg = inp["nsa_q_gain"][0]
    kg = inp["nsa_k_gain"][0]
    gains = np.stack([np.tile(qg, 2), np.tile(kg[0], 2), np.tile(kg[1], 2), np.tile(kg[2], 2)], 1).astype(np.float32)
    pe = inp["nsa_cmp_pe"][0]
    peT = np.concatenate([pe[0].T, pe[1].T], 0).astype(np.float32)
    b1 = inp["nsa_cmp_b1"][0].reshape(2, 2, 128).transpose(2, 0, 1).astype(np.float32)
    w2 = inp["nsa_cmp_w2"][0].reshape(2, 2, 128, 64).transpose(2, 1, 0, 3).astype(np.float32)
    return dict(gains=np.ascontiguousarray(gains), peT=np.ascontiguousarray(peT), b1=np.ascontiguousarray(b1),
                w2=np.ascontiguousarray(w2.reshape(128, 256)))


def emit_nsa(nc, P, st, x, xo, wd, T, pfx="n", limit=99):
    D = 1024
    NB = T // 256
    NT = T // 128
    sbn = [0]

    def sb(shape, dt, name=None):
        sbn[0] += 1
        return st.enter_context(nc.sbuf_tensor(f"{pfx}_{name or 't'}{sbn[0]}", shape, dt))

    pball = st.enter_context(nc.psum_tensor(f"{pfx}_psum", [128, 8, 512], F32))
    pb = [pball[:, i, :] for i in range(8)]
    PB = lambda i: (pfx + "pb", i)

    def V(fn, r=(), w=()): P.op("vector", fn, r, w)
    def G(fn, r=(), w=()): P.op("gpsimd", fn, r, w)
    def A(fn, r=(), w=()): P.op("scalar", fn, r, w)

    def mm(out, lhsT, rhs, start=True, stop=True, r=(), w=()):
        P.op("tensor", ("matmul", C(out=out, lhsT=lhsT, rhs=rhs, start=start, stop=stop)), r, w, pe_accum=not start)

    def tr(out, in_, ident, r=(), w=()):
        P.op("tensor", ("transpose", C(out=out, in_=in_, identity=ident)), r, w)

    ones = sb([128, 512], F32, "ones")
    onesb = sb([64, 2048], BF16, "onesb")
    zerob = sb([128, 4, 128], BF16, "zerob")
    ident = sb([128, 128], F32, "ident")
    identb = sb([128, 128], BF16, "identb")
    BD = sb([128, 128], F32, "BD")
    CB = sb([128, 4, 128], BF16, "CB")
    AB = sb([128, 4, 128], BF16, "AB")
    E = sb([128, 32, 128], BF16, "E")
    ov = sb([128, 2, 64], BF16, "ov")
    ovf = sb([128, 2, 64], F32, "ovf")
    G(("memset", C(ones[:], 1.0)), w=["ones"])
    G(("memset", C(onesb[:], 1.0)), w=["onesb"])
    G(("memset", C(zerob[:], 0.0)), w=["zerob"])
    G(("affine_select", C(out=ident[:], in_=ones[:, 0:128], pattern=[[-1, 128]], compare_op=ALU.is_equal, fill=0.0, base=0,
                          channel_multiplier=1)), ["ones"], ["ident"])
    G(("tensor_copy", C(out=identb[:], in_=ident[:])), ["ident"], ["identb"])
    G(("memset", C(BD[:], 0.0)), w=["BD"])
    G(("memset", C(BD[0:64, 0:64], 1.0)), w=["BD"])
    G(("memset", C(BD[64:128, 64:128], 1.0)), w=["BD"])
    G(("affine_select", C(out=CB[:], in_=zerob[:], pattern=[[0, 4], [1, 128]], compare_op=ALU.is_ge, fill=NEG, base=0,
                          channel_multiplier=-1)), ["zerob"], ["CB"])
    G(("affine_select", C(out=AB[:], in_=zerob[:], pattern=[[0, 4], [-1, 128]], compare_op=ALU.is_gt, fill=NEG, base=0,
                          channel_multiplier=1)), ["zerob"], ["AB"])
    ob3 = onesb[:].rearrange("p (a b) -> p a b", a=32)
    G(("memset", C(E[:], 0.0)), w=["E"])
    G(("affine_select", C(out=E[0:64, :, 0:64], in_=ob3, pattern=[[-2, 32], [0, 64]], compare_op=ALU.is_equal, fill=0.0, base=0,
                          channel_multiplier=1)), ["onesb"], ["E"])
    G(("affine_select", C(out=E[0:64, :, 64:128], in_=ob3, pattern=[[-2, 32], [0, 64]], compare_op=ALU.is_equal, fill=0.0, base=-1,
                          channel_multiplier=1)), ["onesb"], ["E"])
    for c in range(2):
        G(("affine_select", C(out=ovf[:, c, :], in_=ones[:, 0:64], pattern=[[-64, 64]], compare_op=ALU.is_ge, fill=0.0,
                              base=2048 * c + 31, channel_multiplier=16)), ["ones"], ["ovf"])
        G(("affine_select", C(out=ovf[:, c, :], in_=ovf[:, c, :], pattern=[[64, 64]], compare_op=ALU.is_ge, fill=0.0,
                              base=63 - 2048 * c, channel_multiplier=-16)), ["ovf"], ["ovf"])
    G(("tensor_copy", C(out=ov[:], in_=ovf[:])), ["ovf"], ["ov"])

    gb = sb([128, D], F32, "gb")
    gains = sb([128, 4], F32, "gains")
    peT = sb([128, 32], F32, "peT")
    peTb = sb([128, 32], BF16, "peTb")
    b1 = sb([128, 2, 2], F32, "b1")
    cb = sb([128, 2, 2], F32, "cb")
    w2f = sb([128, 256], F32, "w2f")
    w2b = sb([128, 2, 2, 64], BF16, "w2b")
    w1b = sb([128, 32, 256], BF16, "w1b")
    woutb = sb([128, 8, 1024], BF16, "woutb")
    stg = [sb([128, 1024], F32, f"stg{i}") for i in range(2)]
    stgb = [sb([128, 1024], BF16, f"stgb{i}") for i in range(2)]
    wslot = [sb([128, 8, 128], BF16, f"wslot{i}") for i in range(2)]
    NCH = 29
    wscr = nc.dram_tensor(pfx + "_wscr", [NCH, 128, 8, 128], BF16, kind="Internal").ap()
    P.dma("sync", pfx + "gb", gb[:], wd["g"].partition_broadcast(128), writes=["gb"])
    P.dma("sync", pfx + "gains", gains[:], wd["gains"], writes=["gains"])
    P.dma("sync", pfx + "peT", peT[:], wd["peT"], writes=["peT"])
    P.dma("sync", pfx + "b1", b1[:].rearrange("p a b -> p (a b)"), wd["b1"].rearrange("p a b -> p (a b)"), writes=["b1"])
    P.dma("sync", pfx + "w2f", w2f[:], wd["w2"], writes=["w2f"])
    V(("tensor_copy", C(out=w2b[:].rearrange("p a b c -> p (a b c)"), in_=w2f[:])), ["w2f"], ["w2b"])
    V(("tensor_copy", C(out=peTb[:], in_=peT[:])), ["peT"], ["peTb"])
    V(("tensor_scalar", C(out=gains[:, 0:1], in0=gains[:, 0:1], scalar1=0.125, scalar2=None, op0=ALU.mult)), ["gains"], ["gains"])
    nst = [0]

    def load_cast(dst_ap, src_ap, np_, ncols, wres, p0=0):
        i = nst[0] % 2
        nst[0] += 1
        q = "sync" if i == 0 else "gpsimd"
        P.dma(q, pfx + f"stg{i}", stg[i][p0:p0 + np_, 0:ncols], src_ap, writes=[("stg", i)])
        eng = "vector" if i == 0 else "gpsimd"
        P.op(eng, ("tensor_copy", C(out=dst_ap, in_=stg[i][p0:p0 + np_, 0:ncols])), [("stg", i)], [wres])
        return i

    CQ, CKS, CKW, CZ, CCV, CVS, CVW, CG = 0, 8, 10, 12, 20, 24, 26, 28
    win_v = wd["w_in"].rearrange("(c p) f -> p c f", p=128)
    WS_ALL = []

    def scr_write(i, dst, src, key):
        P.dma("sync" if i == 0 else "gpsimd", pfx + f"wscr{i}", dst, src, reads=[("stgb", i)], writes=[key])
        WS_ALL.append(key)

    for dc in range(8):
        i = nst[0] % 2
        load_cast(stgb[i][:], win_v[:, dc, 0:1024], 128, 1024, ("stgb", i))
        srcv = stgb[i][:].rearrange("p (m e j n) -> p m e j n", m=2, e=2, j=4)
        for m in range(2):
            for e in range(2):
                dst = wscr[CQ + m * 4:CQ + m * 4 + 4, :, dc, e * 64:(e + 1) * 64].rearrange("j p n -> p j n")
                scr_write(i, dst, srcv[:, m, e, :, :], ("wscr", "q", dc, m, e))
        i = nst[0] % 2
        load_cast(stgb[i][:], win_v[:, dc, 1024:2048], 128, 1024, ("stgb", i))
        s4 = stgb[i][:].rearrange("p (a g n) -> p a g n", a=4, g=4)
        for a_ in range(2):
            dst = wscr[CCV:CCV + 4, :, dc, a_ * 64:(a_ + 1) * 64].rearrange("g p n -> p g n")
            scr_write(i, dst, s4[:, a_, :, :], ("wscr", "cv", dc, a_))
        s2 = stgb[i][:].rearrange("p (a c f) -> p a c f", a=4, c=2)
        scr_write(i, wscr[CKS:CKS + 2, :, dc, :].rearrange("c p f -> p c f"), s2[:, 2, :, :], ("wscr", "ks", dc))
        scr_write(i, wscr[CVS:CVS + 2, :, dc, :].rearrange("c p f -> p c f"), s2[:, 3, :, :], ("wscr", "vs", dc))
        i = nst[0] % 2
        load_cast(stgb[i][:], win_v[:, dc, 2048:3072], 128, 1024, ("stgb", i))
        s8 = stgb[i][:].rearrange("p (c f) -> p c f", c=8)
        scr_write(i, wscr[CKW:CKW + 2, :, dc, :].rearrange("c p f -> p c f"), s8[:, 0:2, :], ("wscr", "kw", dc))
        scr_write(i, wscr[CVW:CVW + 2, :, dc, :].rearrange("c p f -> p c f"), s8[:, 2:4, :], ("wscr", "vw", dc))
        scr_write(i, wscr[CZ:CZ + 4, :, dc, :].rearrange("c p f -> p c f"), s8[:, 4:8, :], ("wscr", "z0", dc))
        i = nst[0] % 2
        load_cast(stgb[i][:, 0:560], win_v[:, dc, 3072:3632], 128, 560, ("stgb", i))
        s5 = stgb[i][:, 0:512].rearrange("p (c f) -> p c f", c=4)
        scr_write(i, wscr[CZ + 4:CZ + 8, :, dc, :].rearrange("c p f -> p c f"), s5, ("wscr", "z1", dc))
        scr_write(i, wscr[CG, :, dc, 0:48], stgb[i][:, 512:560], ("wscr", "g", dc))
    wout_v = wd["w_out"].rearrange("(c p) f -> p c f", p=128)
    for c in range(8):
        load_cast(woutb[:, c, :], wout_v[:, c, :], 128, 1024, "woutb")
    for kv in range(2):
        w1v = wd["w1"][kv].rearrange("(l d) f -> d l f", d=64)
        for l4 in range(0, 32, 4):
            i = nst[0] % 2
            P.dma("sync" if i == 0 else "gpsimd", pfx + f"stg{i}", stg[i][kv * 64:(kv + 1) * 64, :].rearrange("p (l f) -> p l f", l=4),
                  w1v[:, l4:l4 + 4, :], writes=[("stg", i)])
            nst[0] += 1
            P.op("vector" if i == 0 else "gpsimd",
                 ("tensor_copy", C(out=w1b[kv * 64:(kv + 1) * 64, l4:l4 + 4, :].rearrange("p l f -> p (l f)"),
                                   in_=stg[i][kv * 64:(kv + 1) * 64, :])), [("stg", i)], ["w1b"])
    for kv in range(2):
        ps_ = slice(kv * 64, (kv + 1) * 64)
        for fc in range(2):
            for l in range(32):
                mm(pb[0][:, (kv * 2 + fc):(kv * 2 + fc) + 1], w1b[ps_, l, fc * 128:(fc + 1) * 128], peTb[ps_, l:l + 1], l == 0, l == 31,
                   ["w1b", "peTb"], [PB(0)])
    V(("tensor_tensor", C(out=cb[:].rearrange("p a b -> p (a b)"), in0=pb[0][:, 0:4], in1=b1[:].rearrange("p a b -> p (a b)"), op=ALU.add)),
      [PB(0), "b1"], ["cb"])
    if limit <= 0:
        return []

    ksT = sb([128, 2, T], BF16, "ksT")
    kwT = sb([128, 2, T], BF16, "kwT")
    vs_tok = sb([128, NT, 4, 65], BF16, "vs_tok")
    vw_tok = sb([128, NT, 4, 65], BF16, "vw_tok")
    kcT = sb([128, 2, 256], BF16, "kcT")
    vcT = sb([128, 2, 256], F32, "vcT")
    vc_tok = sb([128, 2, 4, 65], BF16, "vc_tok")
    G(("memset", C(vs_tok[:], 1.0)), w=["vtok"])
    G(("memset", C(vw_tok[:], 1.0)), w=["vtok"])
    G(("memset", C(vc_tok[:], 1.0)), w=["vc_tok"])
    G(("memset", C(kcT[:], 0.0)), w=["kcT"])
    G(("memset", C(vcT[:], 0.0)), w=["vcT"])
    xt = [sb([128, D], F32, f"xt{i}") for i in range(2)]
    ss = sb([128, 1], F32, "ss")
    rs = sb([128, 1], F32, "rs")
    ht = sb([128, D], F32, "ht")
    hT = sb([128, 8, 256], BF16, "hT")
    qT = sb([128, 2, 4, 256], BF16, "qT")
    szT = sb([128, 8, 256], BF16, "szT")
    craw = sb([128, 4, 272], BF16, "craw")
    gsb = sb([128, 2, 48], F32, "gsb")
    sq = sb([128, 256], F32, "sq")
    rstd = sb([128, 256], F32, "rstd")
    xs = sb([128, 256], F32, "xs")
    g1 = sb([128, 256], F32, "g1")
    hid = sb([128, 4, 64], BF16, "hid")
    kcv = sb([128, 2, 2, 16], F32, "kcv")
    Pc = sb([128, 2, 4, 128], BF16, "Pc")
    Ps = [sb([128, 4, 128], BF16, f"Ps{i}") for i in range(2)]
    cmpb = sb([128, 2, 4, 128], BF16, "cmpb")
    Aadd = sb([128, 64], F32, "Aadd")
    A0 = sb([128, 64], F32, "A0")
    imp = sb([128, 64], F32, "imp")
    imp2 = sb([128, 64], F32, "imp2")
    mx = sb([128, 16], F32, "mx")
    selm = sb([128, 64], F32, "selm")
    selmT = [sb([128, 4, 128], BF16, f"selmT{i_}") for i_ in range(2)]
    rc = sb([128, 4], F32, "rc")
    cf = sb([128, 4], F32, "cf")
    oacc = sb([128, 16, 64], F32, "oacc")
    yT = sb([128, 8, 128], BF16, "yT")
    outt = sb([128, D], F32, "outt")
    G(("memset", C(craw[:], 0.0)), w=["craw"])
    for i_ in range(2):
        G(("memset", C(selmT[i_][:], 0.0)), w=[("selmT", i_)])
    pvctr = [0]
    jctr = [0]

    def pvbank():
        k = 2 + (pvctr[0] % 3)
        pvctr[0] += 1
        return k
    G(("memset", C(A0[:], 0.0)), w=["A0"])
    G(("memset", C(A0[:, 0:1], 10000.0)), w=["A0"])
    D0 = sb([128, 64], F32, "D0")
    Dd = sb([128, 64], F32, "Dd")
    At = sb([128, 64], F32, "At")
    G(("iota", C(D0[:], pattern=[[-1, 64]], base=0, channel_multiplier=0, allow_small_or_imprecise_dtypes=True)), w=["D0"])
    G(("tensor_scalar", C(out=D0[64:128, :], in0=D0[64:128, :], scalar1=1.0, scalar2=None, op0=ALU.add)), ["D0"], ["D0"])
    eps = 1e-6
    wcnt = [0]

    def wload(ch):
        s_ = wcnt[0] % 2
        wcnt[0] += 1
        if ch == CG:
            P.dma("sync", pfx + f"ws{s_}", wslot[s_][:, :, 0:48], wscr[ch][:, :, 0:48], reads=WS_ALL, writes=[("wslot", s_)])
        else:
            P.dma("sync", pfx + f"ws{s_}", wslot[s_][:], wscr[ch], reads=WS_ALL, writes=[("wslot", s_)])
        return s_

    def rmsnorm_evac(ps_ap, gcol, dst_ap, pbi, wres, ncols=256):
        A(("activation", C(out=sq[:, 0:ncols], in_=ps_ap, func=AF.Square)), [PB(pbi)], ["sq"])
        mm(pb[7][:, 0:ncols], BD[:], sq[:, 0:ncols], True, True, ["BD", "sq"], [PB(7)])
        A(("activation", C(out=rstd[:, 0:ncols], in_=pb[7][:, 0:ncols], func=AF.Sqrt, scale=1.0 / 64, bias=eps)), [PB(7)], ["rstd"])
        V(("reciprocal", C(out=rstd[:, 0:ncols], in_=rstd[:, 0:ncols])), ["rstd"], ["rstd"])
        V(("scalar_tensor_tensor", C(out=dst_ap, in0=ps_ap, scalar=gains[:, gcol:gcol + 1], in1=rstd[:, 0:ncols], op0=ALU.mult, op1=ALU.mult)),
          [PB(pbi), "gains", "rstd"], [wres])

    outs = []
    for b in range(NB):
        t0 = b * 256
        for i in range(2):
            xb = xt[i]
            P.dma("sync", pfx + f"x{i}", xb[:], x[t0 + i * 128:t0 + (i + 1) * 128, :], writes=[("xt", i)])
            A(("activation", C(out=ht[:], in_=xb[:], func=AF.Square, accum_out=ss[:])), [("xt", i)], ["ht", "ss"])
            A(("activation", C(out=rs[:], in_=ss[:], func=AF.Sqrt, scale=1.0 / D, bias=eps)), ["ss"], ["rs"])
            V(("reciprocal", C(out=rs[:], in_=rs[:])), ["rs"], ["rs"])
            V(("scalar_tensor_tensor", C(out=ht[:], in0=xb[:], scalar=rs[:, 0:1], in1=gb[:], op0=ALU.mult, op1=ALU.mult)),
              [("xt", i), "rs", "gb"], ["ht"])
            for half in range(2):
                for c in range(4):
                    cc = half * 4 + c
                    tr(pb[half][:, c * 128:(c + 1) * 128], ht[:, cc * 128:(cc + 1) * 128], ident[:], ["ht", "ident"], [PB(half)])
                pv = pb[half].rearrange("p (c t) -> p c t", c=4)
                dst = hT[:, half * 4:(half + 1) * 4, i * 128:(i + 1) * 128]
                if half == 0:
                    V(("tensor_copy", C(out=dst, in_=pv)), [PB(half)], ["hT"])
                else:
                    A(("copy", C(out=dst, in_=pv)), [PB(half)], ["hT"])
        if limit <= 1:
            return []
        def proj_fm(ch, pbi, M=128):
            s_ = wload(ch)
            for dc in range(8):
                mm(pb[pbi][0:M, 0:256], wslot[s_][:, dc, 0:M], hT[:, dc, :], dc == 0, dc == 7, [("wslot", s_), "hT"], [PB(pbi)])
        for m in range(2):
            for j in range(4):
                pbi = (m * 4 + j) % 2
                proj_fm(CQ + m * 4 + j, pbi)
                rmsnorm_evac(pb[pbi][:, 0:256], 0, qT[:, m, j, :], pbi, "qT")
        for m in range(2):
            proj_fm(CKS + m, m)
            rmsnorm_evac(pb[m][:, 0:256], 2, ksT[:, m, t0:t0 + 256], m, "kT")
        for m in range(2):
            proj_fm(CKW + m, m)
            rmsnorm_evac(pb[m][:, 0:256], 3, kwT[:, m, t0:t0 + 256], m, "kT")
        for c in range(8):
            proj_fm(CZ + c, c % 2)
            A(("activation", C(out=szT[:, c, :], in_=pb[c % 2][:, 0:256], func=AF.Silu)), [PB(c % 2)], ["szT"])
        for g in range(4):
            proj_fm(CCV + g, g % 2)
            A(("copy", C(out=craw[:, g, 16:272], in_=pb[g % 2][:, 0:256])), [PB(g % 2)], ["craw"])
        for (ch0, vtok) in ((CVS, vs_tok), (CVW, vw_tok)):
            for c2 in range(2):
                s_ = wload(ch0 + c2)
                for i in range(2):
                    for dc in range(8):
                        mm(pb[i][:, 0:128], hT[:, dc, i * 128:(i + 1) * 128], wslot[s_][:, dc, :], dc == 0, dc == 7,
                           [("wslot", s_), "hT"], [PB(i)])
                    V(("tensor_copy", C(out=vtok[:, 2 * b + i, 2 * c2:2 * c2 + 2, 0:64],
                                        in_=pb[i][:, 0:128].rearrange("p (g n) -> p g n", g=2))), [PB(i)], ["vtok"])
        s_ = wload(CG)
        for i in range(2):
            for dc in range(8):
                mm(pb[i][:, 0:48], hT[:, dc, i * 128:(i + 1) * 128], wslot[s_][:, dc, 0:48], dc == 0, dc == 7, [("wslot", s_), "hT"], [PB(i)])
            A(("activation", C(out=gsb[:, i, :], in_=pb[i][:, 0:48], func=AF.Sigmoid)), [PB(i)], ["gsb"])
        if limit <= 2:
            return []
        i0 = 1 if b == 0 else 0
        ni = 16 - i0
        n0 = 16 * b - 1 + i0
        ph = pb[2]
        for kv in range(2):
            ps_ = slice(kv * 64, (kv + 1) * 64)
            for fc in range(2):
                reg = (kv * 2 + fc) * 64
                for l in range(32):
                    rhs = craw[ps_, :, l + 16 * i0:l + 16 * i0 + 16 * (ni - 1) + 1:16]
                    outp = ph[:, reg:reg + 64].rearrange("p (g i) -> p g i", g=4)[:, :, 0:ni]
                    mm(outp, w1b[ps_, l, fc * 128:(fc + 1) * 128], rhs, l == 0, l == 31, ["w1b", "craw"], [PB(2)])
        for kv in range(2):
            for fc in range(2):
                reg = (kv * 2 + fc) * 64
                A(("activation", C(out=xs[:, reg:reg + 64], in_=ph[:, reg:reg + 64], func=AF.Identity, bias=cb[:, kv, fc:fc + 1])),
                  [PB(2), "cb"], ["xs"])
        V(("tensor_tensor", C(out=g1[:], in0=xs[:], in1=xs[:], op=ALU.mult)), ["xs"], ["g1"])
        V(("tensor_scalar", C(out=g1[:], in0=g1[:], scalar1=0.044715, scalar2=1.0, op0=ALU.mult, op1=ALU.add)), ["g1"], ["g1"])
        V(("tensor_tensor", C(out=g1[:], in0=g1[:], in1=xs[:], op=ALU.mult)), ["g1", "xs"], ["g1"])
        A(("activation", C(out=g1[:], in_=g1[:], func=AF.Sigmoid, scale=GELU_C)), ["g1"], ["g1"])
        V(("tensor_tensor", C(out=hid[:].rearrange("p a b -> p (a b)"), in0=g1[:], in1=xs[:], op=ALU.mult)), ["g1", "xs"], ["hid"])
        pk = pb[3][:, 0:64].rearrange("p (kv m i) -> p kv m i", kv=2, m=2)
        for kv in range(2):
            for g in range(4):
                m_, e_ = g // 2, g % 2
                for fc in range(2):
                    mm(pk[e_ * 64:(e_ + 1) * 64, kv, m_, 0:ni], w2b[:, fc, kv, :], hid[:, kv * 2 + fc, g * 16:g * 16 + ni], fc == 0, fc == 1,
                       ["w2b", "hid"], [PB(3)])
        V(("tensor_copy", C(out=kcv[:, :, :, 0:ni], in_=pk[:, :, :, 0:ni])), [PB(3)], ["kcv"])
        for m_ in range(2):
            G(("tensor_copy", C(out=vcT[:, m_, n0:n0 + ni], in_=kcv[:, 1, m_, 0:ni])), ["kcv"], ["vcT"])
        kflat = kcv[:, 0, :, :].rearrange("p m i -> p (m i)")
        A(("activation", C(out=sq[:, 0:32], in_=kflat, func=AF.Square)), ["kcv"], ["sq"])
        mm(pb[7][:, 0:32], BD[:], sq[:, 0:32], True, True, ["BD", "sq"], [PB(7)])
        A(("activation", C(out=rstd[:, 0:32], in_=pb[7][:, 0:32], func=AF.Sqrt, scale=1.0 / 64, bias=eps)), [PB(7)], ["rstd"])
        V(("reciprocal", C(out=rstd[:, 0:32], in_=rstd[:, 0:32])), ["rstd"], ["rstd"])
        V(("scalar_tensor_tensor", C(out=sq[:, 0:32], in0=kflat, scalar=gains[:, 1:2], in1=rstd[:, 0:32], op0=ALU.mult, op1=ALU.mult)),
          ["kcv", "gains", "rstd"], ["sq"])
        for m_ in range(2):
            G(("tensor_copy", C(out=kcT[:, m_, n0:n0 + ni], in_=sq[:, m_ * 16:m_ * 16 + ni])), ["sq"], ["kcT"])
        for c in range(2):
            for m_ in range(2):
                tr(pb[4][:, (c * 2 + m_) * 128:(c * 2 + m_ + 1) * 128], vcT[:, m_, c * 128:(c + 1) * 128], ident[:], ["vcT", "ident"], [PB(4)])
        V(("tensor_copy", C(out=vc_tok[:, :, :, 0:64], in_=pb[4].rearrange("p (c g n) -> p c g n", c=2, g=4))), [PB(4)], ["vc_tok"])
        G(("tensor_copy", C(out=craw[:, :, 0:16], in_=craw[:, :, 256:272])), ["craw"], ["craw"])
        if limit <= 3:
            return []
        for il in range(2):
            i = 2 * b + il
            tl = slice(il * 128, (il + 1) * 128)
            V(("tensor_scalar", C(out=Dd[:], in0=D0[:], scalar1=float(2 * i), scalar2=None, op0=ALU.add)), ["D0"], ["Dd"])
            V(("tensor_scalar", C(out=Aadd[:], in0=Dd[:], scalar1=0.0, scalar2=None, op0=ALU.is_ge)), ["Dd"], ["Aadd"])
            V(("tensor_scalar", C(out=At[:], in0=Dd[:], scalar1=1.0, scalar2=10000.0, op0=ALU.is_le, op1=ALU.mult)), ["Dd"], ["At"])
            V(("tensor_tensor", C(out=Aadd[:], in0=Aadd[:], in1=At[:], op=ALU.mult)), ["Aadd", "At"], ["Aadd"])
            V(("tensor_tensor", C(out=Aadd[:], in0=Aadd[:], in1=A0[:], op=ALU.max)), ["Aadd", "A0"], ["Aadd"])
            V(("tensor_scalar", C(out=At[:], in0=Dd[:], scalar1=0.0, scalar2=-1e30, op0=ALU.is_lt, op1=ALU.mult)), ["Dd"], ["At"])
            V(("tensor_tensor", C(out=Aadd[:], in0=Aadd[:], in1=At[:], op=ALU.add)), ["Aadd", "At"], ["Aadd"])
            cts = []
            for c in range(2):
                base = 128 * i - 2048 * c - 31
                if base + 127 < 0:
                    continue
                need_bias = base - 16 * 127 < 0
                cts.append((c, need_bias))
                if need_bias:
                    G(("affine_select", C(out=cmpb[:, c, :, :], in_=zerob[:], pattern=[[0, 4], [1, 128]], compare_op=ALU.is_ge, fill=NEG,
                                          base=base, channel_multiplier=-16)), ["zerob"], ["cmpb"])
            need_sel = i >= 8
            first = {g: True for g in range(4)}

            def gparams(g):
                m_, e_ = g // 2, g % 2
                ps_ = slice(e_ * 64, (e_ + 1) * 64)
                return m_, ps_, qT[ps_, m_, :, tl]

            def combine(g, br, bank):
                pO = pb[bank]
                pO3 = pO.rearrange("p (h n) -> p h n", h=4)
                src = [PB(bank)]
                V(("tensor_scalar", C(out=rc[:], in0=pO3[:, :, 64], scalar1=1e-30, scalar2=None, op0=ALU.max)), src, ["rc"])
                V(("reciprocal", C(out=rc[:], in_=rc[:])), ["rc"], ["rc"])
                V(("tensor_tensor", C(out=cf[:], in0=rc[:], in1=gsb[:, il, br * 16 + 4 * g:br * 16 + 4 * g + 4], op=ALU.mult)),
                  ["rc", "gsb"], ["cf"])
                for h in range(4):
                    if first[g]:
                        V(("tensor_scalar", C(out=oacc[:, 4 * g + h, :], in0=pO[:, h * 128:h * 128 + 64], scalar1=cf[:, h:h + 1], scalar2=None,
                                              op0=ALU.mult)), src + ["cf"], ["oacc"])
                    else:
                        V(("scalar_tensor_tensor", C(out=oacc[:, 4 * g + h, :], in0=pO[:, h * 128:h * 128 + 64], scalar=cf[:, h:h + 1],
                                                     in1=oacc[:, 4 * g + h, :], op0=ALU.mult, op1=ALU.add)), src + ["cf", "oacc"], ["oacc"])
                first[g] = False

            def pv(bank, P_ap_of_h, v_ap, is_first, is_last, rres):
                for h in range(4):
                    P.op("tensor", ("matmul", C(out=pb[bank][:, h * 128:h * 128 + 65], lhsT=P_ap_of_h(h), rhs=v_ap,
                                                start=(is_first and h == 0), stop=is_last, skip_group_check=True)),
                         rres, [PB(bank)], pe_accum=not (is_first and h == 0))

            def stage1(g):
                m_, ps_, rq = gparams(g)
                for ci, (c, nb_) in enumerate(cts):
                    psS = pb[ci % 2]
                    mm(psS, kcT[ps_, m_, c * 128:(c + 1) * 128], rq, True, not nb_, ["kcT", "qT"], [PB(ci % 2)])
                    if nb_:
                        mm(psS, identb[:], cmpb[:, c, :, :], False, True, ["identb", "cmpb"], [PB(ci % 2)])
                    A(("activation", C(out=Pc[:, c, :, :], in_=psS.rearrange("p (h t) -> p h t", h=4), func=AF.Exp)), [PB(ci % 2)], ["Pc"])
                bank = pvbank()
                for ci, (c, nb_) in enumerate(cts):
                    pv(bank, lambda h, c=c: Pc[:, c, h, :], vc_tok[:, c, g, :], ci == 0, ci == len(cts) - 1, ["Pc", "vc_tok"])
                pI = pb[5]
                for h in range(4):
                    for ci, (c, nb_) in enumerate(cts):
                        mm(pI[:, h * 64:(h + 1) * 64], Pc[:, c, h, :], ov[:, c, :], ci == 0, ci == len(cts) - 1, ["Pc", "ov"], [PB(5)])
                combine(g, 0, bank)
                if not need_sel:
                    return
                for h in range(4):
                    if h == 0:
                        V(("tensor_scalar", C(out=imp[:], in0=pI[:, 0:64], scalar1=rc[:, 0:1], scalar2=None, op0=ALU.mult)), [PB(5), "rc"], ["imp"])
                    else:
                        V(("scalar_tensor_tensor", C(out=imp[:], in0=pI[:, h * 64:(h + 1) * 64], scalar=rc[:, h:h + 1], in1=imp[:], op0=ALU.mult,
                                                     op1=ALU.add)), [PB(5), "rc", "imp"], ["imp"])
                V(("tensor_tensor", C(out=imp[:], in0=imp[:], in1=Aadd[:], op=ALU.add)), ["imp", "Aadd"], ["imp"])
                V(("max", C(out=mx[:, 0:8], in_=imp[:])), ["imp"], ["mx"])
                V(("match_replace", C(out=imp2[:], in_to_replace=mx[:, 0:8], in_values=imp[:], imm_value=-3e38)), ["imp", "mx"], ["imp2"])
                V(("max", C(out=mx[:, 8:16], in_=imp2[:])), ["imp2"], ["mx"])
                V(("tensor_scalar", C(out=selm[:], in0=imp[:], scalar1=mx[:, 15:16], scalar2=NEG, op0=ALU.is_lt, op1=ALU.mult)),
                  ["imp", "mx"], ["selm"])
                tr(pb[6][0:64, 0:128], selm[:], ident[:], ["selm", "ident"], [PB(6)])
                V(("tensor_copy", C(out=selmT[g % 2][0:64, :, :], in_=pb[6][0:64, 0:128].unsqueeze(1).broadcast_to([64, 4, 128]))),
                  [PB(6)], [("selmT", g % 2)])

            def stage2(g):
                m_, ps_, rq = gparams(g)
                jobs = []
                wt = list(range(max(0, i - 4), i + 1))
                for ti, s_t in enumerate(wt):
                    jobs.append((2, kwT, vw_tok, s_t, ti == 0, ti == len(wt) - 1))
                for ti, s_t in enumerate(range(0, i + 1)):
                    jobs.append((1, ksT, vs_tok, s_t, ti == 0, ti == i))
                pend = None
                bank = None
                for k, (br, kT, vtok, s_t, jf, jl) in enumerate(jobs + [None] if False else jobs):
                    par = jctr[0] % 2
                    jctr[0] += 1
                    psS = pb[par]
                    extra = []
                    if br == 1 and need_sel:
                        extra.append((E[:, s_t, :], selmT[g % 2][:], ["E", ("selmT", g % 2)]))
                    if s_t == i:
                        extra.append((identb[:], CB[:], ["identb", "CB"]))
                    if br == 2 and s_t == i - 4:
                        extra.append((identb[:], AB[:], ["identb", "AB"]))
                    mm(psS, kT[ps_, m_, s_t * 128:(s_t + 1) * 128], rq, True, len(extra) == 0, ["kT", "qT"], [PB(par)])
                    for xi, (l_, r_, rr) in enumerate(extra):
                        mm(psS, l_, r_, False, xi == len(extra) - 1, rr, [PB(par)])
                    A(("activation", C(out=Ps[par][:], in_=psS.rearrange("p (h t) -> p h t", h=4), func=AF.Exp)), [PB(par)], [("Ps", par)])
                    if pend is not None:
                        flush(g, pend)
                    pend = (br, vtok, s_t, jf, jl, par)
                flush(g, pend)

            cur_bank = {}

            def flush(g, pend):
                br, vtok, s_t, jf, jl, par = pend
                if jf:
                    cur_bank[(g, br)] = pvbank()
                bank = cur_bank[(g, br)]
                pv(bank, lambda h: Ps[par][:, h, :], vtok[:, s_t, g, :], jf, jl, [("Ps", par), "vtok"])
                if jl:
                    combine(g, br, bank)

            stage1(0)
            for g in range(4):
                if g < 3:
                    stage1(g + 1)
                stage2(g)
            if limit <= 4:
                continue
            ofl = oacc[:].rearrange("p a b -> p (a b)")
            for c in range(8):
                tr(pb[c // 4][:, (c % 4) * 128:(c % 4 + 1) * 128], ofl[:, c * 128:(c + 1) * 128], ident[:], ["oacc", "ident"], [PB(c // 4)])
            for hf in range(2):
                V(("tensor_tensor", C(out=yT[:, hf * 4:(hf + 1) * 4, :], in0=pb[hf].rearrange("p (c t) -> p c t", c=4),
                                      in1=szT[:, hf * 4:(hf + 1) * 4, tl], op=ALU.mult)), [PB(hf), "szT"], ["yT"])
            P.dma("gpsimd", pfx + "xr", xt[0][:], x[t0 + il * 128:t0 + (il + 1) * 128, :], writes=[("xt", 0)])
            for hf in range(2):
                for dc in range(8):
                    mm(pb[2 + hf], yT[:, dc, :], woutb[:, dc, hf * 512:(hf + 1) * 512], dc == 0, dc == 7, ["yT", "woutb"], [PB(2 + hf)])
                V(("tensor_tensor", C(out=outt[:, hf * 512:(hf + 1) * 512], in0=pb[2 + hf], in1=xt[0][:, hf * 512:(hf + 1) * 512], op=ALU.add)),
                  [PB(2 + hf), ("xt", 0)], ["outt"])
            P.dma("sync", pfx + "xo", xo[t0 + il * 128:t0 + (il + 1) * 128, :], outt[:], reads=["outt"], writes=[(pfx + "xo", i)])
            outs.append((pfx + "xo", i))
    return outs


T_SEQ = 4096
FUSED = True
_CACHE = {}


def _dt(nc, n, s):
    return nc.dram_tensor(n, s, F32, kind="ExternalInput").ap()


def _rwkv_wd(nc):
    return dict(g=_dt(nc, "r_g", [1, 1024]), vecs=_dt(nc, "r_vecs", [128, NV, 8]), w_in=_dt(nc, "r_w_in", [1024, 4096]),
                w1=_dt(nc, "r_w1", [1024, 64]), a1=_dt(nc, "r_a1", [1024, 64]), w2=_dt(nc, "r_w2", [64, 1024]),
                a2=_dt(nc, "r_a2", [64, 1024]), w_out=_dt(nc, "r_w_out", [1024, 1024]))


def _nsa_wd(nc):
    return dict(g=_dt(nc, "n_g", [1, 1024]), w_in=_dt(nc, "n_w_in", [1024, 3632]), w_out=_dt(nc, "n_w_out", [1024, 1024]),
                w1=_dt(nc, "n_w1", [2, 2048, 256]), gains=_dt(nc, "n_gains", [128, 4]), peT=_dt(nc, "n_peT", [128, 32]),
                b1=_dt(nc, "n_b1", [128, 2, 2]), w2=_dt(nc, "n_w2", [128, 256]))


def _build(which):
    T = T_SEQ
    nc = bass.Bass("TRN2", target_bir_lowering=False)
    x = _dt(nc, "x", [T, 1024])
    xo = nc.dram_tensor("xo", [T, 1024], F32, kind="ExternalOutput").ap()
    if which == "fused":
        x1 = nc.dram_tensor("x1_scr", [T, 1024], F32, kind="Internal").ap()
        rwd = _rwkv_wd(nc)
        nwd = _nsa_wd(nc)
        with ExitStack() as st:
            P = Prog(nc, st)
            outs = emit_rwkv(nc, P, st, x, x1, rwd, T)
            P.final_wait("sync", outs)
            P.emit()
        with ExitStack() as st:
            P = Prog(nc, st)
            outs = emit_nsa(nc, P, st, x1, xo, nwd, T)
            P.final_wait("sync", outs)
            P.emit()
    else:
        wd = _rwkv_wd(nc) if which == "rwkv" else _nsa_wd(nc)
        with ExitStack() as st:
            P = Prog(nc, st)
            outs = (emit_rwkv if which == "rwkv" else emit_nsa)(nc, P, st, x, xo, wd, T)
            P.final_wait("sync", outs)
            P.emit()
    return nc


def _get(which):
    if which not in _CACHE:
        _CACHE[which] = _build(which)
    return _CACHE[which]


def kernel(**inputs):
    inp = {k: np.asarray(v) for k, v in inputs.items()}
    x = np.ascontiguousarray(inp["x"], dtype=np.float32)
    B = x.shape[0]
    f32 = lambda a: np.ascontiguousarray(a, dtype=np.float32)
    rmap = {"r_g": f32(inp["norm_g"][0:1]), "r_vecs": rwkv_host_vecs(inp), "r_w_in": f32(inp["rwkv_w_in"][0]),
            "r_w1": f32(inp["rwkv_w1"][0]), "r_a1": f32(inp["rwkv_a1"][0]), "r_w2": f32(inp["rwkv_w2"][0]),
            "r_a2": f32(inp["rwkv_a2"][0]), "r_w_out": f32(inp["rwkv_w_out"][0])}
    hp = nsa_host(inp)
    nmap = {"n_g": f32(inp["norm_g"][1:2]), "n_w_in": f32(inp["nsa_w_in"][0]), "n_w_out": f32(inp["nsa_w_out"][0]),
            "n_w1": f32(inp["nsa_cmp_w1"][0]), "n_gains": hp["gains"], "n_peT": hp["peT"], "n_b1": hp["b1"], "n_w2": hp["w2"]}
    cores = list(range(B))
    if FUSED:
        nc = _get("fused")
        res = run_bass_kernel_spmd(nc, [{"x": x[i], **rmap, **nmap} for i in cores], core_ids=cores)
        return np.stack([np.asarray(res.results[i]["xo"], dtype=np.float32) for i in cores], 0)
    nc = _get("rwkv")
    res = run_bass_kernel_spmd(nc, [{"x": x[i], **rmap} for i in cores], core_ids=cores)
    x1 = [np.ascontiguousarray(res.results[i]["xo"], dtype=np.float32) for i in cores]
    nc = _get("nsa")
    res = run_bass_kernel_spmd(nc, [{"x": x1[i], **nmap} for i in cores], core_ids=cores)
    return np.stack([np.asarray(res.results[i]["xo"], dtype=np.float32) for i in cores], 0)
```

```python
from contextlib import ExitStack
from concourse.bass_utils import run_bass_kernel_spmd
import numpy as np
import concourse.bass as bass
import concourse.mybir as mybir

F32 = mybir.dt.float32
BF16 = mybir.dt.bfloat16
ALU = mybir.AluOpType
AF = mybir.ActivationFunctionType
AX = mybir.AxisListType

ENGINES = ("tensor", "vector", "scalar", "gpsimd", "sync")
CH = 30000


class Prog:
    def __init__(self, nc, stack, same_engine_sync=True):
        self.nc = nc
        self.stack = stack
        self.ops = {e: [] for e in ENGINES}
        self.cnt = {e: 0 for e in ENGINES}
        self.sems = {}
        self.res_w = {}
        self.res_r = {}
        self.dma_cnt = {}
        self.seen = {e: {} for e in ENGINES}
        self.same_engine_sync = same_engine_sync
        self.nwaits = 0
        self.max_ops = 10**9
        self.nops = 0
        self.last_line = None

    def sem(self, key):
        if key not in self.sems:
            name = "s_" + "_".join(str(k) for k in (key if isinstance(key, tuple) else (key,)))
            self.sems[key] = self.stack.enter_context(self.nc.semaphore(name))
        return self.sems[key]

    def _deps(self, eng, reads, writes, pe_accum=False):
        waits = {}

        def need(dep):
            if dep is None:
                return
            semkey, val, deng = dep
            if deng == eng and semkey[0] == "c":
                if not self.same_engine_sync:
                    return
                if eng == "tensor" and pe_accum:
                    return
            if self.seen[eng].get(semkey, 0) >= val:
                return
            if waits.get(semkey, 0) < val:
                waits[semkey] = val

        for r in reads:
            need(self.res_w.get(r))
        for w in writes:
            need(self.res_w.get(w))
            for rd in self.res_r.get(w, ()):
                need(rd)
        for k, v in waits.items():
            self.seen[eng][k] = v
        self.nwaits += len(waits)
        return list(waits.items())

    def _record(self, dep, reads, writes):
        for r in reads:
            self.res_r.setdefault(r, []).append(dep)
        for w in writes:
            self.res_w[w] = dep
            self.res_r[w] = []

    def op(self, eng, fn, reads=(), writes=(), pe_accum=False):
        isps = lambda r: isinstance(r, tuple) and isinstance(r[0], str) and r[0].endswith("pb")
        writes = list(writes) + [r for r in reads if isps(r)]
        reads = [r for r in reads if not isps(r)]
        waits = self._deps(eng, reads, writes, pe_accum)
        i = self.cnt[eng]
        self.cnt[eng] += 1
        semkey = ("c", eng, i // CH)
        self.sem(semkey)
        for k, _ in waits:
            self.sem(k)
        dep = (semkey, i % CH + 1, eng)
        self.ops[eng].append((waits, fn, semkey, 1))
        self._record(dep, reads, writes)

    def dma(self, eng, semname, out, in_, reads=(), writes=(), **kw):
        waits = self._deps(eng, reads, writes)
        semkey = ("d", semname)
        self.sem(semkey)
        for k, _ in waits:
            self.sem(k)
        n = self.dma_cnt.get(semname, 0) + 1
        self.dma_cnt[semname] = n
        dep = (semkey, 16 * n, eng)
        self.ops[eng].append((waits, lambda e: e.dma_start(out=out, in_=in_, **kw), semkey, 16))
        self._record(dep, reads, writes)

    def final_wait(self, eng, resources):
        waits = self._deps(eng, resources, ())
        for k, _ in waits:
            self.sem(k)
        self.ops[eng].append((waits, None, None, 0))

    def emit(self):
        nc = self.nc
        with nc.Block() as block:
            def mk(engname):
                def body(e):
                    for waits, fn, semkey, inc in self.ops[engname]:
                        for k, v in waits:
                            e.wait_ge(self.sems[k], v)
                        if fn is not None:
                            try:
                                ins = getattr(e, fn[0])(*fn[1][0], **fn[1][1]) if isinstance(fn, tuple) else fn(e)
                            except Exception:
                                print("EMIT FAIL", engname, fn[0] if isinstance(fn, tuple) else fn, {k: (v if not hasattr(v, "shape") else ("AP", v.shape)) for k, v in fn[1][1].items()} if isinstance(fn, tuple) else "")
                                raise
                            ins.then_inc(self.sems[semkey], inc)
                return body
            block.tensor(mk("tensor"))
            block.vector(mk("vector"))
            block.scalar(mk("scalar"))
            block.gpsimd(mk("gpsimd"))
            block.sync(mk("sync"))


def C(*a, **k):
    return (a, k)


D = 1024
NV = 13
I_W0, I_A0, I_KK, I_KA, I_RK, I_LG, I_LB = 6, 7, 8, 9, 10, 11, 12
DEC = -float(np.exp(-0.5))


def rwkv_host_vecs(inp):
    rows = [inp["rwkv_mu"][0][i] for i in range(6)] + [inp[k][0].reshape(-1) for k in
            ["rwkv_w0", "rwkv_a0", "rwkv_k_k", "rwkv_k_a"]]
    rows.append(np.tile(inp["rwkv_r_k"][0].reshape(16, 64), 1).reshape(-1))
    rows += [inp["rwkv_lnx_g"][0], inp["rwkv_lnx_b"][0]]
    v = np.stack([np.asarray(r, np.float32).reshape(8, 128) for r in rows], 0)
    return np.ascontiguousarray(v.transpose(2, 0, 1))


def emit_rwkv(nc, P, st, x, xo, wd, T, pfx="r", limit=99):
    NB = T // 256
    sbn = [0]

    def sb(shape, dt, name=None):
        sbn[0] += 1
        return st.enter_context(nc.sbuf_tensor(f"{pfx}_{name or 't'}{sbn[0]}", shape, dt))

    pball = st.enter_context(nc.psum_tensor(f"{pfx}_psum", [128, 8, 512], F32))
    pb = [pball[:, i, :] for i in range(8)]
    PB = lambda i: (pfx + "pb", i)

    RN = {"t1": "sq", "rk": "sq", "Lp": "rn", "eLp": "kkr", "BtT": "kf", "KtT": "a"}
    cn = lambda l: [RN.get(x, x) if isinstance(x, str) else x for x in l]

    def V(fn, r=(), w=()): P.op("vector", fn, cn(r), cn(w))
    def G(fn, r=(), w=()): P.op("gpsimd", fn, cn(r), cn(w))
    def A(fn, r=(), w=()): P.op("scalar", fn, cn(r), cn(w))

    def mm(out, lhsT, rhs, start=True, stop=True, r=(), w=()):
        P.op("tensor", ("matmul", C(out=out, lhsT=lhsT, rhs=rhs, start=start, stop=stop)), cn(r), cn(w), pe_accum=not start)

    def tr(out, in_, ident, r=(), w=()):
        P.op("tensor", ("transpose", C(out=out, in_=in_, identity=ident)), cn(r), cn(w))

    ones = sb([128, 512], F32, "ones")
    ident = sb([128, 128], F32, "ident")
    triS = sb([128, 128], F32, "triS")
    triI = sb([128, 128], F32, "triI")
    triL = sb([128, 128], F32, "triL")
    BD = sb([128, 128], F32, "BD")
    m01 = sb([128, 256], F32, "m01")
    G(("memset", C(ones[:], 1.0)), w=["ones"])
    G(("affine_select", C(out=ident[:], in_=ones[:, 0:128], pattern=[[-1, 128]], compare_op=ALU.is_equal,
                                fill=0.0, base=0, channel_multiplier=1)), r=["ones"], w=["ident"])
    o1 = ones[:, 0:128]
    G(("affine_select", C(out=triS[:], in_=o1, pattern=[[1, 128]], compare_op=ALU.is_gt,
                                fill=0.0, base=0, channel_multiplier=-1)), r=["ones"], w=["triS"])
    G(("affine_select", C(out=triI[:], in_=o1, pattern=[[1, 128]], compare_op=ALU.is_ge,
                                fill=0.0, base=0, channel_multiplier=-1)), r=["ones"], w=["triI"])
    G(("affine_select", C(out=triL[:], in_=o1, pattern=[[-1, 128]], compare_op=ALU.is_gt,
                                fill=0.0, base=0, channel_multiplier=1)), r=["ones"], w=["triL"])
    G(("memset", C(BD[:], 0.0)), w=["BD"])
    G(("memset", C(BD[0:64, 0:64], 1.0)), w=["BD"])
    G(("memset", C(BD[64:128, 64:128], 1.0)), w=["BD"])
    G(("memset", C(m01[:], 1.0)), w=["m01"])
    G(("memset", C(m01[:, 0:1], 0.0)), w=["m01"])
    G(("memset", C(m01[:, 128:129], 0.0)), w=["m01"])

    vecs = sb([128, NV, 8], F32, "vecs")
    gb = sb([128, D], F32, "gb")
    wslot = [sb([128, 8, 4, 128], BF16, f"wslot{i}") for i in range(2)]
    wscr = nc.dram_tensor(pfx + "_wscr", [8, 128, 8, 4, 128], BF16, kind="Internal").ap()
    wscr_w = wscr.rearrange("h p d c f -> p h d c f")
    woutb = sb([128, 8, 1024], BF16, "woutb")
    w1b = sb([128, 8, 64], BF16, "w1b")
    a1b = sb([128, 8, 64], BF16, "a1b")
    w2b = sb([64, 1024], BF16, "w2b")
    a2b = sb([64, 1024], BF16, "a2b")
    stg = [sb([128, 1024], F32, f"stg{i}") for i in range(2)]
    stgb = [sb([128, 1024], BF16, f"stgb{i}") for i in range(2)]
    P.dma("sync", pfx + "vecs", vecs[:], wd["vecs"], writes=["vecs"])
    P.dma("sync", pfx + "gb", gb[:], wd["g"].partition_broadcast(128), writes=["gb"])
    nst = [0]
    WS_ALL = [("wscr", c, ci) for c in range(8) for ci in range(4)]

    def load_cast(dst_ap, src_ap, np_, ncols, wres):
        i = nst[0] % 2
        nst[0] += 1
        q = "sync" if i == 0 else "gpsimd"
        P.dma(q, pfx + f"stg{i}", stg[i][0:np_, 0:ncols], src_ap, writes=[("stg", i)])
        eng = "vector" if i == 0 else "gpsimd"
        P.op(eng, ("tensor_copy", C(out=dst_ap, in_=stg[i][0:np_, 0:ncols])), [("stg", i)], [wres])

    win_v = wd["w_in"].rearrange("(c p) f -> p c f", p=128)
    for c in range(8):
        for ci in range(4):
            i = nst[0] % 2
            load_cast(stgb[i][:], win_v[:, c, ci * 1024:(ci + 1) * 1024], 128, 1024, ("stgb", i))
            P.dma("sync" if i == 0 else "gpsimd", pfx + f"wscr{i}", wscr_w[:, :, c, ci, :],
                  stgb[i][:].rearrange("p (h f) -> p h f", h=8), reads=[("stgb", i)], writes=[("wscr", c, ci)])
    wout_v = wd["w_out"].rearrange("(c p) f -> p c f", p=128)
    for c in range(8):
        load_cast(woutb[:, c, :], wout_v[:, c, :], 128, 1024, "woutb")
    load_cast(w1b[:], wd["w1"].rearrange("(c p) f -> p c f", p=128), 128, 512, "w1b")
    load_cast(a1b[:], wd["a1"].rearrange("(c p) f -> p c f", p=128), 128, 512, "a1b")
    load_cast(w2b[:], wd["w2"], 64, 1024, "w2b")
    load_cast(a2b[:], wd["a2"], 64, 1024, "a2b")

    if limit <= 0:
        return []
    xt = [sb([128, D], F32, f"xt{i}") for i in range(2)]
    ss = sb([128, 1], F32, "ss")
    rs = sb([128, 1], F32, "rs")
    ht = sb([128, D], F32, "ht")
    hT = sb([128, 8, 257], F32, "hT")
    dh = [sb([128, 256], F32, f"dh{i}") for i in range(2)]
    xm = sb([128, 6, 8, 256], BF16, "xm")
    la = sb([64, 256], BF16, "la")
    lw = sb([64, 256], BF16, "lw")
    rh = sb([128, 8, 256], BF16, "rh")
    ah = sb([128, 8, 256], BF16, "ah")
    bh = sb([128, 8, 256], BF16, "bh")
    kh = sb([128, 8, 256], BF16, "kh")
    Vt = sb([128, 2, 1024], BF16, "Vt")
    Bt = sb([128, 2, 1024], BF16, "Bt")
    Kt = sb([128, 2, 1024], BF16, "Kt")
    sz = sb([128, 8, 256], BF16, "sz")
    bonus = sb([128, 8, 256], BF16, "bonus")
    gC = sb([128, 8, 2], F32, "gC")
    tmp = {n: sb([128, 256], F32, n) for n in
           ["kf", "kkr", "sq", "rn", "kk", "a", "kp", "bb", "sig", "L", "eL", "enL", "E2", "vf"]}
    for k_, v_ in RN.items():
        tmp[k_] = tmp[v_]
    ST = sb([128, 8, 64], F32, "ST")
    STb = sb([128, 8, 64], BF16, "STb")
    Q = [sb([128, 8, 128], BF16, f"Q{i}") for i in range(2)]
    QT = [sb([128, 8, 128], BF16, f"QT{i}") for i in range(2)]
    Z = sb([128, 8, 128], F32, "Z")
    Zb = sb([128, 8, 128], BF16, "Zb")
    WT = sb([128, 16, 128], BF16, "WT")
    Mak = sb([128, 16, 128], BF16, "Mak")
    Mrb = sb([128, 16, 128], BF16, "Mrb")
    Mrk = sb([128, 16, 128], BF16, "Mrk")
    Xn = sb([128, 1024], BF16, "Xn")
    Ub = sb([128, 1024], BF16, "Ub")
    of = sb([128, 16, 64], F32, "of")
    mean = sb([128, 16], F32, "mean")
    ex2 = sb([128, 16], F32, "ex2")
    var = sb([128, 16], F32, "var")
    yT = sb([128, 8, 128], BF16, "yT")
    ytmp = sb([128, 128], F32, "ytmp")
    xr = xt[0]
    outt = sb([128, D], F32, "outt")
    osq = outt[:].rearrange("p (a b) -> p a b", a=16)

    G(("memset", C(ST[:], 0.0)), w=["ST"])
    G(("memset", C(STb[:], 0.0)), w=["STb"])
    G(("memset", C(hT[:, :, 0:1], 0.0)), w=["hT"])

    vcol = lambda i, hp: vecs[:, i, hp:hp + 1]
    eps = 1e-6

    for b in range(NB):
        t0 = b * 256
        for i in range(2):
            xb = xt[i]
            P.dma("sync", pfx + f"x{i}", xb[:], x[t0 + i * 128:t0 + (i + 1) * 128, :], writes=[("xt", i)])
            A(("activation", C(out=ht[:], in_=xb[:], func=AF.Square, accum_out=ss[:])), [("xt", i)], ["ht", "ss"])
            A(("activation", C(out=rs[:], in_=ss[:], func=AF.Sqrt, scale=1.0 / D, bias=eps)), ["ss"], ["rs"])
            V(("reciprocal", C(out=rs[:], in_=rs[:])), ["rs"], ["rs"])
            V(("scalar_tensor_tensor", C(out=ht[:], in0=xb[:], scalar=rs[:, 0:1], in1=gb[:], op0=ALU.mult, op1=ALU.mult)),
              [("xt", i), "rs", "gb"], ["ht"])
            for half in range(2):
                for c in range(4):
                    cc = half * 4 + c
                    tr(pb[half][:, c * 128:(c + 1) * 128], ht[:, cc * 128:(cc + 1) * 128], ident[:], ["ht", "ident"], [PB(half)])
                pv = pb[half].rearrange("p (c t) -> p c t", c=4)
                dst = hT[:, half * 4:(half + 1) * 4, 1 + i * 128:1 + (i + 1) * 128]
                if half == 0:
                    V(("tensor_copy", C(out=dst, in_=pv)), [PB(half)], ["hT"])
                else:
                    A(("copy", C(out=dst, in_=pv)), [PB(half)], ["hT"])
        if limit <= 1:
            return []
        n = 0
        for dc in range(8):
            dd = dh[dc % 2]
            V(("tensor_tensor", C(out=dd[:], in0=hT[:, dc, 0:256], in1=hT[:, dc, 1:257], op=ALU.subtract)),
              ["hT"], [("dh", dc % 2)])
            for c in range(6):
                fn = ("scalar_tensor_tensor", C(out=xm[:, c, dc, :], in0=dd[:], scalar=vcol(c, dc),
                                                                        in1=hT[:, dc, 1:257], op0=ALU.mult, op1=ALU.add))
                V(fn, [("dh", dc % 2), "hT", "vecs"], [("xm", c)])
                n += 1
        V(("tensor_copy", C(out=hT[:, :, 0:1], in_=hT[:, :, 256:257])), ["hT"], ["hT"])
        if limit <= 2:
            return []
        for dc in range(8):
            mm(pb[2][0:64, 0:256], a1b[:, dc, :], xm[:, 5, dc, :], dc == 0, dc == 7, [("xm", 5), "a1b"], [PB(2)])
        V(("tensor_copy", C(out=la[:], in_=pb[2][0:64, 0:256])), [PB(2)], ["la"])
        for dc in range(8):
            mm(pb[3][0:64, 0:256], w1b[:, dc, :], xm[:, 4, dc, :], dc == 0, dc == 7, [("xm", 4), "w1b"], [PB(3)])
        A(("activation", C(out=lw[:], in_=pb[3][0:64, 0:256], func=AF.Tanh)), [PB(3)], ["lw"])
        if limit <= 3:
            return []
        for hp in range(8):
            fs = slice(hp * 128, (hp + 1) * 128)
            n_it = b * 8 + hp
            if n_it == 0:
                P.dma("sync", pfx + "ws0", wslot[0][:], wscr[0], reads=WS_ALL, writes=[("wslot", 0)])
            if n_it + 1 < NB * 8:
                sl_ = (n_it + 1) % 2
                P.dma("sync", pfx + f"ws{sl_}", wslot[sl_][:], wscr[(hp + 1) % 8], reads=WS_ALL, writes=[("wslot", sl_)])
            wsl = wslot[n_it % 2]
            pR, pK, pV_, pZ = pb[0][:, 0:256], pb[0][:, 256:512], pb[1][:, 0:256], pb[1][:, 256:512]
            pA, pU = pb[2][:, 0:256], pb[2][:, 256:512]
            pN, pBS = pb[3][:, 0:256], pb[3][:, 256:512]
            for ci, (po, pbi) in enumerate([(pR, 0), (pK, 0), (pV_, 1), (pZ, 1)]):
                for dc in range(8):
                    mm(po, wsl[:, dc, ci, :], xm[:, ci, dc, :], dc == 0, dc == 7,
                       [("xm", ci), ("wslot", n_it % 2)], [PB(pbi)])
            mm(pA, a2b[0:64, fs], la[:], True, True, ["a2b", "la"], [PB(2)])
            mm(pU, w2b[0:64, fs], lw[:], True, True, ["w2b", "lw"], [PB(2)])
            t = tmp
            A(("copy", C(out=t["kf"][:], in_=pK)), [PB(0)], ["kf"])
            V(("tensor_scalar", C(out=t["kkr"][:], in0=pK, scalar1=vcol(I_KK, hp), scalar2=None, op0=ALU.mult)), [PB(0), "vecs"], ["kkr"])
            A(("activation", C(out=t["sq"][:], in_=t["kkr"][:], func=AF.Square)), ["kkr"], ["sq"])
            mm(pN, BD[:], t["sq"][:], True, True, ["BD", "sq"], [PB(3)])
            A(("activation", C(out=t["rn"][:], in_=pN, func=AF.Sqrt)), [PB(3)], ["rn"])
            V(("tensor_scalar", C(out=t["rn"][:], in0=t["rn"][:], scalar1=1e-12, scalar2=None, op0=ALU.max)), ["rn"], ["rn"])
            V(("reciprocal", C(out=t["rn"][:], in_=t["rn"][:])), ["rn"], ["rn"])
            V(("tensor_tensor", C(out=t["kk"][:], in0=t["kkr"][:], in1=t["rn"][:], op=ALU.mult)), ["kkr", "rn"], ["kk"])
            A(("activation", C(out=t["a"][:], in_=pA, func=AF.Sigmoid, bias=vcol(I_A0, hp))), [PB(2), "vecs"], ["a"])
            V(("tensor_scalar", C(out=t["t1"][:], in0=t["a"][:], scalar1=-1.0, scalar2=vcol(I_KA, hp), op0=ALU.add, op1=ALU.mult)),
              ["a", "vecs"], ["t1"])
            V(("scalar_tensor_tensor", C(out=t["kp"][:], in0=t["t1"][:], scalar=1.0, in1=t["kf"][:], op0=ALU.add, op1=ALU.mult)),
              ["t1", "kf"], ["kp"])
            V(("tensor_tensor", C(out=t["bb"][:], in0=t["kk"][:], in1=t["a"][:], op=ALU.mult)), ["kk", "a"], ["bb"])
            A(("activation", C(out=t["sig"][:], in_=pU, func=AF.Sigmoid, bias=vcol(I_W0, hp))), [PB(2), "vecs"], ["sig"])
            A(("mul", C(out=t["sig"][:], in_=t["sig"][:], mul=DEC)), ["sig"], ["sig"])
            V(("tensor_tensor_scan", C(out=t["L"][:], data0=m01[:], data1=t["sig"][:], initial=0.0, op0=ALU.mult, op1=ALU.add)),
              ["m01", "sig"], ["L"])
            V(("tensor_tensor", C(out=t["Lp"][:], in0=t["L"][:], in1=t["sig"][:], op=ALU.subtract)), ["L", "sig"], ["Lp"])
            A(("activation", C(out=t["eL"][:], in_=t["L"][:], func=AF.Exp)), ["L"], ["eL"])
            A(("activation", C(out=t["eLp"][:], in_=t["Lp"][:], func=AF.Exp)), ["Lp"], ["eLp"])
            A(("activation", C(out=t["enL"][:], in_=t["L"][:], func=AF.Exp, scale=-1.0)), ["L"], ["enL"])
            for j in range(2):
                cs = slice(j * 128, (j + 1) * 128)
                A(("activation", C(out=t["E2"][:, cs], in_=t["L"][:, cs], func=AF.Exp, scale=-1.0,
                                                    bias=t["L"][:, j * 128 + 127:j * 128 + 128])), ["L"], ["E2"])
            V(("tensor_tensor", C(out=rh[:, hp, :], in0=pR, in1=t["eL"][:], op=ALU.mult)), [PB(0), "eL"], [("rh", hp)])
            V(("scalar_tensor_tensor", C(out=t["rk"][:], in0=pR, scalar=vcol(I_RK, hp), in1=t["kp"][:], op0=ALU.mult, op1=ALU.mult)),
              [PB(0), "vecs", "kp"], ["rk"])
            mm(pBS, BD[:], t["rk"][:], True, True, ["BD", "rk"], [PB(3)])
            A(("copy", C(out=t["vf"][:], in_=pV_)), [PB(1)], ["vf"])
            V(("tensor_tensor", C(out=bonus[:, hp, :], in0=pBS, in1=t["vf"][:], op=ALU.mult)), [PB(3), "vf"], [("bonus", hp)])
            A(("activation", C(out=sz[:, hp, :], in_=pZ, func=AF.Silu)), [PB(1)], [("sz", hp)])
            V(("tensor_tensor", C(out=ah[:, hp, :], in0=t["kk"][:], in1=t["eLp"][:], op=ALU.mult)), ["kk", "eLp"], [("ah", hp)])
            V(("tensor_tensor", C(out=bh[:, hp, :], in0=t["bb"][:], in1=t["enL"][:], op=ALU.mult)), ["bb", "enL"], [("bh", hp)])
            V(("tensor_tensor", C(out=kh[:, hp, :], in0=t["kp"][:], in1=t["enL"][:], op=ALU.mult)), ["kp", "enL"], [("kh", hp)])
            V(("tensor_tensor", C(out=t["BtT"][:], in0=t["bb"][:], in1=t["E2"][:], op=ALU.mult)), ["bb", "E2"], ["BtT"])
            V(("tensor_tensor", C(out=t["KtT"][:], in0=t["kp"][:], in1=t["E2"][:], op=ALU.mult)), ["kp", "E2"], ["KtT"])
            V(("tensor_copy", C(out=gC[:, hp, :], in_=t["eL"][:, 127:256:128])), ["eL"], ["gC"])
            for j in range(2):
                cs = slice(j * 128, (j + 1) * 128)
                for si, (src, sres) in enumerate([(t["BtT"], "BtT"), (t["KtT"], "KtT"), (t["vf"], "vf")]):
                    tr(pb[4 + j][:, si * 128:(si + 1) * 128], src[:, cs], ident[:], [sres, "ident"], [PB(4 + j)])
                V(("tensor_copy", C(out=Bt[:, j, fs], in_=pb[4 + j][:, 0:128])), [PB(4 + j)], [("Bt", j)])
                A(("copy", C(out=Kt[:, j, fs], in_=pb[4 + j][:, 128:256])), [PB(4 + j)], [("Kt", j)])
                V(("tensor_copy", C(out=Vt[:, j, fs], in_=pb[4 + j][:, 256:384])), [PB(4 + j)], [("Vt", j)])
        if limit <= 4:
            continue
        for j in range(2):
            ts = slice(j * 128, (j + 1) * 128)
            P8 = lambda b0: pball[:, b0:b0 + 2, :].rearrange("p a (b c) -> p (a b) c", b=4)
            bc = lambda m_: m_[:].unsqueeze(1).broadcast_to([128, 8, 128])
            for hb in range(2):
                hs = slice(hb * 8, hb * 8 + 8)
                hq = []
                for q in range(8):
                    h = hb * 8 + q
                    hp, hh = h // 2, h % 2
                    ps_ = slice(hh * 64, hh * 64 + 64)
                    hq.append((hp, ah[ps_, hp, ts], bh[ps_, hp, ts], kh[ps_, hp, ts], rh[ps_, hp, ts], q // 4, slice((q % 4) * 128, (q % 4 + 1) * 128)))
                for (hp, a_, b_, k_, r_, bo, qs) in hq:
                    mm(pb[0 + bo][:, qs], b_, a_, True, True, [("ah", hp), ("bh", hp)], [PB(0 + bo)])
                    mm(pb[2 + bo][:, qs], a_, b_, True, True, [("ah", hp), ("bh", hp)], [PB(2 + bo)])
                    mm(pb[4 + bo][:, qs], k_, a_, True, True, [("ah", hp), ("kh", hp)], [PB(4 + bo)])
                    mm(pb[6 + bo][:, qs], b_, r_, True, True, [("rh", hp), ("bh", hp)], [PB(6 + bo)])
                V(("scalar_tensor_tensor", C(out=Z[:], in0=P8(0), scalar=-1.0, in1=bc(triS), op0=ALU.mult, op1=ALU.mult)),
                  [PB(0), PB(1), "triS"], ["Z"])
                V(("scalar_tensor_tensor", C(out=Q[0][:], in0=P8(2), scalar=-1.0, in1=bc(triL), op0=ALU.mult, op1=ALU.mult)),
                  [PB(2), PB(3), "triL"], [("Q", 0)])
                A(("copy", C(out=QT[0][:], in_=Z[:])), ["Z"], [("QT", 0)])
                for (hp, a_, b_, k_, r_, bo, qs) in hq:
                    mm(pb[0 + bo][:, qs], k_, r_, True, True, [("rh", hp), ("kh", hp)], [PB(0 + bo)])
                V(("tensor_tensor", C(out=Mak[:, hs, :], in0=P8(4), in1=bc(triS), op=ALU.mult)), [PB(4), PB(5), "triS"], ["Mak"])
                V(("tensor_tensor", C(out=Mrb[:, hs, :], in0=P8(6), in1=bc(triI), op=ALU.mult)), [PB(6), PB(7), "triI"], ["Mrb"])
                V(("tensor_tensor", C(out=Z[:], in0=Z[:], in1=bc(ident), op=ALU.add)), ["Z", "ident"], ["Z"])
                A(("copy", C(out=Zb[:], in_=Z[:])), ["Z"], ["Zb"])
                V(("tensor_tensor", C(out=Mrk[:, hs, :], in0=P8(0), in1=bc(triI), op=ALU.mult)), [PB(0), PB(1), "triI"], ["Mrk"])

                def sq(l):
                    c_, n_ = l % 2, (l + 1) % 2
                    for q in range(8):
                        bo, qs = q // 4, slice((q % 4) * 128, (q % 4 + 1) * 128)
                        mm(pb[2 + bo][:, qs], QT[c_][:, q, :], Q[c_][:, q, :], True, True, [("Q", c_), ("QT", c_)], [PB(2 + bo)])
                        mm(pb[4 + bo][:, qs], Q[c_][:, q, :], QT[c_][:, q, :], True, True, [("Q", c_), ("QT", c_)], [PB(4 + bo)])
                    V(("tensor_copy", C(out=Q[n_][:], in_=P8(2))), [PB(2), PB(3)], [("Q", n_)])
                    A(("copy", C(out=QT[n_][:], in_=P8(4))), [PB(4), PB(5)], [("QT", n_)])

                def zupd(l):
                    n_ = (l + 1) % 2
                    for q in range(8):
                        bo, qs = q // 4, slice((q % 4) * 128, (q % 4 + 1) * 128)
                        mm(pb[6 + bo][:, qs], Q[n_][:, q, :], Zb[:, q, :], True, True, [("Q", n_), "Zb"], [PB(6 + bo)])
                    V(("tensor_tensor", C(out=Z[:], in0=P8(6), in1=Z[:], op=ALU.add)), [PB(6), PB(7), "Z"], ["Z"])
                    if l < 5:
                        A(("copy", C(out=Zb[:], in_=Z[:])), ["Z"], ["Zb"])
                    else:
                        A(("copy", C(out=WT[:, hs, :], in_=Z[:])), ["Z"], ["WT"])

                sq(0)
                for l in range(6):
                    if l + 1 < 6:
                        sq(l + 1)
                    zupd(l)
            if limit <= 5:
                continue
            pX = pball[:, 0:2, :].rearrange("p a b -> p (a b)")
            pUu = pball[:, 2:4, :].rearrange("p a b -> p (a b)")
            pO = pball[:, 4:6, :].rearrange("p a b -> p (a b)")
            pS = pb[6]
            hd = lambda h: (h // 2, slice((h % 2) * 64, (h % 2) * 64 + 64), slice(h * 64, (h + 1) * 64))
            for h in range(16):
                hp, ps_, vs = hd(h)
                mm(pX[:, vs], ah[ps_, hp, ts], STb[ps_, hp, :], True, False, [("ah", hp), "STb"], [PB(h // 8)])
                mm(pX[:, vs], Mak[:, h, :], Vt[:, j, vs], False, True, ["Mak", ("Vt", j)], [PB(h // 8)])
            V(("tensor_scalar", C(out=Xn[:, 0:512], in0=pX[:, 0:512], scalar1=-1.0, scalar2=None, op0=ALU.mult)), [PB(0)], ["Xn"])
            A(("mul", C(out=Xn[:, 512:1024], in_=pX[:, 512:1024], mul=-1.0)), [PB(1)], ["Xn"])
            for h in range(16):
                hp, ps_, vs = hd(h)
                mm(pUu[:, vs], WT[:, h, :], Xn[:, vs], True, True, ["WT", "Xn"], [PB(2 + h // 8)])
            V(("tensor_copy", C(out=Ub[:, 0:512], in_=pUu[:, 0:512])), [PB(2)], ["Ub"])
            A(("copy", C(out=Ub[:, 512:1024], in_=pUu[:, 512:1024])), [PB(3)], ["Ub"])
            for h in range(16):
                hp, ps_, vs = hd(h)
                mm(pO[:, vs], rh[ps_, hp, ts], STb[ps_, hp, :], True, False, [("rh", hp), "STb"], [PB(4 + h // 8)])
                mm(pO[:, vs], Mrb[:, h, :], Ub[:, vs], False, False, ["Mrb", "Ub"], [PB(4 + h // 8)])
                mm(pO[:, vs], Mrk[:, h, :], Vt[:, j, vs], False, True, ["Mrk", ("Vt", j)], [PB(4 + h // 8)])
            for h in range(16):
                hp, ps_, vs = hd(h)
                mm(pS[ps_, hp * 64:(hp + 1) * 64], Bt[:, j, vs], Ub[:, vs], True, False, [("Bt", j), "Ub"], [PB(6)])
                mm(pS[ps_, hp * 64:(hp + 1) * 64], Kt[:, j, vs], Vt[:, j, vs], False, True, [("Kt", j), ("Vt", j)], [PB(6)])
            ofl = of[:].rearrange("p a b -> p (a b)")
            V(("tensor_copy", C(out=ofl[:, 0:512], in_=pO[:, 0:512])), [PB(4)], ["of"])
            A(("copy", C(out=ofl[:, 512:1024], in_=pO[:, 512:1024])), [PB(5)], ["of"])
            for hp in range(8):
                V(("scalar_tensor_tensor", C(out=ST[:, hp, :], in0=ST[:, hp, :], scalar=gC[:, hp, j:j + 1],
                                                         in1=pS[:, hp * 64:(hp + 1) * 64], op0=ALU.mult, op1=ALU.add)),
                  ["ST", "gC", PB(6)], ["ST"])
            A(("copy", C(out=STb[:], in_=ST[:])), ["ST"], ["STb"])
            if limit <= 6:
                continue
            A(("activation", C(out=osq, in_=of[:], func=AF.Square)), ["of"], ["outt"])
            V(("tensor_reduce", C(out=mean[:], in_=of[:], axis=AX.X, op=ALU.add)), ["of"], ["mean"])
            V(("tensor_reduce", C(out=ex2[:], in_=osq, axis=AX.X, op=ALU.add)), ["outt"], ["ex2"])
            V(("tensor_scalar", C(out=mean[:], in0=mean[:], scalar1=1.0 / 64, scalar2=None, op0=ALU.mult)), ["mean"], ["mean"])
            V(("tensor_tensor", C(out=var[:], in0=mean[:], in1=mean[:], op=ALU.mult)), ["mean"], ["var"])
            V(("scalar_tensor_tensor", C(out=var[:], in0=ex2[:], scalar=1.0 / 64, in1=var[:], op0=ALU.mult, op1=ALU.subtract)),
              ["ex2", "var"], ["var"])
            A(("activation", C(out=var[:], in_=var[:], func=AF.Sqrt, bias=64e-5)), ["var"], ["var"])
            V(("reciprocal", C(out=var[:], in_=var[:])), ["var"], ["var"])
            V(("tensor_tensor", C(out=of[:], in0=of[:], in1=mean[:].unsqueeze(2).broadcast_to([128, 16, 64]), op=ALU.subtract)),
              ["of", "mean"], ["of"])
            V(("tensor_tensor", C(out=of[:], in0=of[:], in1=var[:].unsqueeze(2).broadcast_to([128, 16, 64]), op=ALU.mult)),
              ["of", "var"], ["of"])
            for hp in range(8):
                tr(pb[hp // 4][:, (hp % 4) * 128:(hp % 4 + 1) * 128], ofl[:, hp * 128:(hp + 1) * 128], ident[:], ["of", "ident"], [PB(hp // 4)])
            for hp in range(8):
                src = pb[hp // 4][:, (hp % 4) * 128:(hp % 4 + 1) * 128]
                V(("scalar_tensor_tensor", C(out=ytmp[:], in0=src, scalar=vcol(I_LG, hp), in1=bonus[:, hp, ts],
                                                                 op0=ALU.mult, op1=ALU.add)), [PB(hp // 4), "vecs", ("bonus", hp)], ["ytmp"])
                V(("scalar_tensor_tensor", C(out=yT[:, hp, :], in0=ytmp[:], scalar=vcol(I_LB, hp), in1=sz[:, hp, ts],
                                                         op0=ALU.add, op1=ALU.mult)), ["ytmp", "vecs", ("sz", hp)], ["yT"])
            P.dma("gpsimd", pfx + "xr", xr[:], x[t0 + j * 128:t0 + (j + 1) * 128, :], writes=[("xt", 0)])
            for hf in range(2):
                for dc in range(8):
                    mm(pb[2 + hf], yT[:, dc, :], woutb[:, dc, hf * 512:(hf + 1) * 512], dc == 0, dc == 7, ["yT", "woutb"], [PB(2 + hf)])
                V(("tensor_tensor", C(out=outt[:, hf * 512:(hf + 1) * 512], in0=pb[2 + hf], in1=xr[:, hf * 512:(hf + 1) * 512],
                                                  op=ALU.add)), [PB(2 + hf), ("xt", 0)], ["outt"])
            P.dma("sync", pfx + "xo", xo[t0 + j * 128:t0 + (j + 1) * 128, :], outt[:], reads=["outt"], writes=[(pfx + "xo", b * 2 + j)])
    return [(pfx + "xo", i) for i in range(NB * 2)]


NEG = -30000.0
GELU_C = 1.5957691216057308


def nsa_host(inp):
    qg = inp["nsa_q_gain"][0]
    kg = inp["nsa_k_gain"][0]
    gains = np.stack([np.tile(qg, 2), np.tile(kg[0], 2), np.tile(kg[1], 2), np.tile(kg[2], 2)], 1).astype(np.float32)
    pe = inp["nsa_cmp_pe"][0]
    peT = np.concatenate([pe[0].T, pe[1].T], 0).astype(np.float32)
    b1 = inp["nsa_cmp_b1"][0].reshape(2, 2, 128).transpose(2, 0, 1).astype(np.float32)
    w2 = inp["nsa_cmp_w2"][0].reshape(2, 2, 128, 64).transpose(2, 1, 0, 3).astype(np.float32)
    return dict(gains=np.ascontiguousarray(gains), peT=np.ascontiguousarray(peT), b1=np.ascontiguousarray(b1),
                w2=np.ascontiguousarray(w2.reshape(128, 256)))


def emit_nsa(nc, P, st, x, xo, wd, T, pfx="n", limit=99):
    D = 1024
    NB = T // 256
    NT = T // 128
    sbn = [0]

    def sb(shape, dt, name=None):
        sbn[0] += 1
        return st.enter_context(nc.sbuf_tensor(f"{pfx}_{name or 't'}{sbn[0]}", shape, dt))

    pball = st.enter_context(nc.psum_tensor(f"{pfx}_psum", [128, 8, 512], F32))
    pb = [pball[:, i, :] for i in range(8)]
    PB = lambda i: (pfx + "pb", i)

    def V(fn, r=(), w=()): P.op("vector", fn, r, w)
    def G(fn, r=(), w=()): P.op("gpsimd", fn, r, w)
    def A(fn, r=(), w=()): P.op("scalar", fn, r, w)

    def mm(out, lhsT, rhs, start=True, stop=True, r=(), w=()):
        P.op("tensor", ("matmul", C(out=out, lhsT=lhsT, rhs=rhs, start=start, stop=stop)), r, w, pe_accum=not start)

    def tr(out, in_, ident, r=(), w=()):
        P.op("tensor", ("transpose", C(out=out, in_=in_, identity=ident)), r, w)

    ones = sb([128, 512], F32, "ones")
    onesb = sb([64, 2048], BF16, "onesb")
    zerob = sb([128, 4, 128], BF16, "zerob")
    ident = sb([128, 128], F32, "ident")
    identb = sb([128, 128], BF16, "identb")
    BD = sb([128, 128], F32, "BD")
    CB = sb([128, 4, 128], BF16, "CB")
    AB = sb([128, 4, 128], BF16, "AB")
    E = sb([128, 32, 128], BF16, "E")
    ov = sb([128, 2, 64], BF16, "ov")
    ovf = sb([128, 2, 64], F32, "ovf")
    G(("memset", C(ones[:], 1.0)), w=["ones"])
    G(("memset", C(onesb[:], 1.0)), w=["onesb"])
    G(("memset", C(zerob[:], 0.0)), w=["zerob"])
    G(("affine_select", C(out=ident[:], in_=ones[:, 0:128], pattern=[[-1, 128]], compare_op=ALU.is_equal, fill=0.0, base=0,
                          channel_multiplier=1)), ["ones"], ["ident"])
    G(("tensor_copy", C(out=identb[:], in_=ident[:])), ["ident"], ["identb"])
    G(("memset", C(BD[:], 0.0)), w=["BD"])
    G(("memset", C(BD[0:64, 0:64], 1.0)), w=["BD"])
    G(("memset", C(BD[64:128, 64:128], 1.0)), w=["BD"])
    G(("affine_select", C(out=CB[:], in_=zerob[:], pattern=[[0, 4], [1, 128]], compare_op=ALU.is_ge, fill=NEG, base=0,
                          channel_multiplier=-1)), ["zerob"], ["CB"])
    G(("affine_select", C(out=AB[:], in_=zerob[:], pattern=[[0, 4], [-1, 128]], compare_op=ALU.is_gt, fill=NEG, base=0,
                          channel_multiplier=1)), ["zerob"], ["AB"])
    ob3 = onesb[:].rearrange("p (a b) -> p a b", a=32)
    G(("memset", C(E[:], 0.0)), w=["E"])
    G(("affine_select", C(out=E[0:64, :, 0:64], in_=ob3, pattern=[[-2, 32], [0, 64]], compare_op=ALU.is_equal, fill=0.0, base=0,
                          channel_multiplier=1)), ["onesb"], ["E"])
    G(("affine_select", C(out=E[0:64, :, 64:128], in_=ob3, pattern=[[-2, 32], [0, 64]], compare_op=ALU.is_equal, fill=0.0, base=-1,
                          channel_multiplier=1)), ["onesb"], ["E"])
    for c in range(2):
        G(("affine_select", C(out=ovf[:, c, :], in_=ones[:, 0:64], pattern=[[-64, 64]], compare_op=ALU.is_ge, fill=0.0,
                              base=2048 * c + 31, channel_multiplier=16)), ["ones"], ["ovf"])
        G(("affine_select", C(out=ovf[:, c, :], in_=ovf[:, c, :], pattern=[[64, 64]], compare_op=ALU.is_ge, fill=0.0,
                              base=63 - 2048 * c, channel_multiplier=-16)), ["ovf"], ["ovf"])
    G(("tensor_copy", C(out=ov[:], in_=ovf[:])), ["ovf"], ["ov"])

    gb = sb([128, D], F32, "gb")
    gains = sb([128, 4], F32, "gains")
    peT = sb([128, 32], F32, "peT")
    peTb = sb([128, 32], BF16, "peTb")
    b1 = sb([128, 2, 2], F32, "b1")
    cb = sb([128, 2, 2], F32, "cb")
    w2f = sb([128, 256], F32, "w2f")
    w2b = sb([128, 2, 2, 64], BF16, "w2b")
    w1b = sb([128, 32, 256], BF16, "w1b")
    woutb = sb([128, 8, 1024], BF16, "woutb")
    stg = [sb([128, 1024], F32, f"stg{i}") for i in range(2)]
    stgb = [sb([128, 1024], BF16, f"stgb{i}") for i in range(2)]
    wslot = [sb([128, 8, 128], BF16, f"wslot{i}") for i in range(2)]
    NCH = 29
    wscr = nc.dram_tensor(pfx + "_wscr", [NCH, 128, 8, 128], BF16, kind="Internal").ap()
    P.dma("sync", pfx + "gb", gb[:], wd["g"].partition_broadcast(128), writes=["gb"])
    P.dma("sync", pfx + "gains", gains[:], wd["gains"], writes=["gains"])
    P.dma("sync", pfx + "peT", peT[:], wd["peT"], writes=["peT"])
    P.dma("sync", pfx + "b1", b1[:].rearrange("p a b -> p (a b)"), wd["b1"].rearrange("p a b -> p (a b)"), writes=["b1"])
    P.dma("sync", pfx + "w2f", w2f[:], wd["w2"], writes=["w2f"])
    V(("tensor_copy", C(out=w2b[:].rearrange("p a b c -> p (a b c)"), in_=w2f[:])), ["w2f"], ["w2b"])
    V(("tensor_copy", C(out=peTb[:], in_=peT[:])), ["peT"], ["peTb"])
    V(("tensor_scalar", C(out=gains[:, 0:1], in0=gains[:, 0:1], scalar1=0.125, scalar2=None, op0=ALU.mult)), ["gains"], ["gains"])
    nst = [0]

    def load_cast(dst_ap, src_ap, np_, ncols, wres, p0=0):
        i = nst[0] % 2
        nst[0] += 1
        q = "sync" if i == 0 else "gpsimd"
        P.dma(q, pfx + f"stg{i}", stg[i][p0:p0 + np_, 0:ncols], src_ap, writes=[("stg", i)])
        eng = "vector" if i == 0 else "gpsimd"
        P.op(eng, ("tensor_copy", C(out=dst_ap, in_=stg[i][p0:p0 + np_, 0:ncols])), [("stg", i)], [wres])
        return i

    CQ, CKS, CKW, CZ, CCV, CVS, CVW, CG = 0, 8, 10, 12, 20, 24, 26, 28
    win_v = wd["w_in"].rearrange("(c p) f -> p c f", p=128)
    WS_ALL = []

    def scr_write(i, dst, src, key):
        P.dma("sync" if i == 0 else "gpsimd", pfx + f"wscr{i}", dst, src, reads=[("stgb", i)], writes=[key])
        WS_ALL.append(key)

    for dc in range(8):
        i = nst[0] % 2
        load_cast(stgb[i][:], win_v[:, dc, 0:1024], 128, 1024, ("stgb", i))
        srcv = stgb[i][:].rearrange("p (m e j n) -> p m e j n", m=2, e=2, j=4)
        for m in range(2):
            for e in range(2):
                dst = wscr[CQ + m * 4:CQ + m * 4 + 4, :, dc, e * 64:(e + 1) * 64].rearrange("j p n -> p j n")
                scr_write(i, dst, srcv[:, m, e, :, :], ("wscr", "q", dc, m, e))
        i = nst[0] % 2
        load_cast(stgb[i][:], win_v[:, dc, 1024:2048], 128, 1024, ("stgb", i))
        s4 = stgb[i][:].rearrange("p (a g n) -> p a g n", a=4, g=4)
        for a_ in range(2):
            dst = wscr[CCV:CCV + 4, :, dc, a_ * 64:(a_ + 1) * 64].rearrange("g p n -> p g n")
            scr_write(i, dst, s4[:, a_, :, :], ("wscr", "cv", dc, a_))
        s2 = stgb[i][:].rearrange("p (a c f) -> p a c f", a=4, c=2)
        scr_write(i, wscr[CKS:CKS + 2, :, dc, :].rearrange("c p f -> p c f"), s2[:, 2, :, :], ("wscr", "ks", dc))
        scr_write(i, wscr[CVS:CVS + 2, :, dc, :].rearrange("c p f -> p c f"), s2[:, 3, :, :], ("wscr", "vs", dc))
        i = nst[0] % 2
        load_cast(stgb[i][:], win_v[:, dc, 2048:3072], 128, 1024, ("stgb", i))
        s8 = stgb[i][:].rearrange("p (c f) -> p c f", c=8)
        scr_write(i, wscr[CKW:CKW + 2, :, dc, :].rearrange("c p f -> p c f"), s8[:, 0:2, :], ("wscr", "kw", dc))
        scr_write(i, wscr[CVW:CVW + 2, :, dc, :].rearrange("c p f -> p c f"), s8[:, 2:4, :], ("wscr", "vw", dc))
        scr_write(i, wscr[CZ:CZ + 4, :, dc, :].rearrange("c p f -> p c f"), s8[:, 4:8, :], ("wscr", "z0", dc))
        i = nst[0] % 2
        load_cast(stgb[i][:, 0:560], win_v[:, dc, 3072:3632], 128, 560, ("stgb", i))
        s5 = stgb[i][:, 0:512].rearrange("p (c f) -> p c f", c=4)
        scr_write(i, wscr[CZ + 4:CZ + 8, :, dc, :].rearrange("c p f -> p c f"), s5, ("wscr", "z1", dc))
        scr_write(i, wscr[CG, :, dc, 0:48], stgb[i][:, 512:560], ("wscr", "g", dc))
    wout_v = wd["w_out"].rearrange("(c p) f -> p c f", p=128)
    for c in range(8):
        load_cast(woutb[:, c, :], wout_v[:, c, :], 128, 1024, "woutb")
    for kv in range(2):
        w1v = wd["w1"][kv].rearrange("(l d) f -> d l f", d=64)
        for l4 in range(0, 32, 4):
            i = nst[0] % 2
            P.dma("sync" if i == 0 else "gpsimd", pfx + f"stg{i}", stg[i][kv * 64:(kv + 1) * 64, :].rearrange("p (l f) -> p l f", l=4),
                  w1v[:, l4:l4 + 4, :], writes=[("stg", i)])
            nst[0] += 1
            P.op("vector" if i == 0 else "gpsimd",
                 ("tensor_copy", C(out=w1b[kv * 64:(kv + 1) * 64, l4:l4 + 4, :].rearrange("p l f -> p (l f)"),
                                   in_=stg[i][kv * 64:(kv + 1) * 64, :])), [("stg", i)], ["w1b"])
    for kv in range(2):
        ps_ = slice(kv * 64, (kv + 1) * 64)
        for fc in range(2):
            for l in range(32):
                mm(pb[0][:, (kv * 2 + fc):(kv * 2 + fc) + 1], w1b[ps_, l, fc * 128:(fc + 1) * 128], peTb[ps_, l:l + 1], l == 0, l == 31,
                   ["w1b", "peTb"], [PB(0)])
    V(("tensor_tensor", C(out=cb[:].rearrange("p a b -> p (a b)"), in0=pb[0][:, 0:4], in1=b1[:].rearrange("p a b -> p (a b)"), op=ALU.add)),
      [PB(0), "b1"], ["cb"])
    if limit <= 0:
        return []

    ksT = sb([128, 2, T], BF16, "ksT")
    kwT = sb([128, 2, T], BF16, "kwT")
    vs_tok = sb([128, NT, 4, 65], BF16, "vs_tok")
    vw_tok = sb([128, NT, 4, 65], BF16, "vw_tok")
    kcT = sb([128, 2, 256], BF16, "kcT")
    vcT = sb([128, 2, 256], F32, "vcT")
    vc_tok = sb([128, 2, 4, 65], BF16, "vc_tok")
    G(("memset", C(vs_tok[:], 1.0)), w=["vtok"])
    G(("memset", C(vw_tok[:], 1.0)), w=["vtok"])
    G(("memset", C(vc_tok[:], 1.0)), w=["vc_tok"])
    G(("memset", C(kcT[:], 0.0)), w=["kcT"])
    G(("memset", C(vcT[:], 0.0)), w=["vcT"])
    xt = [sb([128, D], F32, f"xt{i}") for i in range(2)]
    ss = sb([128, 1], F32, "ss")
    rs = sb([128, 1], F32, "rs")
    ht = sb([128, D], F32, "ht")
    hT = sb([128, 8, 256], BF16, "hT")
    qT = sb([128, 2, 4, 256], BF16, "qT")
    szT = sb([128, 8, 256], BF16, "szT")
    craw = sb([128, 4, 272], BF16, "craw")
    gsb = sb([128, 2, 48], F32, "gsb")
    sq = sb([128, 256], F32, "sq")
    rstd = sb([128, 256], F32, "rstd")
    xs = sb([128, 256], F32, "xs")
    g1 = sb([128, 256], F32, "g1")
    hid = sb([128, 4, 64], BF16, "hid")
    kcv = sb([128, 2, 2, 16], F32, "kcv")
    Pc = sb([128, 2, 4, 128], BF16, "Pc")
    Ps = [sb([128, 4, 128], BF16, f"Ps{i}") for i in range(2)]
    cmpb = sb([128, 2, 4, 128], BF16, "cmpb")
    Aadd = sb([128, 64], F32, "Aadd")
    A0 = sb([128, 64], F32, "A0")
    imp = sb([128, 64], F32, "imp")
    imp2 = sb([128, 64], F32, "imp2")
    mx = sb([128, 16], F32, "mx")
    selm = sb([128, 64], F32, "selm")
    selmT = [sb([128, 4, 128], BF16, f"selmT{i_}") for i_ in range(2)]
    rc = sb([128, 4], F32, "rc")
    cf = sb([128, 4], F32, "cf")
    oacc = sb([128, 16, 64], F32, "oacc")
    yT = sb([128, 8, 128], BF16, "yT")
    outt = sb([128, D], F32, "outt")
    G(("memset", C(craw[:], 0.0)), w=["craw"])
    for i_ in range(2):
        G(("memset", C(selmT[i_][:], 0.0)), w=[("selmT", i_)])
    pvctr = [0]
    jctr = [0]

    def pvbank():
        k = 2 + (pvctr[0] % 3)
        pvctr[0] += 1
        return k
    G(("memset", C(A0[:], 0.0)), w=["A0"])
    G(("memset", C(A0[:, 0:1], 10000.0)), w=["A0"])
    D0 = sb([128, 64], F32, "D0")
    Dd = sb([128, 64], F32, "Dd")
    At = sb([128, 64], F32, "At")
    G(("iota", C(D0[:], pattern=[[-1, 64]], base=0, channel_multiplier=0, allow_small_or_imprecise_dtypes=True)), w=["D0"])
    G(("tensor_scalar", C(out=D0[64:128, :], in0=D0[64:128, :], scalar1=1.0, scalar2=None, op0=ALU.add)), ["D0"], ["D0"])
    eps = 1e-6
    wcnt = [0]

    def wload(ch):
        s_ = wcnt[0] % 2
        wcnt[0] += 1
        if ch == CG:
            P.dma("sync", pfx + f"ws{s_}", wslot[s_][:, :, 0:48], wscr[ch][:, :, 0:48], reads=WS_ALL, writes=[("wslot", s_)])
        else:
            P.dma("sync", pfx + f"ws{s_}", wslot[s_][:], wscr[ch], reads=WS_ALL, writes=[("wslot", s_)])
        return s_

    def rmsnorm_evac(ps_ap, gcol, dst_ap, pbi, wres, ncols=256):
        A(("activation", C(out=sq[:, 0:ncols], in_=ps_ap, func=AF.Square)), [PB(pbi)], ["sq"])
        mm(pb[7][:, 0:ncols], BD[:], sq[:, 0:ncols], True, True, ["BD", "sq"], [PB(7)])
        A(("activation", C(out=rstd[:, 0:ncols], in_=pb[7][:, 0:ncols], func=AF.Sqrt, scale=1.0 / 64, bias=eps)), [PB(7)], ["rstd"])
        V(("reciprocal", C(out=rstd[:, 0:ncols], in_=rstd[:, 0:ncols])), ["rstd"], ["rstd"])
        V(("scalar_tensor_tensor", C(out=dst_ap, in0=ps_ap, scalar=gains[:, gcol:gcol + 1], in1=rstd[:, 0:ncols], op0=ALU.mult, op1=ALU.mult)),
          [PB(pbi), "gains", "rstd"], [wres])

    outs = []
    for b in range(NB):
        t0 = b * 256
        for i in range(2):
            xb = xt[i]
            P.dma("sync", pfx + f"x{i}", xb[:], x[t0 + i * 128:t0 + (i + 1) * 128, :], writes=[("xt", i)])
            A(("activation", C(out=ht[:], in_=xb[:], func=AF.Square, accum_out=ss[:])), [("xt", i)], ["ht", "ss"])
            A(("activation", C(out=rs[:], in_=ss[:], func=AF.Sqrt, scale=1.0 / D, bias=eps)), ["ss"], ["rs"])
            V(("reciprocal", C(out=rs[:], in_=rs[:])), ["rs"], ["rs"])
            V(("scalar_tensor_tensor", C(out=ht[:], in0=xb[:], scalar=rs[:, 0:1], in1=gb[:], op0=ALU.mult, op1=ALU.mult)),
              [("xt", i), "rs", "gb"], ["ht"])
            for half in range(2):
                for c in range(4):
                    cc = half * 4 + c
                    tr(pb[half][:, c * 128:(c + 1) * 128], ht[:, cc * 128:(cc + 1) * 128], ident[:], ["ht", "ident"], [PB(half)])
                pv = pb[half].rearrange("p (c t) -> p c t", c=4)
                dst = hT[:, half * 4:(half + 1) * 4, i * 128:(i + 1) * 128]
                if half == 0:
                    V(("tensor_copy", C(out=dst, in_=pv)), [PB(half)], ["hT"])
                else:
                    A(("copy", C(out=dst, in_=pv)), [PB(half)], ["hT"])
        if limit <= 1:
            return []
        def proj_fm(ch, pbi, M=128):
            s_ = wload(ch)
            for dc in range(8):
                mm(pb[pbi][0:M, 0:256], wslot[s_][:, dc, 0:M], hT[:, dc, :], dc == 0, dc == 7, [("wslot", s_), "hT"], [PB(pbi)])
        for m in range(2):
            for j in range(4):
                pbi = (m * 4 + j) % 2
                proj_fm(CQ + m * 4 + j, pbi)
                rmsnorm_evac(pb[pbi][:, 0:256], 0, qT[:, m, j, :], pbi, "qT")
        for m in range(2):
            proj_fm(CKS + m, m)
            rmsnorm_evac(pb[m][:, 0:256], 2, ksT[:, m, t0:t0 + 256], m, "kT")
        for m in range(2):
            proj_fm(CKW + m, m)
            rmsnorm_evac(pb[m][:, 0:256], 3, kwT[:, m, t0:t0 + 256], m, "kT")
        for c in range(8):
            proj_fm(CZ + c, c % 2)
            A(("activation", C(out=szT[:, c, :], in_=pb[c % 2][:, 0:256], func=AF.Silu)), [PB(c % 2)], ["szT"])
        for g in range(4):
            proj_fm(CCV + g, g % 2)
            A(("copy", C(out=craw[:, g, 16:272], in_=pb[g % 2][:, 0:256])), [PB(g % 2)], ["craw"])
        for (ch0, vtok) in ((CVS, vs_tok), (CVW, vw_tok)):
            for c2 in range(2):
                s_ = wload(ch0 + c2)
                for i in range(2):
                    for dc in range(8):
                        mm(pb[i][:, 0:128], hT[:, dc, i * 128:(i + 1) * 128], wslot[s_][:, dc, :], dc == 0, dc == 7,
                           [("wslot", s_), "hT"], [PB(i)])
                    V(("tensor_copy", C(out=vtok[:, 2 * b + i, 2 * c2:2 * c2 + 2, 0:64],
                                        in_=pb[i][:, 0:128].rearrange("p (g n) -> p g n", g=2))), [PB(i)], ["vtok"])
        s_ = wload(CG)
        for i in range(2):
            for dc in range(8):
                mm(pb[i][:, 0:48], hT[:, dc, i * 128:(i + 1) * 128], wslot[s_][:, dc, 0:48], dc == 0, dc == 7, [("wslot", s_), "hT"], [PB(i)])
            A(("activation", C(out=gsb[:, i, :], in_=pb[i][:, 0:48], func=AF.Sigmoid)), [PB(i)], ["gsb"])
        if limit <= 2:
            return []
        i0 = 1 if b == 0 else 0
        ni = 16 - i0
        n0 = 16 * b - 1 + i0
        ph = pb[2]
        for kv in range(2):
            ps_ = slice(kv * 64, (kv + 1) * 64)
            for fc in range(2):
                reg = (kv * 2 + fc) * 64
                for l in range(32):
                    rhs = craw[ps_, :, l + 16 * i0:l + 16 * i0 + 16 * (ni - 1) + 1:16]
                    outp = ph[:, reg:reg + 64].rearrange("p (g i) -> p g i", g=4)[:, :, 0:ni]
                    mm(outp, w1b[ps_, l, fc * 128:(fc + 1) * 128], rhs, l == 0, l == 31, ["w1b", "craw"], [PB(2)])
        for kv in range(2):
            for fc in range(2):
                reg = (kv * 2 + fc) * 64
                A(("activation", C(out=xs[:, reg:reg + 64], in_=ph[:, reg:reg + 64], func=AF.Identity, bias=cb[:, kv, fc:fc + 1])),
                  [PB(2), "cb"], ["xs"])
        V(("tensor_tensor", C(out=g1[:], in0=xs[:], in1=xs[:], op=ALU.mult)), ["xs"], ["g1"])
        V(("tensor_scalar", C(out=g1[:], in0=g1[:], scalar1=0.044715, scalar2=1.0, op0=ALU.mult, op1=ALU.add)), ["g1"], ["g1"])
        V(("tensor_tensor", C(out=g1[:], in0=g1[:], in1=xs[:], op=ALU.mult)), ["g1", "xs"], ["g1"])
        A(("activation", C(out=g1[:], in_=g1[:], func=AF.Sigmoid, scale=GELU_C)), ["g1"], ["g1"])
        V(("tensor_tensor", C(out=hid[:].rearrange("p a b -> p (a b)"), in0=g1[:], in1=xs[:], op=ALU.mult)), ["g1", "xs"], ["hid"])
        pk = pb[3][:, 0:64].rearrange("p (kv m i) -> p kv m i", kv=2, m=2)
        for kv in range(2):
            for g in range(4):
                m_, e_ = g // 2, g % 2
                for fc in range(2):
                    mm(pk[e_ * 64:(e_ + 1) * 64, kv, m_, 0:ni], w2b[:, fc, kv, :], hid[:, kv * 2 + fc, g * 16:g * 16 + ni], fc == 0, fc == 1,
                       ["w2b", "hid"], [PB(3)])
        V(("tensor_copy", C(out=kcv[:, :, :, 0:ni], in_=pk[:, :, :, 0:ni])), [PB(3)], ["kcv"])
        for m_ in range(2):
            G(("tensor_copy", C(out=vcT[:, m_, n0:n0 + ni], in_=kcv[:, 1, m_, 0:ni])), ["kcv"], ["vcT"])
        kflat = kcv[:, 0, :, :].rearrange("p m i -> p (m i)")
        A(("activation", C(out=sq[:, 0:32], in_=kflat, func=AF.Square)), ["kcv"], ["sq"])
        mm(pb[7][:, 0:32], BD[:], sq[:, 0:32], True, True, ["BD", "sq"], [PB(7)])
        A(("activation", C(out=rstd[:, 0:32], in_=pb[7][:, 0:32], func=AF.Sqrt, scale=1.0 / 64, bias=eps)), [PB(7)], ["rstd"])
        V(("reciprocal", C(out=rstd[:, 0:32], in_=rstd[:, 0:32])), ["rstd"], ["rstd"])
        V(("scalar_tensor_tensor", C(out=sq[:, 0:32], in0=kflat, scalar=gains[:, 1:2], in1=rstd[:, 0:32], op0=ALU.mult, op1=ALU.mult)),
          ["kcv", "gains", "rstd"], ["sq"])
        for m_ in range(2):
            G(("tensor_copy", C(out=kcT[:, m_, n0:n0 + ni], in_=sq[:, m_ * 16:m_ * 16 + ni])), ["sq"], ["kcT"])
        for c in range(2):
            for m_ in range(2):
                tr(pb[4][:, (c * 2 + m_) * 128:(c * 2 + m_ + 1) * 128], vcT[:, m_, c * 128:(c + 1) * 128], ident[:], ["vcT", "ident"], [PB(4)])
        V(("tensor_copy", C(out=vc_tok[:, :, :, 0:64], in_=pb[4].rearrange("p (c g n) -> p c g n", c=2, g=4))), [PB(4)], ["vc_tok"])
        G(("tensor_copy", C(out=craw[:, :, 0:16], in_=craw[:, :, 256:272])), ["craw"], ["craw"])
        if limit <= 3:
            return []
        for il in range(2):
            i = 2 * b + il
            tl = slice(il * 128, (il + 1) * 128)
            V(("tensor_scalar", C(out=Dd[:], in0=D0[:], scalar1=float(2 * i), scalar2=None, op0=ALU.add)), ["D0"], ["Dd"])
            V(("tensor_scalar", C(out=Aadd[:], in0=Dd[:], scalar1=0.0, scalar2=None, op0=ALU.is_ge)), ["Dd"], ["Aadd"])
            V(("tensor_scalar", C(out=At[:], in0=Dd[:], scalar1=1.0, scalar2=10000.0, op0=ALU.is_le, op1=ALU.mult)), ["Dd"], ["At"])
            V(("tensor_tensor", C(out=Aadd[:], in0=Aadd[:], in1=At[:], op=ALU.mult)), ["Aadd", "At"], ["Aadd"])
            V(("tensor_tensor", C(out=Aadd[:], in0=Aadd[:], in1=A0[:], op=ALU.max)), ["Aadd", "A0"], ["Aadd"])
            V(("tensor_scalar", C(out=At[:], in0=Dd[:], scalar1=0.0, scalar2=-1e30, op0=ALU.is_lt, op1=ALU.mult)), ["Dd"], ["At"])
            V(("tensor_tensor", C(out=Aadd[:], in0=Aadd[:], in1=At[:], op=ALU.add)), ["Aadd", "At"], ["Aadd"])
            cts = []
            for c in range(2):
                base = 128 * i - 2048 * c - 31
                if base + 127 < 0:
                    continue
                need_bias = base - 16 * 127 < 0
                cts.append((c, need_bias))
                if need_bias:
                    G(("affine_select", C(out=cmpb[:, c, :, :], in_=zerob[:], pattern=[[0, 4], [1, 128]], compare_op=ALU.is_ge, fill=NEG,
                                          base=base, channel_multiplier=-16)), ["zerob"], ["cmpb"])
            need_sel = i >= 8
            first = {g: True for g in range(4)}

            def gparams(g):
                m_, e_ = g // 2, g % 2
                ps_ = slice(e_ * 64, (e_ + 1) * 64)
                return m_, ps_, qT[ps_, m_, :, tl]

            def combine(g, br, bank):
                pO = pb[bank]
                pO3 = pO.rearrange("p (h n) -> p h n", h=4)
                src = [PB(bank)]
                V(("tensor_scalar", C(out=rc[:], in0=pO3[:, :, 64], scalar1=1e-30, scalar2=None, op0=ALU.max)), src, ["rc"])
                V(("reciprocal", C(out=rc[:], in_=rc[:])), ["rc"], ["rc"])
                V(("tensor_tensor", C(out=cf[:], in0=rc[:], in1=gsb[:, il, br * 16 + 4 * g:br * 16 + 4 * g + 4], op=ALU.mult)),
                  ["rc", "gsb"], ["cf"])
                for h in range(4):
                    if first[g]:
                        V(("tensor_scalar", C(out=oacc[:, 4 * g + h, :], in0=pO[:, h * 128:h * 128 + 64], scalar1=cf[:, h:h + 1], scalar2=None,
                                              op0=ALU.mult)), src + ["cf"], ["oacc"])
                    else:
                        V(("scalar_tensor_tensor", C(out=oacc[:, 4 * g + h, :], in0=pO[:, h * 128:h * 128 + 64], scalar=cf[:, h:h + 1],
                                                     in1=oacc[:, 4 * g + h, :], op0=ALU.mult, op1=ALU.add)), src + ["cf", "oacc"], ["oacc"])
                first[g] = False

            def pv(bank, P_ap_of_h, v_ap, is_first, is_last, rres):
                for h in range(4):
                    P.op("tensor", ("matmul", C(out=pb[bank][:, h * 128:h * 128 + 65], lhsT=P_ap_of_h(h), rhs=v_ap,
                                                start=(is_first and h == 0), stop=is_last, skip_group_check=True)),
                         rres, [PB(bank)], pe_accum=not (is_first and h == 0))

            def stage1(g):
                m_, ps_, rq = gparams(g)
                for ci, (c, nb_) in enumerate(cts):
                    psS = pb[ci % 2]
                    mm(psS, kcT[ps_, m_, c * 128:(c + 1) * 128], rq, True, not nb_, ["kcT", "qT"], [PB(ci % 2)])
                    if nb_:
                        mm(psS, identb[:], cmpb[:, c, :, :], False, True, ["identb", "cmpb"], [PB(ci % 2)])
                    A(("activation", C(out=Pc[:, c, :, :], in_=psS.rearrange("p (h t) -> p h t", h=4), func=AF.Exp)), [PB(ci % 2)], ["Pc"])
                bank = pvbank()
                for ci, (c, nb_) in enumerate(cts):
                    pv(bank, lambda h, c=c: Pc[:, c, h, :], vc_tok[:, c, g, :], ci == 0, ci == len(cts) - 1, ["Pc", "vc_tok"])
                pI = pb[5]
                for h in range(4):
                    for ci, (c, nb_) in enumerate(cts):
                        mm(pI[:, h * 64:(h + 1) * 64], Pc[:, c, h, :], ov[:, c, :], ci == 0, ci == len(cts) - 1, ["Pc", "ov"], [PB(5)])
                combine(g, 0, bank)
                if not need_sel:
                    return
                for h in range(4):
                    if h == 0:
                        V(("tensor_scalar", C(out=imp[:], in0=pI[:, 0:64], scalar1=rc[:, 0:1], scalar2=None, op0=ALU.mult)), [PB(5), "rc"], ["imp"])
                    else:
                        V(("scalar_tensor_tensor", C(out=imp[:], in0=pI[:, h * 64:(h + 1) * 64], scalar=rc[:, h:h + 1], in1=imp[:], op0=ALU.mult,
                                                     op1=ALU.add)), [PB(5), "rc", "imp"], ["imp"])
                V(("tensor_tensor", C(out=imp[:], in0=imp[:], in1=Aadd[:], op=ALU.add)), ["imp", "Aadd"], ["imp"])
                V(("max", C(out=mx[:, 0:8], in_=imp[:])), ["imp"], ["mx"])
                V(("match_replace", C(out=imp2[:], in_to_replace=mx[:, 0:8], in_values=imp[:], imm_value=-3e38)), ["imp", "mx"], ["imp2"])
                V(("max", C(out=mx[:, 8:16], in_=imp2[:])), ["imp2"], ["mx"])
                V(("tensor_scalar", C(out=selm[:], in0=imp[:], scalar1=mx[:, 15:16], scalar2=NEG, op0=ALU.is_lt, op1=ALU.mult)),
                  ["imp", "mx"], ["selm"])
                tr(pb[6][0:64, 0:128], selm[:], ident[:], ["selm", "ident"], [PB(6)])
                V(("tensor_copy", C(out=selmT[g % 2][0:64, :, :], in_=pb[6][0:64, 0:128].unsqueeze(1).broadcast_to([64, 4, 128]))),
                  [PB(6)], [("selmT", g % 2)])

            def stage2(g):
                m_, ps_, rq = gparams(g)
                jobs = []
                wt = list(range(max(0, i - 4), i + 1))
                for ti, s_t in enumerate(wt):
                    jobs.append((2, kwT, vw_tok, s_t, ti == 0, ti == len(wt) - 1))
                for ti, s_t in enumerate(range(0, i + 1)):
                    jobs.append((1, ksT, vs_tok, s_t, ti == 0, ti == i))
                pend = None
                bank = None
                for k, (br, kT, vtok, s_t, jf, jl) in enumerate(jobs + [None] if False else jobs):
                    par = jctr[0] % 2
                    jctr[0] += 1
                    psS = pb[par]
                    extra = []
                    if br == 1 and need_sel:
                        extra.append((E[:, s_t, :], selmT[g % 2][:], ["E", ("selmT", g % 2)]))
                    if s_t == i:
                        extra.append((identb[:], CB[:], ["identb", "CB"]))
                    if br == 2 and s_t == i - 4:
                        extra.append((identb[:], AB[:], ["identb", "AB"]))
                    mm(psS, kT[ps_, m_, s_t * 128:(s_t + 1) * 128], rq, True, len(extra) == 0, ["kT", "qT"], [PB(par)])
                    for xi, (l_, r_, rr) in enumerate(extra):
                        mm(psS, l_, r_, False, xi == len(extra) - 1, rr, [PB(par)])
                    A(("activation", C(out=Ps[par][:], in_=psS.rearrange("p (h t) -> p h t", h=4), func=AF.Exp)), [PB(par)], [("Ps", par)])
                    if pend is not None:
                        flush(g, pend)
                    pend = (br, vtok, s_t, jf, jl, par)
                flush(g, pend)

            cur_bank = {}

            def flush(g, pend):
                br, vtok, s_t, jf, jl, par = pend
                if jf:
                    cur_bank[(g, br)] = pvbank()
                bank = cur_bank[(g, br)]
                pv(bank, lambda h: Ps[par][:, h, :], vtok[:, s_t, g, :], jf, jl, [("Ps", par), "vtok"])
                if jl:
                    combine(g, br, bank)

            stage1(0)
            for g in range(4):
                if g < 3:
                    stage1(g + 1)
                stage2(g)
            if limit <= 4:
                continue
            ofl = oacc[:].rearrange("p a b -> p (a b)")
            for c in range(8):
                tr(pb[c // 4][:, (c % 4) * 128:(c % 4 + 1) * 128], ofl[:, c * 128:(c + 1) * 128], ident[:], ["oacc", "ident"], [PB(c // 4)])
            for hf in range(2):
                V(("tensor_tensor", C(out=yT[:, hf * 4:(hf + 1) * 4, :], in0=pb[hf].rearrange("p (c t) -> p c t", c=4),
                                      in1=szT[:, hf * 4:(hf + 1) * 4, tl], op=ALU.mult)), [PB(hf), "szT"], ["yT"])
            P.dma("gpsimd", pfx + "xr", xt[0][:], x[t0 + il * 128:t0 + (il + 1) * 128, :], writes=[("xt", 0)])
            for hf in range(2):
                for dc in range(8):
                    mm(pb[2 + hf], yT[:, dc, :], woutb[:, dc, hf * 512:(hf + 1) * 512], dc == 0, dc == 7, ["yT", "woutb"], [PB(2 + hf)])
                V(("tensor_tensor", C(out=outt[:, hf * 512:(hf + 1) * 512], in0=pb[2 + hf], in1=xt[0][:, hf * 512:(hf + 1) * 512], op=ALU.add)),
                  [PB(2 + hf), ("xt", 0)], ["outt"])
            P.dma("sync", pfx + "xo", xo[t0 + il * 128:t0 + (il + 1) * 128, :], outt[:], reads=["outt"], writes=[(pfx + "xo", i)])
            outs.append((pfx + "xo", i))
    return outs


T_SEQ = 4096
FUSED = True
_CACHE = {}


def _dt(nc, n, s):
    return nc.dram_tensor(n, s, F32, kind="ExternalInput").ap()


def _rwkv_wd(nc):
    return dict(g=_dt(nc, "r_g", [1, 1024]), vecs=_dt(nc, "r_vecs", [128, NV, 8]), w_in=_dt(nc, "r_w_in", [1024, 4096]),
                w1=_dt(nc, "r_w1", [1024, 64]), a1=_dt(nc, "r_a1", [1024, 64]), w2=_dt(nc, "r_w2", [64, 1024]),
                a2=_dt(nc, "r_a2", [64, 1024]), w_out=_dt(nc, "r_w_out", [1024, 1024]))


def _nsa_wd(nc):
    return dict(g=_dt(nc, "n_g", [1, 1024]), w_in=_dt(nc, "n_w_in", [1024, 3632]), w_out=_dt(nc, "n_w_out", [1024, 1024]),
                w1=_dt(nc, "n_w1", [2, 2048, 256]), gains=_dt(nc, "n_gains", [128, 4]), peT=_dt(nc, "n_peT", [128, 32]),
                b1=_dt(nc, "n_b1", [128, 2, 2]), w2=_dt(nc, "n_w2", [128, 256]))


def _build(which):
    T = T_SEQ
    nc = bass.Bass("TRN2", target_bir_lowering=False)
    x = _dt(nc, "x", [T, 1024])
    xo = nc.dram_tensor("xo", [T, 1024], F32, kind="ExternalOutput").ap()
    if which == "fused":
        x1 = nc.dram_tensor("x1_scr", [T, 1024], F32, kind="Internal").ap()
        rwd = _rwkv_wd(nc)
        nwd = _nsa_wd(nc)
        with ExitStack() as st:
            P = Prog(nc, st)
            outs = emit_rwkv(nc, P, st, x, x1, rwd, T)
            P.final_wait("sync", outs)
            P.emit()
        with ExitStack() as st:
            P = Prog(nc, st)
            outs = emit_nsa(nc, P, st, x1, xo, nwd, T)
            P.final_wait("sync", outs)
            P.emit()
    else:
        wd = _rwkv_wd(nc) if which == "rwkv" else _nsa_wd(nc)
        with ExitStack() as st:
            P = Prog(nc, st)
            outs = (emit_rwkv if which == "rwkv" else emit_nsa)(nc, P, st, x, xo, wd, T)
            P.final_wait("sync", outs)
            P.emit()
    return nc


def _get(which):
    if which not in _CACHE:
        _CACHE[which] = _build(which)
    return _CACHE[which]


def kernel(**inputs):
    inp = {k: np.asarray(v) for k, v in inputs.items()}
    x = np.ascontiguousarray(inp["x"], dtype=np.float32)
    B = x.shape[0]
    f32 = lambda a: np.ascontiguousarray(a, dtype=np.float32)
    rmap = {"r_g": f32(inp["norm_g"][0:1]), "r_vecs": rwkv_host_vecs(inp), "r_w_in": f32(inp["rwkv_w_in"][0]),
            "r_w1": f32(inp["rwkv_w1"][0]), "r_a1": f32(inp["rwkv_a1"][0]), "r_w2": f32(inp["rwkv_w2"][0]),
            "r_a2": f32(inp["rwkv_a2"][0]), "r_w_out": f32(inp["rwkv_w_out"][0])}
    hp = nsa_host(inp)
    nmap = {"n_g": f32(inp["norm_g"][1:2]), "n_w_in": f32(inp["nsa_w_in"][0]), "n_w_out": f32(inp["nsa_w_out"][0]),
            "n_w1": f32(inp["nsa_cmp_w1"][0]), "n_gains": hp["gains"], "n_peT": hp["peT"], "n_b1": hp["b1"], "n_w2": hp["w2"]}
    cores = list(range(B))
    if FUSED:
        nc = _get("fused")
        res = run_bass_kernel_spmd(nc, [{"x": x[i], **rmap, **nmap} for i in cores], core_ids=cores)
        return np.stack([np.asarray(res.results[i]["xo"], dtype=np.float32) for i in cores], 0)
    nc = _get("rwkv")
    res = run_bass_kernel_spmd(nc, [{"x": x[i], **rmap} for i in cores], core_ids=cores)
    x1 = [np.ascontiguousarray(res.results[i]["xo"], dtype=np.float32) for i in cores]
    nc = _get("nsa")
    res = run_bass_kernel_spmd(nc, [{"x": x1[i], **nmap} for i in cores], core_ids=cores)
    return np.stack([np.asarray(res.results[i]["xo"], dtype=np.float32) for i in cores], 0)
```
